# Optimizing a Trainium2 kernel written in Bass

```python
import math
import jax, jax.numpy as jnp
from jax import lax
import numpy as np

D_MODEL = 1024
BATCH = 16
SEQ = 2048
DEPTH = 4

HEAD_DIM = 64
SB_HEADS = 8
SB_WIDTH = SB_HEADS * HEAD_DIM
NSA_HEADS = 8
NSA_KV_HEADS = 2
NSA_GROUP = NSA_HEADS // NSA_KV_HEADS
NSA_WIDTH = NSA_HEADS * HEAD_DIM
NSA_KV_WIDTH = NSA_KV_HEADS * HEAD_DIM
NSA_BRANCHES = 3
CMP_LEN = 32
CMP_STRIDE = 16
CMP_HIDDEN = 256
SEL_BLOCK = 64
SEL_TOP_N = 8
WINDOW = 512
Q_BLOCK = 128
ROPE_THETA = 10000.0
NORM_EPS = 1e-6
FORCED_BONUS = 1e4
NEG_INF = -1e30

IN_SPLITS = (SB_WIDTH, SB_WIDTH, SB_WIDTH, SB_WIDTH,
             NSA_WIDTH,
             NSA_KV_WIDTH, NSA_KV_WIDTH,
             NSA_KV_WIDTH, NSA_KV_WIDTH,
             NSA_KV_WIDTH, NSA_KV_WIDTH,
             NSA_BRANCHES * NSA_HEADS,
             NSA_WIDTH,
             D_MODEL, D_MODEL)
N_IN = 4 * SB_WIDTH + NSA_WIDTH + 6 * NSA_KV_WIDTH + NSA_BRANCHES * NSA_HEADS + NSA_WIDTH + 2 * D_MODEL

kernel_name = 'hybrid_stickbreak_nsa_block'


def rms_norm(x, g):
    xf = x.astype(jnp.float32)
    y = xf * lax.rsqrt(jnp.mean(xf * xf, axis=-1, keepdims=True) + NORM_EPS)
    return (y * g.astype(jnp.float32)).astype(x.dtype)


def rope(x, pos):
    half = HEAD_DIM // 2
    inv_freq = ROPE_THETA ** (-jnp.arange(half, dtype=jnp.float32) / half)
    ang = pos.astype(jnp.float32)[:, None, :, None] * inv_freq
    cos, sin = jnp.cos(ang), jnp.sin(ang)
    xf = x.astype(jnp.float32)
    x1, x2 = xf[..., :half], xf[..., half:]
    return jnp.concatenate([x1 * cos - x2 * sin, x2 * cos + x1 * sin], axis=-1).astype(x.dtype)


def split_heads(t, n):
    b, l, _ = t.shape
    return t.reshape(b, l, n, HEAD_DIM).transpose(0, 2, 1, 3)


def merge_heads(t):
    b, n, l, d = t.shape
    return t.transpose(0, 2, 1, 3).reshape(b, l, n * d)


def masked_softmax(s, mask):
    return jax.nn.softmax(jnp.where(mask, s.astype(jnp.float32), NEG_INF), axis=-1)


def stick_breaking_attention(q, k, v):
    s_len = q.shape[2]
    scale = HEAD_DIM ** -0.5
    outs = []
    for i in range(s_len // Q_BLOCK):
        end = (i + 1) * Q_BLOCK
        qb = q[:, :, i * Q_BLOCK:end]
        kb, vb = k[:, :, :end], v[:, :, :end]
        z = jnp.einsum('bhqd,bhkd->bhqk', qb, kb).astype(jnp.float32) * scale
        t = i * Q_BLOCK + jnp.arange(Q_BLOCK)
        s = jnp.arange(end)
        mask = s[None, :] < t[:, None]
        log_1m = jnp.where(mask, jax.nn.log_sigmoid(-z), 0.0)
        between = lax.cumsum(log_1m, axis=3, reverse=True) - log_1m
        w = jnp.where(mask, jnp.exp(jax.nn.log_sigmoid(z) + between), 0.0)
        outs.append(jnp.einsum('bhqk,bhkd->bhqd', w.astype(vb.dtype), vb))
    return jnp.concatenate(outs, axis=2)


def compress_blocks(x, idx, pe, w1, b1, w2):
    blocks = x[:, :, idx] + pe
    flat = blocks.reshape(blocks.shape[:3] + (CMP_LEN * HEAD_DIM,))
    return jax.nn.silu(flat @ w1 + b1) @ w2


def nsa_attention(q, k_cmp, v_cmp, k_slc, v_slc, k_win, v_win, branch_gate, pos,
                  k_cmp_norm_g, cmp_pe, cmp_w1, cmp_b1, cmp_w2):
    b, _, s_len, _ = q.shape
    g_n, r_n = NSA_KV_HEADS, NSA_GROUP
    scale = HEAD_DIM ** -0.5
    t = jnp.arange(s_len)
    qg = q.reshape(b, g_n, r_n, s_len, HEAD_DIM)

    n_cmp = (s_len - CMP_LEN) // CMP_STRIDE + 1
    cmp_start = jnp.arange(n_cmp) * CMP_STRIDE
    idx = cmp_start[:, None] + jnp.arange(CMP_LEN)[None, :]
    kc = compress_blocks(k_cmp, idx, cmp_pe[0], cmp_w1[0], cmp_b1[0], cmp_w2[0])
    vc = compress_blocks(v_cmp, idx, cmp_pe[1], cmp_w1[1], cmp_b1[1], cmp_w2[1])
    kc = rope(rms_norm(kc, k_cmp_norm_g), jnp.mean(pos[:, idx].astype(jnp.float32), axis=-1))
    s_cmp = jnp.einsum('bgrqd,bgcd->bgrqc', qg, kc) * scale
    mask_c = (cmp_start + CMP_LEN - 1)[None, :] <= t[:, None]
    p_cmp = masked_softmax(s_cmp, mask_c) * jnp.any(mask_c, axis=-1)[:, None]
    o_cmp = jnp.einsum('bgrqc,bgcd->bgrqd', p_cmp.astype(vc.dtype), vc)

    n_sel = s_len // SEL_BLOCK
    top_n = min(SEL_TOP_N, n_sel)
    sel_start = jnp.arange(n_sel) * SEL_BLOCK
    overlap = ((cmp_start[:, None] < sel_start[None, :] + SEL_BLOCK)
               & (cmp_start[:, None] + CMP_LEN > sel_start[None, :])).astype(jnp.float32)
    imp = jnp.einsum('bgrqc,cj->bgqj', p_cmp, overlap)
    cur = t // SEL_BLOCK
    j = jnp.arange(n_sel)
    forced = (j[None, :] == 0) | (j[None, :] == cur[:, None]) | (j[None, :] == cur[:, None] - 1)
    imp = jnp.where(sel_start[None, :] <= t[:, None],
                    imp + jnp.where(forced, FORCED_BONUS, 0.0), NEG_INF)
    _, sel_idx = lax.top_k(imp, top_n)

    nb = s_len // Q_BLOCK
    q_blocks = qg.reshape(b, g_n, r_n, nb, Q_BLOCK, HEAD_DIM).transpose(3, 0, 1, 2, 4, 5)
    idx_blocks = sel_idx.reshape(b, g_n, nb, Q_BLOCK, top_n).transpose(2, 0, 1, 3, 4)
    pad = ((0, 0), (0, 0), (WINDOW, 0), (0, 0))
    kw_pad, vw_pad = jnp.pad(k_win, pad), jnp.pad(v_win, pad)
    gather = jax.vmap(jax.vmap(lambda arr, ii: arr[ii]))
    n_keys = top_n * SEL_BLOCK

    def block_step(args):
        qb, ib, i = args
        tq = i * Q_BLOCK + jnp.arange(Q_BLOCK)
        tok = (ib[..., None] * SEL_BLOCK + jnp.arange(SEL_BLOCK)).reshape(b, g_n, Q_BLOCK * n_keys)
        ks = gather(k_slc, tok).reshape(b, g_n, Q_BLOCK, n_keys, HEAD_DIM)
        vs = gather(v_slc, tok).reshape(b, g_n, Q_BLOCK, n_keys, HEAD_DIM)
        tok = tok.reshape(b, g_n, Q_BLOCK, n_keys)
        ss = jnp.einsum('bgrqd,bgqkd->bgrqk', qb, ks) * scale
        ps = masked_softmax(ss, (tok <= tq[:, None])[:, :, None])
        o_s = jnp.einsum('bgrqk,bgqkd->bgrqd', ps.astype(vs.dtype), vs)
        kw = lax.dynamic_slice_in_dim(kw_pad, i * Q_BLOCK, WINDOW + Q_BLOCK, axis=2)
        vw = lax.dynamic_slice_in_dim(vw_pad, i * Q_BLOCK, WINDOW + Q_BLOCK, axis=2)
        p = i * Q_BLOCK - WINDOW + jnp.arange(WINDOW + Q_BLOCK)
        mw = (p[None, :] <= tq[:, None]) & (p[None, :] > tq[:, None] - WINDOW) & (p[None, :] >= 0)
        sw = jnp.einsum('bgrqd,bgkd->bgrqk', qb, kw) * scale
        pw = masked_softmax(sw, mw)
        o_w = jnp.einsum('bgrqk,bgkd->bgrqd', pw.astype(vw.dtype), vw)
        return o_s, o_w

    o_slc, o_win = lax.map(block_step, (q_blocks, idx_blocks, jnp.arange(nb)))
    o_slc = o_slc.transpose(1, 2, 3, 0, 4, 5).reshape(b, g_n, r_n, s_len, HEAD_DIM)
    o_win = o_win.transpose(1, 2, 3, 0, 4, 5).reshape(b, g_n, r_n, s_len, HEAD_DIM)

    gates = jax.nn.sigmoid(branch_gate.astype(jnp.float32)).reshape(b, s_len, NSA_BRANCHES, g_n, r_n)
    gates = gates.transpose(2, 0, 3, 4, 1)[..., None]
    o = gates[0] * o_cmp + gates[1] * o_slc + gates[2] * o_win
    return o.transpose(0, 3, 1, 2, 4).reshape(b, s_len, NSA_WIDTH)


def hybrid_layer(x, pos, norm_g, w_in, q_norm_g, k_norm_g, cmp_pe, cmp_w1, cmp_b1, cmp_w2,
                 w_up_a, w_up_b, w_out):
    h = rms_norm(x, norm_g)
    proj = h @ w_in
    (sb_q, sb_k, sb_v, sb_z, n_q, n_kc, n_vc, n_ks, n_vs, n_kw, n_vw,
     n_gate, n_z, gate_a, gate_b) = jnp.split(proj, list(np.cumsum(IN_SPLITS)[:-1]), axis=-1)

    o_a = stick_breaking_attention(split_heads(sb_q, SB_HEADS), split_heads(sb_k, SB_HEADS),
                                   split_heads(sb_v, SB_HEADS))
    y_a = merge_heads(o_a) * jax.nn.silu(sb_z)

    q = rope(rms_norm(split_heads(n_q, NSA_HEADS), q_norm_g), pos)
    k_slc = rope(rms_norm(split_heads(n_ks, NSA_KV_HEADS), k_norm_g[1]), pos)
    k_win = rope(rms_norm(split_heads(n_kw, NSA_KV_HEADS), k_norm_g[2]), pos)
    o_b = nsa_attention(q, split_heads(n_kc, NSA_KV_HEADS), split_heads(n_vc, NSA_KV_HEADS),
                        k_slc, split_heads(n_vs, NSA_KV_HEADS),
                        k_win, split_heads(n_vw, NSA_KV_HEADS),
                        n_gate, pos, k_norm_g[0], cmp_pe, cmp_w1, cmp_b1, cmp_w2)
    y_b = o_b * jax.nn.silu(n_z)

    merged = jax.nn.sigmoid(gate_a) * (y_a @ w_up_a) + jax.nn.sigmoid(gate_b) * (y_b @ w_up_b)
    return (x + merged @ w_out).astype(x.dtype)


def setup_inputs(seed: int = 0) -> dict:
    key = jax.random.key(seed)
    ks = jax.random.split(key, 14)
    f32 = jnp.float32
    x = jax.random.normal(ks[0], (BATCH, SEQ, D_MODEL), f32)
    offsets = jax.random.randint(ks[1], (BATCH, 1), 0, SEQ, dtype=jnp.int32)
    positions = (jnp.arange(SEQ, dtype=jnp.int32)[None, :] + offsets).astype(jnp.int32)
    norm_g = 1.0 + 0.02 * jax.random.normal(ks[2], (DEPTH, D_MODEL), f32)
    w_in = jax.random.normal(ks[3], (DEPTH, D_MODEL, N_IN), f32) * D_MODEL ** -0.5
    q_norm_g = 1.0 + 0.02 * jax.random.normal(ks[4], (DEPTH, HEAD_DIM), f32)
    k_norm_g = 1.0 + 0.02 * jax.random.normal(ks[5], (DEPTH, NSA_BRANCHES, HEAD_DIM), f32)
    cmp_pe = 0.1 * jax.random.normal(ks[6], (DEPTH, 2, CMP_LEN, HEAD_DIM), f32)
    cmp_w1 = jax.random.normal(ks[7], (DEPTH, 2, CMP_LEN * HEAD_DIM, CMP_HIDDEN), f32) * (CMP_LEN * HEAD_DIM) ** -0.5
    cmp_b1 = 0.01 * jax.random.normal(ks[8], (DEPTH, 2, CMP_HIDDEN), f32)
    cmp_w2 = jax.random.normal(ks[9], (DEPTH, 2, CMP_HIDDEN, HEAD_DIM), f32) * CMP_HIDDEN ** -0.5
    w_up_a = jax.random.normal(ks[10], (DEPTH, SB_WIDTH, D_MODEL), f32) * SB_WIDTH ** -0.5
    w_up_b = jax.random.normal(ks[11], (DEPTH, NSA_WIDTH, D_MODEL), f32) * NSA_WIDTH ** -0.5
    w_out = jax.random.normal(ks[12], (DEPTH, D_MODEL, D_MODEL), f32) * D_MODEL ** -0.5
    return {'x': x, 'positions': positions, 'norm_g': norm_g, 'w_in': w_in,
            'q_norm_g': q_norm_g, 'k_norm_g': k_norm_g, 'cmp_pe': cmp_pe,
            'cmp_w1': cmp_w1, 'cmp_b1': cmp_b1, 'cmp_w2': cmp_w2,
            'w_up_a': w_up_a, 'w_up_b': w_up_b, 'w_out': w_out}


def reference(x, positions, norm_g, w_in, q_norm_g, k_norm_g, cmp_pe, cmp_w1, cmp_b1, cmp_w2,
              w_up_a, w_up_b, w_out):
    for layer in range(DEPTH):
        x = hybrid_layer(x, positions, norm_g[layer], w_in[layer], q_norm_g[layer], k_norm_g[layer],
                         cmp_pe[layer], cmp_w1[layer], cmp_b1[layer], cmp_w2[layer],
                         w_up_a[layer], w_up_b[layer], w_out[layer])
    return x
```

```python
import contextlib
import math
import numpy as np
import concourse.bass as bass
import concourse.mybir as mybir
from concourse.bass_utils import run_bass_kernel_spmd

F32 = mybir.dt.float32
BF16 = mybir.dt.bfloat16
I32 = mybir.dt.int32
AF = mybir.ActivationFunctionType
ALU = mybir.AluOpType
AX = mybir.AxisListType

S = 2048
D = 1024
NIN = 5912
NCORES = 8
NB = 2
DEPTH = 4
NEG = -30000.0
EPS = 1e-6

ENGS = ["tensor", "vector", "scalar", "gpsimd", "sync"]
EPOCH = 30000
NDMASEM = 12


DEFCOST = {"tensor": 560.0, "scalar": 600.0, "vector": 650.0, "gpsimd": 1200.0, "sync": 60.0}
WINDOW = 64


class Sched:
    def __init__(self, nc, stack):
        self.nc = nc
        self.stack = stack
        self.cnt = {e: 0 for e in ENGS}
        self.esem = {}
        for e in ENGS:
            if e != "sync":
                self.esem[e] = stack.enter_context(nc.semaphore(f"es_{e}_0"))
        self.eepoch = {e: 0 for e in ENGS}
        self.dsem = {}
        self.dcnt = {}
        self.dnext = {}
        for q in ["sync", "gpsimd", "scalar"]:
            self.dsem[q] = [stack.enter_context(nc.semaphore(f"ds_{q}_{i}")) for i in range(NDMASEM)]
            self.dcnt[q] = [0] * NDMASEM
            self.dnext[q] = 0
        self.ninstr = 0
        self.reorder = True
        self._reset()

    def _reset(self):
        self.nodes = []
        self.last_w = {}
        self.last_r = {}

    def op(self, eng, fn, reads=(), writes=(), dma=False, c=None):
        preds = set()
        for b in reads:
            w = self.last_w.get(b)
            if w is not None:
                preds.add(w)
            if b.startswith("bank"):
                for r_ in self.last_r.get(b, ()):
                    if self.nodes[r_][0] != eng:
                        preds.add(r_)
        for b in writes:
            w = self.last_w.get(b)
            if w is not None:
                preds.add(w)
            preds.update(self.last_r.get(b, ()))
        nid = len(self.nodes)
        if c is None:
            c = 3000.0 if dma else DEFCOST[eng]
        self.nodes.append((eng, fn, dma, preds, float(c)))
        for b in reads:
            self.last_r.setdefault(b, []).append(nid)
        for b in writes:
            self.last_w[b] = nid
            self.last_r[b] = []
        return nid

    def _simulate(self):
        import heapq
        nodes = self.nodes
        n = len(nodes)
        order = {e: [] for e in ENGS}
        if not self.reorder:
            for nid, nd in enumerate(nodes):
                order[nd[0]].append(nid)
            return order
        succ = [[] for _ in range(n)]
        indeg = [0] * n
        for nid, nd in enumerate(nodes):
            for p in nd[3]:
                succ[p].append(nid)
            indeg[nid] = len(nd[3])
        ready = {e: [] for e in ENGS}
        for nid, nd in enumerate(nodes):
            if indeg[nid] == 0:
                heapq.heappush(ready[nd[0]], nid)
        free_at = {e: 0.0 for e in ENGS}
        events = []
        now = 0.0
        left = n
        while left:
            progressed = False
            for e in ENGS:
                if free_at[e] <= now and ready[e]:
                    nid = heapq.heappop(ready[e])
                    nd = nodes[nid]
                    if nd[2]:
                        free_at[e] = now + 60.0
                        heapq.heappush(events, (free_at[e], -1))
                    else:
                        free_at[e] = now + nd[4]
                    heapq.heappush(events, (now + nd[4], nid))
                    order[e].append(nid)
                    left -= 1
                    progressed = True
            if not progressed:
                assert events, "scheduler deadlock"
                t, nid = heapq.heappop(events)
                now = max(now, t)
                while True:
                    if nid >= 0:
                        for sx in succ[nid]:
                            indeg[sx] -= 1
                            if indeg[sx] == 0:
                                heapq.heappush(ready[nodes[sx][0]], sx)
                    if events and events[0][0] <= now:
                        t, nid = heapq.heappop(events)
                    else:
                        break
        return order

    def flush(self):
        nc = self.nc
        nodes = self.nodes
        order = self._simulate()
        tok = [None] * len(nodes)
        extra = {}
        for e in ENGS:
            for nid in order[e]:
                if nodes[nid][2]:
                    i = self.dnext[e]
                    self.dnext[e] = (i + 1) % NDMASEM
                    sem = self.dsem[e][i]
                    if self.dcnt[e][i] > 0:
                        extra[nid] = (sem, self.dcnt[e][i])
                    self.dcnt[e][i] += 16
                    tok[nid] = (sem, self.dcnt[e][i], 16)
                else:
                    if self.cnt[e] >= EPOCH:
                        self.eepoch[e] += 1
                        self.esem[e] = self.stack.enter_context(nc.semaphore(f"es_{e}_{self.eepoch[e]}"))
                        self.cnt[e] = 0
                    self.cnt[e] += 1
                    tok[nid] = (self.esem[e], self.cnt[e], 1)
        prog = {}
        for e in ENGS:
            seen = {}
            lst = []
            for nid in order[e]:
                waits = {}
                cands = [tok[p][:2] for p in nodes[nid][3]]
                if nid in extra:
                    cands.append(extra[nid])
                for (sm, v) in cands:
                    k = id(sm)
                    if seen.get(k, 0) >= v:
                        continue
                    if k not in waits or waits[k][1] < v:
                        waits[k] = (sm, v)
                wl = list(waits.values())
                for (sm, v) in wl:
                    seen[id(sm)] = v
                lst.append((wl, nodes[nid][1], tok[nid][0], tok[nid][2]))
                self.ninstr += 1 + len(wl)
            prog[e] = lst
        finals = {}
        for q in ["sync", "gpsimd", "scalar"]:
            finals[q] = [(self.dsem[q][i], self.dcnt[q][i]) for i in range(NDMASEM) if self.dcnt[q][i] > 0]
        with nc.Block() as block:
            def mk(ename):
                def body(eng):
                    for (wl, fn, sem, inc) in prog[ename]:
                        for (sm, v) in wl:
                            eng.wait_ge(sm, v)
                        fn(eng).then_inc(sem, inc)
                    for (sm, v) in finals.get(ename, []):
                        eng.wait_ge(sm, v)
                return body
            block.tensor(mk("tensor"))
            block.vector(mk("vector"))
            block.scalar(mk("scalar"))
            block.gpsimd(mk("gpsimd"))
            block.sync(mk("sync"))
        self._reset()


class Ring:
    def __init__(self, bufs, name):
        self.bufs = bufs
        self.name = name
        self.i = 0

    def next(self):
        j = self.i % len(self.bufs)
        self.i += 1
        return self.bufs[j], f"{self.name}{j}"


def bc4(ap2d, n=4):
    p, f = ap2d.shape
    return ap2d.unsqueeze(1).broadcast_to([p, n, f])


STOPAT = 99
B0PART = 99


def build_program(NL):
    nc = bass.Bass("TRN2", target_bir_lowering=False)
    dt_in = lambda name, shape, dt=F32: nc.dram_tensor(name, shape, dt, kind="ExternalInput").ap()
    x_in = dt_in("x", [NB, S, D])
    pos_in = dt_in("positions", [NB, S], I32)
    norm_g = dt_in("norm_g", [NL, D])
    w_in = dt_in("w_in", [NL, D, NIN])
    q_norm_g = dt_in("q_norm_g", [NL, 64])
    k_norm_g = dt_in("k_norm_g", [NL, 3, 64])
    cmp_pe = dt_in("cmp_pe", [NL, 2, 32, 64])
    cmp_w1 = dt_in("cmp_w1", [NL, 2, 2048, 256])
    cmp_b1 = dt_in("cmp_b1", [NL, 2, 256])
    cmp_w2 = dt_in("cmp_w2", [NL, 2, 256, 64])
    w_up_a = dt_in("w_up_a", [NL, 512, D])
    w_up_b = dt_in("w_up_b", [NL, 512, D])
    w_out = dt_in("w_out", [NL, D, D])
    invf_in = dt_in("invf", [128, 1])
    out = nc.dram_tensor("out", [NB, S, D], F32, kind="ExternalOutput").ap()
    yTa_d = nc.dram_tensor("yTa_scr", [512, S], BF16).ap()
    yTb_d = nc.dram_tensor("yTb_scr", [512, S], BF16).ap()
    nzT_d = nc.dram_tensor("nzT_scr", [512, S], BF16).ap()

    with contextlib.ExitStack() as st:
        sc = Sched(nc, st)
        E = st.enter_context
        op = sc.op

        uid = [0]

        def sb(name, shape, dt, stack=None):
            uid[0] += 1
            return (stack if stack is not None else st).enter_context(nc.sbuf_tensor(f"{name}_{uid[0]}", shape, dt))

        ident_bf = sb("ident_bf", [128, 128], BF16)
        ident_f = sb("ident_f", [128, 128], F32)
        negtri = sb("negtri", [128, 128], BF16)
        negones = sb("negones", [128, 128], BF16)
        ones_blk = sb("ones_blk", [128, 128], BF16)
        rotM_blk = sb("rotM_blk", [128, 128], BF16)
        maskS = sb("maskS", [128, 128], BF16)
        maskC = sb("maskC", [128, 128], BF16)
        maskW = sb("maskW", [128, 128], BF16)
        Eexp = sb("Eexp", [32, S], BF16)
        cmask = sb("cmask", [127, S], BF16)
        bonus = sb("bonus", [128, 16, 32], F32)
        bonusF = sb("bonusF", [128, 16, 32], F32)
        invf = sb("invf_sb", [128, 1], F32)
        hT = sb("hT", [128, 8, S], BF16)
        cosT = sb("cosT", [128, S], F32)
        sinT = sb("sinT", [128, S], F32)
        cosC = sb("cosC", [64, 127], F32)
        sinC = sb("sinC", [64, 127], F32)
        VO = sb("VO", [127, 2, 97], BF16)
        vs_aug = sb("vs_aug", [128, 16, 2, 65], BF16)
        vw_aug = sb("vw_aug", [128, 16, 2, 65], BF16)
        kcT = sb("kcT", [64, 2, 127], BF16)
        gq = sb("gq", [128, 1], F32)
        gk = sb("gk", [128, 3], F32)
        wst = Ring([sb(f"wst{i}", [128, 8, 128], F32) for i in range(2)], "wst")
        banks = [E(nc.psum_tensor(f"bank{i}", [128, 512], F32)) for i in range(8)]


        def sel(t, ap, pattern, cop, fill, base, cm, key):
            op("gpsimd", lambda e: e.affine_select(out=ap, in_=ap, pattern=pattern, compare_op=cop,
                                                   fill=fill, base=base, channel_multiplier=cm),
               reads=[key], writes=[key])

        def mset(ap, val, key):
            op("gpsimd", lambda e: e.memset(ap, val), writes=[key])

        mset(ident_bf[:], 0.0, "ident_bf")
        sel(ident_bf, ident_bf[:], [[-1, 128]], ALU.not_equal, 1.0, 0, 1, "ident_bf")
        mset(ident_f[:], 0.0, "ident_f")
        sel(ident_f, ident_f[:], [[-1, 128]], ALU.not_equal, 1.0, 0, 1, "ident_f")
        mset(negtri[:], -1.0, "negtri")
        sel(negtri, negtri[:], [[-1, 128]], ALU.is_ge, 0.0, 0, 1, "negtri")
        mset(negones[:], -1.0, "negones")
        mset(ones_blk[:], 1.0, "ones_blk")
        mset(ones_blk[0:64, 64:128], 0.0, "ones_blk")
        mset(ones_blk[64:128, 0:64], 0.0, "ones_blk")
        mset(rotM_blk[:], 0.0, "rotM_blk")
        for q0 in (0, 64):
            sel(rotM_blk, rotM_blk[q0:q0 + 64, q0:q0 + 64], [[-1, 64]], ALU.not_equal, -1.0, -32, 1, "rotM_blk")
            sel(rotM_blk, rotM_blk[q0:q0 + 64, q0:q0 + 64], [[-1, 64]], ALU.not_equal, 1.0, 32, 1, "rotM_blk")
        mset(maskS[:], 0.0, "maskS")
        sel(maskS, maskS[:], [[1, 128]], ALU.is_ge, NEG, -1, -1, "maskS")
        mset(maskC[:], 0.0, "maskC")
        sel(maskC, maskC[:], [[1, 128]], ALU.is_ge, NEG, 0, -1, "maskC")
        mset(maskW[:], 0.0, "maskW")
        sel(maskW, maskW[:], [[-1, 128]], ALU.is_ge, NEG, -1, 1, "maskW")
        mset(Eexp[:], 1.0, "Eexp")
        sel(Eexp, Eexp[:], [[1, S]], ALU.is_ge, 0.0, 0, -64, "Eexp")
        sel(Eexp, Eexp[:], [[-1, S]], ALU.is_ge, 0.0, 63, 64, "Eexp")
        mset(cmask[:], 0.0, "cmask")
        sel(cmask, cmask[:], [[1, S]], ALU.is_ge, NEG, -31, -16, "cmask")
        mset(VO[:], 1.0, "VO")
        sel(VO, VO[:, :, 65:97], [[0, 2], [64, 32]], ALU.is_ge, 0.0, 63, -16, "VO")
        sel(VO, VO[:, :, 65:97], [[0, 2], [-64, 32]], ALU.is_ge, 0.0, 31, 16, "VO")
        mset(vs_aug[:], 1.0, "vs_aug")
        mset(vw_aug[:], 1.0, "vw_aug")
        mset(bonus[:], 0.0, "bonus")
        sel(bonus, bonus[:], [[128, 16], [-64, 32]], ALU.is_ge, -1e30, 0, 1, "bonus")
        mset(bonusF[:], 1e4, "bonusF")
        sel(bonusF, bonusF[:], [[128, 16], [-64, 32]], ALU.is_ge, 0.0, 0, 1, "bonusF")
        sel(bonusF, bonusF[:], [[-128, 16], [64, 32]], ALU.is_ge, 0.0, 127, -1, "bonusF")
        op("gpsimd", lambda e: e.tensor_tensor(out=bonus[:], in0=bonus[:], in1=bonusF[:], op=ALU.add),
           reads=["bonus", "bonusF"], writes=["bonus"])
        mset(bonus[:, :, 0:1], 1e4, "bonus")
        op("sync", lambda e: e.dma_start(out=invf[:], in_=invf_in[:, :]), writes=["invf"], dma=True)
        sc.flush()

        def load_w(dst_ap, src_ap, key, np_=128):
            stg, skey = wst.next()
            a, b = src_ap.shape[1], src_ap.shape[2]
            sv = stg[0:np_, 0:a, 0:b]
            op("sync", lambda e: e.dma_start(out=sv, in_=src_ap), writes=[skey], dma=True)
            op("gpsimd", lambda e: e.tensor_copy(out=dst_ap, in_=sv), reads=[skey], writes=[key])

        def win_cols(l, c0, n):
            return w_in[l, :, c0:c0 + n].rearrange("(k p) m -> p k m", p=128)

        def mm_chain(out_ap, pairs, okey, rkeys):
            n = len(pairs)
            for j, (lh, rh) in enumerate(pairs):
                op("tensor", lambda e, lh=lh, rh=rh, j=j: e.matmul(out_ap, lhsT=lh, rhs=rh,
                                                                    start=(j == 0), stop=(j == n - 1)),
                   reads=rkeys, writes=[okey])

        for b in range(NB):
            with contextlib.ExitStack() as ss_:
                posi = sb("posi", [128, S], I32, ss_)
                posf = sb("posf", [128, S], F32, ss_)
                vv = sb("vv", [128, S], F32, ss_)
                uu = sb("uu", [128, S], F32, ss_)
                ui = sb("ui", [128, S], I32, ss_)
                uf = sb("uf", [128, S], F32, ss_)
                gg = sb("gg", [128, S], F32, ss_)
                mm_ = sb("mm_", [128, S], F32, ss_)
                posm = sb("posm", [128, 127], F32, ss_)
                vm = sb("vm", [128, 127], F32, ss_)
                op("sync", lambda e: e.dma_start(out=posi[:], in_=pos_in[b].partition_broadcast(128)),
                   writes=["posi"], dma=True)
                op("vector", lambda e: e.tensor_copy(out=posf[:], in_=posi[:]), reads=["posi"], writes=["posf"])
                op("vector", lambda e: e.tensor_scalar(out=vv[:], in0=posf[:], scalar1=invf[:, 0:1], scalar2=None,
                                                       op0=ALU.mult), reads=["posf", "invf"], writes=["vv"])
                win = bass.AP(posf[:].tensor, posf[:].offset, [[S, 128], [16, 127], [1, 32]])
                op("vector", lambda e: e.reduce_sum(out=posm[:], in_=win, axis=AX.X), reads=["posf"], writes=["posm"])
                op("vector", lambda e: e.tensor_scalar(out=vm[:], in0=posm[:], scalar1=invf[:, 0:1], scalar2=1.0 / 32,
                                                       op0=ALU.mult, op1=ALU.mult),
                   reads=["posm", "invf"], writes=["vm"])

                def table(vsrc, n, add, dst, dkey, skey, P=128):
                    u = uu[0:P, 0:n]
                    op("vector", lambda e: e.tensor_scalar(out=u, in0=vsrc, scalar1=float(add), scalar2=None,
                                                           op0=ALU.add), reads=[skey], writes=["uu"])
                    op("vector", lambda e: e.tensor_copy(out=ui[0:P, 0:n], in_=u), reads=["uu"], writes=["ui"])
                    op("vector", lambda e: e.tensor_copy(out=uf[0:P, 0:n], in_=ui[0:P, 0:n]), reads=["ui"], writes=["uf"])
                    op("vector", lambda e: e.tensor_tensor(out=gg[0:P, 0:n], in0=u, in1=uf[0:P, 0:n], op=ALU.subtract),
                       reads=["uu", "uf"], writes=["gg"])
                    op("vector", lambda e: e.scalar_tensor_tensor(out=mm_[0:P, 0:n], in0=gg[0:P, 0:n], scalar=0.5,
                                                                  in1=gg[0:P, 0:n], op0=ALU.is_gt, op1=ALU.subtract),
                       reads=["gg"], writes=["mm_"])
                    op("scalar", lambda e: e.activation(out=dst, in_=mm_[0:P, 0:n], func=AF.Sin,
                                                        scale=-2.0 * math.pi), reads=["mm_"], writes=[dkey])

                table(vv[:], S, 0.0, sinT[:], "sinT", "vv")
                table(vv[:], S, 0.25, cosT[:], "cosT", "vv")
                table(vm[0:64, :], 127, 0.0, sinC[:], "sinC", "vm", P=64)
                table(vm[0:64, :], 127, 0.25, cosC[:], "cosC", "vm", P=64)
                sc.flush()

            for l in range(NL):
                x_src = x_in if l == 0 else out
                with contextlib.ExitStack() as s1:
                    gqr = sb("gqr", [128, 1], F32, s1)
                    gbc = sb("gbc", [128, D], F32, s1)
                    xt = [sb(f"p1xt{i}", [128, D], F32, s1) for i in range(2)]
                    junk = sb("p1junk", [128, D], BF16, s1)
                    xs = [sb(f"p1xs{i}", [128, D], BF16, s1) for i in range(2)]
                    ssq = [sb(f"p1ss{i}", [128, 1], F32, s1) for i in range(2)]
                    rt_ = [sb(f"p1rt{i}", [128, 1], F32, s1) for i in range(2)]
                    rs_ = [sb(f"p1rs{i}", [128, 1], F32, s1) for i in range(2)]
                    for q0 in (0, 64):
                        op("sync", lambda e, q0=q0: e.dma_start(out=gqr[q0:q0 + 64, :], in_=q_norm_g[l].rearrange("(p o) -> p o", o=1)),
                           writes=["gqr"], dma=True)
                    op("vector", lambda e: e.tensor_scalar(out=gq[:], in0=gqr[:], scalar1=0.125, scalar2=None,
                                                           op0=ALU.mult), reads=["gqr"], writes=["gq"])
                    for j in range(3):
                        for q0 in (0, 64):
                            op("sync", lambda e, j=j, q0=q0: e.dma_start(out=gk[q0:q0 + 64, j:j + 1],
                                                                         in_=k_norm_g[l, j].rearrange("(p o) -> p o", o=1)),
                               writes=["gk"], dma=True)
                    op("sync", lambda e: e.dma_start(out=gbc[:], in_=norm_g[l].partition_broadcast(128)),
                       writes=["gbc"], dma=True)
                    for tt in range(16):
                        i = tt % 2
                        pT = banks[i][:].bitcast(BF16)
                        op("sync", lambda e, tt=tt, i=i: e.dma_start(out=xt[i][:], in_=x_src[b, tt * 128:(tt + 1) * 128, :]),
                           writes=[f"xt{i}"], dma=True)
                        op("scalar", lambda e, i=i: e.activation(out=junk[:], in_=xt[i][:], func=AF.Square,
                                                                 accum_out=ssq[i][:]),
                           reads=[f"xt{i}"], writes=["junk", f"ss{i}"])
                        op("scalar", lambda e, i=i: e.activation(out=rt_[i][:], in_=ssq[i][:], func=AF.Sqrt,
                                                                 scale=1.0 / D, bias=EPS),
                           reads=[f"ss{i}"], writes=[f"rt{i}"])
                        op("vector", lambda e, i=i: e.reciprocal(out=rs_[i][:], in_=rt_[i][:]),
                           reads=[f"rt{i}"], writes=[f"rs{i}"])
                        op("vector", lambda e, i=i: e.scalar_tensor_tensor(out=xs[i][:], in0=xt[i][:], scalar=rs_[i][:],
                                                                           in1=gbc[:], op0=ALU.mult, op1=ALU.mult),
                           reads=[f"xt{i}", f"rs{i}", "gbc"], writes=[f"xs{i}"])
                        for k in range(8):
                            op("tensor", lambda e, i=i, k=k, pT=pT: e.transpose(out=pT[:, k * 128:(k + 1) * 128],
                                                                                in_=xs[i][:, k * 128:(k + 1) * 128],
                                                                                identity=ident_bf[:]),
                               reads=[f"xs{i}"], writes=[f"bank{i}"])
                        op("scalar", lambda e, i=i, tt=tt, pT=pT: e.copy(out=hT[:, :, tt * 128:(tt + 1) * 128],
                                                                         in_=pT.rearrange("p (k t) -> p k t", k=8)),
                           reads=[f"bank{i}"], writes=["hT"])
                    sc.flush()

                if STOPAT <= 1:
                    continue
                with contextlib.ExitStack() as sa:
                    XP = [[sb(f"XP{p}{i}", [128, S], BF16, sa) for i in range(3)] for p in range(2)]
                    XO = [[sb(f"XO{p}{i}", [64, S], BF16, sa) for i in range(3)] for p in range(2)]
                    v_all = sb("v_all", [128, 16, 512], BF16, sa)
                    wv_all = sb("wv_all", [128, 8, 512], BF16, sa)
                    wA2 = [[sb(f"wA{p}{i}", [128, 8, 128], BF16, sa) for i in range(3)] for p in range(2)]
                    e_sb = [sb(f"e_sb{i}", [128, 512], F32, sa) for i in range(2)]
                    sp_bf = [[sb(f"sp_bf{s}{i}", [128, 512], BF16, sa) for i in range(2)] for s in range(2)]
                    R32 = [sb(f"R32{s}", [128, 512], F32, sa) for s in range(2)]
                    Rbf = [[sb(f"Rbf{s}{i}", [128, 512], BF16, sa) for i in range(2)] for s in range(2)]
                    w_bf = [[sb(f"w_bf{s}{i}", [128, 512], BF16, sa) for i in range(2)] for s in range(2)]
                    ya_sb = [sb(f"ya_sb{i}", [64, 512], BF16, sa) for i in range(2)]
                    ringA = Ring(banks[6:8], "bank6")

                    def ringA_next():
                        j = ringA.i % 2
                        ringA.i += 1
                        return banks[6 + j], f"bank{6 + j}"

                    for c in range(4):
                        load_w(wv_all[:, :, c * 128:(c + 1) * 128], win_cols(l, 1024 + c * 128, 128), f"wv_all{c}")
                    for tt in range(16):
                        ps, pk = ringA_next()
                        mm_chain(ps[:], [(hT[:, k, tt * 128:(tt + 1) * 128], wv_all[:, k, :]) for k in range(8)],
                                 pk, ["wv_all0", "wv_all1", "wv_all2", "wv_all3", "hT"])
                        if tt % 2 == 0:
                            op("vector", lambda e, ps=ps, tt=tt: e.tensor_copy(out=v_all[:, tt, :], in_=ps[:]),
                               reads=[pk], writes=["v_all"])
                        else:
                            op("scalar", lambda e, ps=ps, tt=tt: e.copy(out=v_all[:, tt, :], in_=ps[:]),
                               reads=[pk], writes=["v_all"])

                    for hp in range(4):
                        pp = hp % 2
                        wA = wA2[pp]
                        PK = f"p{pp}"
                        qT = [XP[pp][0], XO[pp][0]]
                        kT = [XP[pp][1], XO[pp][1]]
                        szT = [XP[pp][2], XO[pp][2]]
                        for wi, sec in enumerate((0, 1, 3)):
                            load_w(wA[wi][:], win_cols(l, sec * 512 + hp * 128, 128), f"wA{wi}" + PK)
                        for wi in range(3):
                            for tb in range(4):
                                ps, pk = ringA_next()
                                csl = slice(tb * 512, (tb + 1) * 512)
                                mm_chain(ps[:], [(wA[wi][:, k, :], hT[:, k, csl]) for k in range(8)], pk, [f"wA{wi}" + PK, "hT"])
                                d_ap = XP[pp][wi][:, csl]
                                ek = f"XP{wi}t{tb}" + PK
                                if wi == 0:
                                    op("scalar", lambda e, ps=ps, d_ap=d_ap: e.activation(out=d_ap, in_=ps[:], func=AF.Copy, scale=0.125),
                                       reads=[pk], writes=[ek])
                                elif wi == 1:
                                    op("vector", lambda e, ps=ps, d_ap=d_ap: e.tensor_copy(out=d_ap, in_=ps[:]),
                                       reads=[pk], writes=[ek])
                                else:
                                    op("scalar", lambda e, ps=ps, d_ap=d_ap: e.activation(out=d_ap, in_=ps[:], func=AF.Silu),
                                       reads=[pk], writes=[ek])
                                op("sync", lambda e, pp=pp, wi=wi, csl=csl: e.dma_start(out=XO[pp][wi][:, csl], in_=XP[pp][wi][64:128, csl]),
                                   reads=[ek], writes=[f"XO{wi}t{tb}" + PK], dma=True, c=2500)

                        def xkeys(wi, s, qb=None):
                            nm = "XP" if s == 0 else "XO"
                            if qb is None:
                                return [f"{nm}{wi}t{t}" + PK for t in range(4)]
                            return [f"{nm}{wi}t{qb}" + PK]

                        tiles = []
                        for qb in range(4):
                            nt = 4 * qb + 4
                            for kt in range(nt - 1, -1, -1):
                                tiles.append((qb, kt, kt == nt - 1, kt == 0))

                        def c0_of(qb, kt):
                            return max(kt - 4 * qb, 0) * 128

                        def emit_qk(n, s, kT=kT, qT=qT, xkeys=xkeys):
                            qb, kt, first, last = tiles[n]
                            zb = banks[2 * s + (n % 2)]
                            zk = f"bank{2 * s + (n % 2)}"
                            diag = kt >= 4 * qb
                            c0 = c0_of(qb, kt)
                            op("tensor", lambda e: e.matmul(zb[:, c0:512], lhsT=kT[s][0:64, kt * 128:(kt + 1) * 128],
                                                            rhs=qT[s][0:64, qb * 512 + c0:(qb + 1) * 512], start=True, stop=True),
                               reads=xkeys(1, s) + xkeys(0, s, qb), writes=[zk], c=100 + 0.75 * (512 - c0))
                            if diag:
                                op("tensor", lambda e: e.matmul(zb[:, c0:c0 + 128], lhsT=ident_bf[:], rhs=maskS[:],
                                                                start=False, stop=True, skip_group_check=True),
                                   reads=[], writes=[zk], c=200)

                        for s in range(2):
                            emit_qk(0, s)
                        for n in range(len(tiles)):
                            qb, kt, first, last = tiles[n]
                            par = n % 2
                            c0 = c0_of(qb, kt)
                            diag = kt >= 4 * qb
                            c1 = (kt - 4 * qb + 1) * 128 if diag else 0
                            w = 512 - c0
                            if first:
                                for s in range(2):
                                    op("gpsimd", lambda e, s=s: e.memset(R32[s][:], 0.0), writes=[f"R32{s}"], c=600)
                            for s in range(2):
                                zb = banks[2 * s + par]
                                zk = f"bank{2 * s + par}"
                                op("scalar", lambda e, zb=zb, s=s, c0=c0: e.activation(out=e_sb[s][:, c0:512], in_=zb[:, c0:512], func=AF.Exp),
                                   reads=[zk], writes=[f"e_sb{s}"], c=220 + 0.72 * w)
                                op("scalar", lambda e, s=s, par=par, c0=c0: e.activation(out=sp_bf[s][par][:, c0:512], in_=e_sb[s][:, c0:512],
                                                                                         func=AF.Ln, bias=1.0),
                                   reads=[f"e_sb{s}"], writes=[f"sp_bf{s}{par}"], c=250 + 0.75 * w)
                            for s in range(2):
                                zb = banks[2 * s + par]
                                zk = f"bank{2 * s + par}"
                                op("tensor", lambda e, zb=zb, s=s, par=par, first=first, c0=c0: e.matmul(
                                    zb[:, c0:512], lhsT=negtri[:], rhs=sp_bf[s][par][:, c0:512], start=False, stop=first, skip_group_check=True),
                                   reads=[f"sp_bf{s}{par}"], writes=[zk], c=100 + 0.75 * w)
                                if not first:
                                    op("tensor", lambda e, zb=zb, s=s, par=par, c1=c1: e.matmul(
                                        zb[:, c1:512], lhsT=negones[:], rhs=Rbf[s][par][:, c1:512], start=False, stop=True, skip_group_check=True),
                                       reads=[f"Rbf{s}{par}"], writes=[zk], c=100 + 0.75 * (512 - c1))
                                if not last:
                                    op("gpsimd", lambda e, s=s, par=par, c0=c0: e.tensor_tensor(out=R32[s][:, c0:512], in0=R32[s][:, c0:512],
                                                                                                in1=sp_bf[s][par][:, c0:512], op=ALU.add),
                                       reads=[f"sp_bf{s}{par}", f"R32{s}"], writes=[f"R32{s}"], c=200 + 2.0 * w)
                                    op("vector", lambda e, s=s, par=par, c0=c0: e.tensor_copy(out=Rbf[s][1 - par][:, c0:512], in_=R32[s][:, c0:512]),
                                       reads=[f"R32{s}"], writes=[f"Rbf{s}{1 - par}"], c=100 + 1.1 * w)
                            if n + 1 < len(tiles):
                                for s in range(2):
                                    emit_qk(n + 1, s)
                            for s in range(2):
                                zb = banks[2 * s + par]
                                zk = f"bank{2 * s + par}"
                                op("scalar", lambda e, zb=zb, s=s, par=par, c0=c0: e.activation(out=w_bf[s][par][:, c0:512], in_=zb[:, c0:512], func=AF.Exp),
                                   reads=[zk], writes=[f"w_bf{s}{par}"], c=220 + 0.72 * w)
                            for s in range(2):
                                ob = banks[4 + s]
                                h = 2 * hp + s
                                op("tensor", lambda e, ob=ob, s=s, par=par, kt=kt, first=first, last=last, h=h, c0=c0: e.matmul(
                                    ob[0:64, c0:512], lhsT=v_all[:, kt, h * 64:(h + 1) * 64], rhs=w_bf[s][par][:, c0:512],
                                    start=first, stop=last, skip_group_check=True),
                                   reads=[f"w_bf{s}{par}", "v_all"], writes=[f"bank{4 + s}"], c=100 + 0.75 * w)
                                if last:
                                    op("vector", lambda e, ob=ob, s=s, qb=qb, szT=szT: e.tensor_tensor(
                                        out=ya_sb[s][:], in0=ob[0:64, :], in1=szT[s][0:64, qb * 512:(qb + 1) * 512], op=ALU.mult),
                                       reads=[f"bank{4 + s}"] + xkeys(2, s, qb), writes=[f"ya_sb{s}"])
                                    op("sync", lambda e, s=s, h=h, qb=qb: e.dma_start(out=yTa_d[h * 64:(h + 1) * 64, qb * 512:(qb + 1) * 512], in_=ya_sb[s][:]),
                                       reads=[f"ya_sb{s}"], writes=["yTa_d"], dma=True)
                    sc.flush()

                if STOPAT <= 2:
                    continue
                with contextlib.ExitStack() as sB:
                    nqT = sb("nqT", [64, 8, S], BF16, sB)
                    ksT = sb("ksT", [64, 2, S], BF16, sB)
                    kwT = sb("kwT", [64, 2, S], BF16, sB)
                    gates = sb("gates", [128, 16, 24], F32, sB)
                    selb = sb("selb", [32, 2, S], BF16, sB)
                    kcr = sb("kcr", [64, 2, S], BF16, sB)
                    vcr = sb("vcr", [64, 2, S], BF16, sB)
                    ringB = Ring(banks[3:8], "bank")

                    def ringB_next():
                        j = 3 + (ringB.i % 5)
                        ringB.i += 1
                        return banks[j], f"bank{j}"

                    nr = {nm: [sb(f"nr_{nm}{i}", [128, 512], dt_, sB) for i in range(2)]
                          for nm, dt_ in (("sq", BF16), ("rt", F32), ("qn", BF16), ("t1", F32), ("t2", F32))}
                    nrc = [0]

                    def normrope(ps, pk, n, P, gcol, gkey, cos_ap, sin_ap, tkeys, out_ap, okey):
                        i = nrc[0] % 2
                        nrc[0] += 1
                        sq, rt, qn, t1, t2 = (nr[nm][i][0:P, 0:n] for nm in ("sq", "rt", "qn", "t1", "t2"))
                        rs = rt
                        kk = lambda nm: f"nr_{nm}{i}"
                        raw = ps[0:P, 0:n]
                        op("scalar", lambda e: e.activation(out=sq, in_=raw, func=AF.Square), reads=[pk], writes=[kk("sq")])
                        p2, p2k = ringB_next()
                        op("tensor", lambda e: e.matmul(p2[0:P, 0:n], lhsT=ones_blk[0:P, 0:P], rhs=sq, start=True, stop=True),
                           reads=[kk("sq")], writes=[p2k])
                        op("scalar", lambda e: e.activation(out=rt, in_=p2[0:P, 0:n], func=AF.Sqrt, scale=1.0 / 64, bias=EPS),
                           reads=[p2k], writes=[kk("rt")])
                        op("vector", lambda e: e.reciprocal(out=rs, in_=rt), reads=[kk("rt")], writes=[kk("rt")], c=3300)
                        op("vector", lambda e: e.scalar_tensor_tensor(out=qn, in0=raw, scalar=gcol, in1=rs,
                                                                      op0=ALU.mult, op1=ALU.mult),
                           reads=[pk, kk("rt"), gkey], writes=[kk("qn")])
                        p3, p3k = ringB_next()
                        op("tensor", lambda e: e.matmul(p3[0:P, 0:n], lhsT=rotM_blk[0:P, 0:P], rhs=qn, start=True, stop=True),
                           reads=[kk("qn")], writes=[p3k])
                        op("gpsimd", lambda e: e.tensor_tensor(out=t1, in0=qn, in1=cos_ap, op=ALU.mult),
                           reads=[kk("qn")] + tkeys, writes=[kk("t1")])
                        op("vector", lambda e: e.tensor_tensor(out=t2, in0=p3[0:P, 0:n], in1=sin_ap, op=ALU.mult),
                           reads=[p3k] + tkeys, writes=[kk("t2")])
                        op("gpsimd", lambda e: e.tensor_tensor(out=out_ap, in0=t1, in1=t2, op=ALU.add),
                           reads=[kk("t1"), kk("t2")], writes=[okey])

                    with contextlib.ExitStack() as sB0:
                        wB = [sb(f"wB{i}", [128, 8, 128], BF16, sB0) for i in range(3)]
                        wBr = Ring(wB, "wB")
                        wv4 = sb("wv4", [128, 8, 512], BF16, sB0)
                        tmpO = Ring([sb(f"tmpO{i}", [128, 512], BF16, sB0) for i in range(3)], "tmpO")

                        def proj128(tb, wt, wk):
                            ps, pk = ringB_next()
                            mm_chain(ps[:], [(wt[:, k, :], hT[:, k, tb * 512:(tb + 1) * 512]) for k in range(8)], pk, [wk, "hT"])
                            return ps, pk

                        def shift2(to, tk, dst0, dst1, kbase):
                            op("sync", lambda e: e.dma_start(out=dst0, in_=to[0:64, :]), reads=[tk], writes=[kbase + "a"], dma=True, c=2500)
                            op("sync", lambda e: e.dma_start(out=dst1, in_=to[64:128, :]), reads=[tk], writes=[kbase + "b"], dma=True, c=2500)

                        for pr in range(4):
                            wt, wk = wBr.next()
                            load_w(wt[:], win_cols(l, 2048 + pr * 128, 128), wk)
                            for tb in range(4):
                                csl = slice(tb * 512, (tb + 1) * 512)
                                ps, pk = proj128(tb, wt, wk)
                                to, tk = tmpO.next()
                                normrope(ps, pk, 512, 128, gq[:, 0:1], "gq", cosT[:, csl], sinT[:, csl], ["cosT", "sinT"], to[:], tk)
                                shift2(to, tk, nqT[:, 2 * pr, csl], nqT[:, 2 * pr + 1, csl], f"nqT{pr}_{tb}")
                        for c0, dst, dkey in ((2560, kcr, "kcr"), (2688, vcr, "vcr")) if B0PART >= 2 else ():
                            wt, wk = wBr.next()
                            load_w(wt[:], win_cols(l, c0, 128), wk)
                            for tb in range(4):
                                csl = slice(tb * 512, (tb + 1) * 512)
                                ps, pk = proj128(tb, wt, wk)
                                to, tk = tmpO.next()
                                op("vector", lambda e, ps=ps, to=to: e.tensor_copy(out=to[:], in_=ps[:]), reads=[pk], writes=[tk])
                                shift2(to, tk, dst[:, 0, csl], dst[:, 1, csl], f"{dkey}_{tb}")
                        for c0, dst, dkey, gi in ((2816, ksT, "ksT", 1), (3072, kwT, "kwT", 2)) if B0PART >= 3 else ():
                            wt, wk = wBr.next()
                            load_w(wt[:], win_cols(l, c0, 128), wk)
                            for tb in range(4):
                                csl = slice(tb * 512, (tb + 1) * 512)
                                ps, pk = proj128(tb, wt, wk)
                                to, tk = tmpO.next()
                                normrope(ps, pk, 512, 128, gk[:, gi:gi + 1], "gk", cosT[:, csl], sinT[:, csl], ["cosT", "sinT"], to[:], tk)
                                shift2(to, tk, dst[:, 0, csl], dst[:, 1, csl], f"{dkey}_{tb}")
                        for c in range(4 if B0PART >= 4 else 0):
                            n_ = 128 if c < 3 else 24
                            load_w(wv4[:, :, c * 128:c * 128 + n_], win_cols(l, 2944 + c * 128, n_), f"wv4_{c}")
                        for tt in range(16 if B0PART >= 4 else 0):
                            ps, pk = ringB_next()
                            mm_chain(ps[:, 0:408], [(hT[:, k, tt * 128:(tt + 1) * 128], wv4[:, k, 0:408]) for k in range(8)],
                                     pk, ["wv4_0", "wv4_1", "wv4_2", "wv4_3", "hT"])
                            op("vector", lambda e, ps=ps, tt=tt: e.tensor_copy(
                                out=vs_aug[:, tt, :, 0:64], in_=ps[:, 0:128].rearrange("p (g d) -> p g d", g=2)),
                               reads=[pk], writes=["vs_aug"], c=300)
                            op("vector", lambda e, ps=ps, tt=tt: e.tensor_copy(
                                out=vw_aug[:, tt, :, 0:64], in_=ps[:, 256:384].rearrange("p (g d) -> p g d", g=2)),
                               reads=[pk], writes=["vw_aug"], c=300)
                            op("scalar", lambda e, ps=ps, tt=tt: e.activation(out=gates[:, tt, :], in_=ps[:, 384:408], func=AF.Sigmoid),
                               reads=[pk], writes=["gates"], c=250)
                        for pr in range(4 if B0PART >= 5 else 0):
                            wt, wk = wBr.next()
                            load_w(wt[:], win_cols(l, 3352 + pr * 128, 128), wk)
                            for tb in range(4):
                                csl = slice(tb * 512, (tb + 1) * 512)
                                ps, pk = proj128(tb, wt, wk)
                                to, tk = tmpO.next()
                                op("scalar", lambda e, ps=ps, to=to: e.activation(out=to[:], in_=ps[:], func=AF.Silu),
                                   reads=[pk], writes=[tk])
                                op("sync", lambda e, to=to, pr=pr, csl=csl: e.dma_start(out=nzT_d[pr * 128:(pr + 1) * 128, csl], in_=to[:]),
                                   reads=[tk], writes=[f"nzT_d{pr}_{tb}"], dma=True)
                        sc.flush()

                    if STOPAT > 3:
                        with contextlib.ExitStack() as sB1:
                            w1_bf = sb("w1_bf", [64, 32, 256], BF16, sB1)
                            w1st = [sb(f"w1st{i}", [64, 4, 256], F32, sB1) for i in range(2)]
                            pe_sb = sb("pe_sb", [32, 64], F32, sB1)
                            peT = sb("peT", [64, 32], BF16, sB1)
                            b1_sb = sb("b1_sb", [128, 2], F32, sB1)
                            w2st = sb("w2st", [128, 2, 64], F32, sB1)
                            w2_bf = sb("w2_bf", [128, 2, 64], BF16, sB1)
                            cvec = sb("cvec", [128, 2], F32, sB1)
                            hid = sb("hid", [128, 2, 2, 127], BF16, sB1)
                            for kv in range(2):
                                raw = kcr if kv == 0 else vcr
                                rkey = "kcr" if kv == 0 else "vcr"
                                for c in range(8):
                                    i = c % 2
                                    src = cmp_w1[l, kv, c * 256:(c + 1) * 256, :].rearrange("(l d) h -> d l h", d=64)
                                    op("sync", lambda e, i=i, src=src: e.dma_start(out=w1st[i][:], in_=src),
                                       writes=[f"w1st{i}"], dma=True)
                                    op("gpsimd", lambda e, i=i, c=c: e.tensor_copy(out=w1_bf[:, c * 4:(c + 1) * 4, :], in_=w1st[i][:]),
                                       reads=[f"w1st{i}"], writes=["w1_bf"])
                                op("sync", lambda e, kv=kv: e.dma_start(out=pe_sb[:], in_=cmp_pe[l, kv]), writes=["pe_sb"], dma=True)
                                for hc in range(2):
                                    op("sync", lambda e, kv=kv, hc=hc: e.dma_start(
                                        out=b1_sb[:, hc:hc + 1],
                                        in_=cmp_b1[l, kv, hc * 128:(hc + 1) * 128].rearrange("(p o) -> p o", o=1)),
                                       writes=["b1_sb"], dma=True)
                                op("sync", lambda e, kv=kv: e.dma_start(
                                    out=w2st[:], in_=cmp_w2[l, kv].rearrange("(c p) d -> p c d", p=128)),
                                   writes=["w2st"], dma=True)
                                op("gpsimd", lambda e: e.tensor_copy(out=w2_bf[:], in_=w2st[:]), reads=["w2st"], writes=["w2_bf"])
                                ps, pk = ringB_next()
                                op("tensor", lambda e, ps=ps: e.transpose(out=ps[0:64, 0:32], in_=pe_sb[:], identity=ident_f[0:32, 0:32]),
                                   reads=["pe_sb"], writes=[pk])
                                op("vector", lambda e, ps=ps: e.tensor_copy(out=peT[:], in_=ps[0:64, 0:32]), reads=[pk], writes=["peT"])
                                for hc in range(2):
                                    ps, pk = ringB_next()
                                    mm_chain(ps[:, 0:1], [(w1_bf[:, li, hc * 128:(hc + 1) * 128], peT[:, li:li + 1]) for li in range(32)],
                                             pk, ["w1_bf", "peT"])
                                    op("vector", lambda e, ps=ps, hc=hc: e.tensor_tensor(out=cvec[:, hc:hc + 1], in0=ps[:, 0:1],
                                                                                         in1=b1_sb[:, hc:hc + 1], op=ALU.add),
                                       reads=[pk, "b1_sb"], writes=["cvec"])
                                for g in range(2):
                                    for hc in range(2):
                                        ps, pk = ringB_next()
                                        mm_chain(ps[:, 0:127], [(w1_bf[:, li, hc * 128:(hc + 1) * 128],
                                                                 raw[:, g, li:li + 16 * 126 + 1:16]) for li in range(32)],
                                                 pk, ["w1_bf", rkey])
                                        op("scalar", lambda e, ps=ps, hc=hc, g=g: e.activation(
                                            out=hid[:, hc, g, :], in_=ps[:, 0:127], func=AF.Silu, bias=cvec[:, hc:hc + 1]),
                                           reads=[pk, "cvec"], writes=["hid"])
                                    ps, pk = ringB_next()
                                    if kv == 0:
                                        mm_chain(ps[0:64, 0:127], [(w2_bf[:, hc, :], hid[:, hc, g, :]) for hc in range(2)],
                                                 pk, ["w2_bf", "hid"])
                                        normrope(ps, pk, 127, 64, gk[0:64, 0:1], "gk", cosC[:], sinC[:], ["cosC", "sinC"],
                                                 kcT[:, g, :], "kcT")
                                    else:
                                        mm_chain(ps[0:127, 0:64], [(hid[:, hc, g, :], w2_bf[:, hc, :]) for hc in range(2)],
                                                 pk, ["w2_bf", "hid"])
                                        op("vector", lambda e, ps=ps, g=g: e.tensor_copy(out=VO[:, g, 0:64], in_=ps[0:127, 0:64]),
                                           reads=[pk], writes=["VO"])
                            sc.flush()

                    with contextlib.ExitStack() as sB2:
                      if STOPAT > 4:
                        NP = 3
                        p_bf = [sb(f"p_bf{i}", [128, 512], BF16, sB2) for i in range(NP)]
                        pR = Ring(p_bf, "p_bf")
                        dn = [sb(f"dn{i}", [128, 3, 4], F32, sB2) for i in range(2)]
                        rdn = [sb(f"rdn{i}", [128, 3, 4], F32, sB2) for i in range(2)]
                        coef = [sb(f"coef{i}", [128, 3, 4], F32, sB2) for i in range(2)]
                        imp = [sb(f"imp{i}", [128, 32], F32, sB2) for i in range(2)]
                        top8 = [sb(f"top8{i}", [128, 8], F32, sB2) for i in range(2)]
                        sbt = [sb(f"sbt{i}", [128, 32], F32, sB2) for i in range(2)]
                        obs = [sb(f"obs{i}", [128, 4, 64], F32, sB2) for i in range(2)]
                        nz_t = [sb(f"nz_t{i}", [64, 4, 128], BF16, sB2) for i in range(2)]
                        yb_sb = [sb(f"yb_sb{i}", [64, 4, 128], BF16, sB2) for i in range(2)]
                        ocb, osb, owb = banks[0], banks[1], banks[2]
                        oTs, oTw = banks[3], banks[4]
                        oT_sb = [[sb(f"oT_sb{a}{i}", [65, 512], F32, sB2) for i in range(2)] for a in range(2)]
                        rb2 = [0]

                        def ringB_next():
                            j = 5 + (rb2[0] % 3)
                            rb2[0] += 1
                            return banks[j], f"bank{j}"

                        oc = ocb[:, 0:388].rearrange("p (r c) -> p r c", r=4)
                        os_ = osb[:, 0:260].rearrange("p (r c) -> p r c", r=4)
                        ow = owb[:, 0:260].rearrange("p (r c) -> p r c", r=4)
                        it = 0
                        for g in range(2):
                            for i in range(16):
                                u = it % 2
                                it += 1
                                tsl = slice(i * 128, (i + 1) * 128)
                                q_ap = nqT[:, 4 * g:4 * g + 4, tsl]
                                op("sync", lambda e, u=u, g=g, tsl=tsl: e.dma_start(out=nz_t[u][:], in_=nzT_d[256 * g:256 * (g + 1), tsl].rearrange("(r d) t -> d r t", d=64)),
                                   reads=[], writes=[f"nz_t{u}"], dma=True)
                                ps, pk = ringB_next()
                                op("tensor", lambda e, ps=ps, g=g, q_ap=q_ap: e.matmul(ps[0:127, :], lhsT=kcT[:, g, :], rhs=q_ap,
                                                                                       start=True, stop=False),
                                   reads=["kcT", "nqT"], writes=[pk])
                                op("tensor", lambda e, ps=ps, tsl=tsl: e.matmul(ps[0:127, :], lhsT=ident_bf[0:127, 0:127],
                                                                                rhs=bc4(cmask[:, tsl]), start=False, stop=True),
                                   reads=[], writes=[pk])
                                pb, pbk = pR.next()
                                op("scalar", lambda e, ps=ps, pb=pb: e.activation(out=pb[0:127, :], in_=ps[0:127, :], func=AF.Exp),
                                   reads=[pk], writes=[pbk])
                                for r in range(4):
                                    op("tensor", lambda e, pb=pb, r=r, g=g: e.matmul(oc[:, r, :], lhsT=pb[0:127, r * 128:(r + 1) * 128],
                                                                                     rhs=VO[:, g, :], start=True, stop=True),
                                       reads=[pbk, "VO"], writes=["bank0"])
                                op("vector", lambda e, u=u: e.tensor_scalar(out=dn[u][:, 0, :], in0=oc[:, :, 64], scalar1=1e-30, scalar2=None,
                                                                            op0=ALU.max), reads=["bank0"], writes=[f"dn{u}"])
                                op("vector", lambda e, u=u: e.reciprocal(out=rdn[u][:, 0, :], in_=dn[u][:, 0, :]),
                                   reads=[f"dn{u}"], writes=[f"rdn{u}"])
                                for r in range(4):
                                    in1 = bonus[:, i, :] if r == 0 else imp[u][:]
                                    op("vector", lambda e, u=u, r=r, in1=in1: e.scalar_tensor_tensor(
                                        out=imp[u][:], in0=oc[:, r, 65:97], scalar=rdn[u][:, 0, r:r + 1], in1=in1,
                                        op0=ALU.mult, op1=ALU.add),
                                       reads=["bank0", f"rdn{u}", f"imp{u}"], writes=[f"imp{u}"])
                                op("vector", lambda e, u=u: e.max(out=top8[u][:], in_=imp[u][:]), reads=[f"imp{u}"], writes=[f"top8{u}"])
                                op("vector", lambda e, u=u: e.tensor_scalar(out=sbt[u][:], in0=imp[u][:], scalar1=top8[u][:, 7:8],
                                                                            scalar2=NEG, op0=ALU.is_lt, op1=ALU.mult),
                                   reads=[f"imp{u}", f"top8{u}"], writes=[f"sbt{u}"])
                                ps, pk = ringB_next()
                                op("tensor", lambda e, ps=ps, u=u: e.transpose(out=ps[0:32, 0:128], in_=sbt[u][:], identity=ident_f[:]),
                                   reads=[f"sbt{u}"], writes=[pk])
                                op("vector", lambda e, ps=ps, g=g, tsl=tsl: e.tensor_copy(out=selb[:, g, tsl], in_=ps[0:32, 0:128]),
                                   reads=[pk], writes=["selb"])
                                for kt in range(i + 1):
                                    ksl = slice(kt * 128, (kt + 1) * 128)
                                    ps, pk = ringB_next()
                                    op("tensor", lambda e, ps=ps, g=g, ksl=ksl, q_ap=q_ap: e.matmul(
                                        ps[:], lhsT=ksT[:, g, ksl], rhs=q_ap, start=True, stop=False),
                                       reads=["ksT", "nqT"], writes=[pk])
                                    op("tensor", lambda e, ps=ps, g=g, ksl=ksl, tsl=tsl, kt=kt, i=i: e.matmul(
                                        ps[:], lhsT=Eexp[:, ksl], rhs=bc4(selb[:, g, tsl]), start=False, stop=(kt != i)),
                                       reads=["selb"], writes=[pk])
                                    if kt == i:
                                        op("tensor", lambda e, ps=ps: e.matmul(ps[:], lhsT=ident_bf[:], rhs=bc4(maskC[:]),
                                                                               start=False, stop=True),
                                           reads=[], writes=[pk])
                                    pb, pbk = pR.next()
                                    op("scalar", lambda e, ps=ps, pb=pb: e.activation(out=pb[:], in_=ps[:], func=AF.Exp),
                                       reads=[pk], writes=[pbk])
                                    op("tensor", lambda e, pb=pb, g=g, kt=kt, i=i: e.matmul(
                                        oTs[0:65, :], lhsT=vs_aug[:, kt, g, :], rhs=pb[:], start=(kt == 0), stop=(kt == i)),
                                       reads=[pbk, "vs_aug"], writes=["bank3"])
                                op("scalar", lambda e, u=u: e.copy(out=oT_sb[0][u][:], in_=oTs[0:65, :]),
                                   reads=["bank3"], writes=[f"oT_sb0{u}"])
                                for r in range(4):
                                    op("tensor", lambda e, u=u, r=r: e.transpose(out=os_[:, r, :], in_=oT_sb[0][u][:, r * 128:(r + 1) * 128],
                                                                                 identity=ident_f[0:65, 0:65]),
                                       reads=[f"oT_sb0{u}"], writes=["bank1"], c=300)
                                kts = [kt for kt in range(i - 4, i + 1) if kt >= 0]
                                for kt in kts:
                                    ksl = slice(kt * 128, (kt + 1) * 128)
                                    edge = (kt == i) or (kt == i - 4)
                                    ps, pk = ringB_next()
                                    op("tensor", lambda e, ps=ps, g=g, ksl=ksl, q_ap=q_ap, edge=edge: e.matmul(
                                        ps[:], lhsT=kwT[:, g, ksl], rhs=q_ap, start=True, stop=not edge),
                                       reads=["kwT", "nqT"], writes=[pk])
                                    if edge:
                                        mk_ = maskC if kt == i else maskW
                                        op("tensor", lambda e, ps=ps, mk_=mk_: e.matmul(ps[:], lhsT=ident_bf[:], rhs=bc4(mk_[:]),
                                                                                        start=False, stop=True),
                                           reads=[], writes=[pk])
                                    pb, pbk = pR.next()
                                    op("scalar", lambda e, ps=ps, pb=pb: e.activation(out=pb[:], in_=ps[:], func=AF.Exp),
                                       reads=[pk], writes=[pbk])
                                    op("tensor", lambda e, pb=pb, g=g, kt=kt, kts=kts: e.matmul(
                                        oTw[0:65, :], lhsT=vw_aug[:, kt, g, :], rhs=pb[:], start=(kt == kts[0]), stop=(kt == kts[-1])),
                                       reads=[pbk, "vw_aug"], writes=["bank4"])
                                op("scalar", lambda e, u=u: e.copy(out=oT_sb[1][u][:], in_=oTw[0:65, :]),
                                   reads=["bank4"], writes=[f"oT_sb1{u}"])
                                for r in range(4):
                                    op("tensor", lambda e, u=u, r=r: e.transpose(out=ow[:, r, :], in_=oT_sb[1][u][:, r * 128:(r + 1) * 128],
                                                                                 identity=ident_f[0:65, 0:65]),
                                       reads=[f"oT_sb1{u}"], writes=["bank2"], c=300)
                                op("vector", lambda e, u=u: e.tensor_copy(out=dn[u][:, 1, :], in_=os_[:, :, 64]),
                                   reads=["bank1"], writes=[f"dn{u}"])
                                op("vector", lambda e, u=u: e.tensor_copy(out=dn[u][:, 2, :], in_=ow[:, :, 64]),
                                   reads=["bank2"], writes=[f"dn{u}"])
                                op("vector", lambda e, u=u: e.reciprocal(out=rdn[u][:, 1:3, :], in_=dn[u][:, 1:3, :]),
                                   reads=[f"dn{u}"], writes=[f"rdn{u}"])
                                gview = gates[:, i, :].rearrange("p (b h) -> p b h", b=3)[:, :, 4 * g:4 * g + 4]
                                op("vector", lambda e, u=u, gview=gview: e.tensor_tensor(out=coef[u][:], in0=rdn[u][:], in1=gview, op=ALU.mult),
                                   reads=[f"rdn{u}", "gates"], writes=[f"coef{u}"])
                                for r in range(4):
                                    op("vector", lambda e, u=u, r=r: e.tensor_scalar(out=obs[u][:, r, :], in0=oc[:, r, 0:64],
                                                                                     scalar1=coef[u][:, 0, r:r + 1], scalar2=None, op0=ALU.mult),
                                       reads=["bank0", f"coef{u}"], writes=[f"obs{u}"])
                                    op("vector", lambda e, u=u, r=r: e.scalar_tensor_tensor(
                                        out=obs[u][:, r, :], in0=os_[:, r, 0:64], scalar=coef[u][:, 1, r:r + 1], in1=obs[u][:, r, :],
                                        op0=ALU.mult, op1=ALU.add),
                                       reads=["bank1", f"coef{u}", f"obs{u}"], writes=[f"obs{u}"])
                                    op("vector", lambda e, u=u, r=r: e.scalar_tensor_tensor(
                                        out=obs[u][:, r, :], in0=ow[:, r, 0:64], scalar=coef[u][:, 2, r:r + 1], in1=obs[u][:, r, :],
                                        op0=ALU.mult, op1=ALU.add),
                                       reads=["bank2", f"coef{u}", f"obs{u}"], writes=[f"obs{u}"])
                                ps, pk = ringB_next()
                                tp = ps[0:64, :].rearrange("p (r t) -> p r t", r=4)
                                for r in range(4):
                                    op("tensor", lambda e, tp=tp, u=u, r=r: e.transpose(out=tp[:, r, :], in_=obs[u][:, r, :], identity=ident_f[:]),
                                       reads=[f"obs{u}"], writes=[pk])
                                op("vector", lambda e, tp=tp, u=u: e.tensor_tensor(out=yb_sb[u][:], in0=tp, in1=nz_t[u][:], op=ALU.mult),
                                   reads=[pk, f"nz_t{u}"], writes=[f"yb_sb{u}"])
                                op("sync", lambda e, u=u, g=g, tsl=tsl: e.dma_start(out=yTb_d[256 * g:256 * (g + 1), tsl].rearrange("(r d) t -> d r t", d=64), in_=yb_sb[u][:]),
                                   reads=[f"yb_sb{u}"], writes=["yTb_d"], dma=True)
                        sc.flush()

                with contextlib.ExitStack() as sC:
                    wo = sb("wo", [128, 8, D], BF16, sC)
                    mT = sb("mT", [128, 8, S], BF16, sC)
                    ya_t = sb("ya_t", [128, 4, S], BF16, sC)
                    yb_t = sb("yb_t", [128, 4, S], BF16, sC)
                    wcu = [[sb(f"wcu{p}{i}", [128, 4, 128], BF16, sC) for i in range(2)] for p in range(2)]
                    wcg = [[sb(f"wcg{p}{i}", [128, 8, 128], BF16, sC) for i in range(2)] for p in range(2)]
                    sg = [sb(f"sg{i}", [128, 512], F32, sC) for i in range(2)]
                    m1 = [sb(f"m1{i}", [128, 512], F32, sC) for i in range(2)]
                    m2 = [sb(f"m2{i}", [128, 512], F32, sC) for i in range(2)]
                    xt = [sb(f"cxt{i}", [128, D], F32, sC) for i in range(2)]
                    ringC = Ring(banks, "bank")
                    yav = yTa_d.rearrange("(hp p) s -> p hp s", p=128)
                    ybv = yTb_d.rearrange("(hp p) s -> p hp s", p=128)
                    for tb in range(4):
                        bsl = slice(tb * 512, (tb + 1) * 512)
                        op("sync", lambda e, bsl=bsl: e.dma_start(out=ya_t[:, :, bsl], in_=yav[:, :, bsl]),
                           reads=["yTa_d"], writes=[f"ya_t{tb}"], dma=True)
                        op("sync", lambda e, bsl=bsl: e.dma_start(out=yb_t[:, :, bsl], in_=ybv[:, :, bsl]),
                           reads=["yTb_d"], writes=[f"yb_t{tb}"], dma=True)
                    sgc = 0
                    for dmc in range(8):
                        cs = slice(dmc * 128, (dmc + 1) * 128)
                        p = dmc % 2
                        load_w(wcu[p][0][:], w_up_a[l, :, cs].rearrange("(hp p) m -> p hp m", p=128), f"wcu{p}0")
                        load_w(wcg[p][0][:], win_cols(l, 3864 + dmc * 128, 128), f"wcg{p}0")
                        load_w(wcu[p][1][:], w_up_b[l, :, cs].rearrange("(hp p) m -> p hp m", p=128), f"wcu{p}1")
                        load_w(wcg[p][1][:], win_cols(l, 4888 + dmc * 128, 128), f"wcg{p}1")
                        if dmc >= 1:
                            c = dmc - 1
                            load_w(wo[:, :, c * 128:(c + 1) * 128], w_out[l, :, c * 128:(c + 1) * 128].rearrange("(k p) m -> p k m", p=128), "wo")
                        if dmc == 7:
                            load_w(wo[:, :, 7 * 128:8 * 128], w_out[l, :, 7 * 128:8 * 128].rearrange("(k p) m -> p k m", p=128), "wo")
                        for tb in range(4):
                            bsl = slice(tb * 512, (tb + 1) * 512)
                            j = sgc % 2
                            sgc += 1
                            for br, (yt, ytk, mm) in enumerate(((ya_t, f"ya_t{tb}", m1), (yb_t, f"yb_t{tb}", m2))):
                                pu, puk = ringC.next()
                                mm_chain(pu[:], [(wcu[p][br][:, hp, :], yt[:, hp, bsl]) for hp in range(4)], puk, [f"wcu{p}{br}", ytk])
                                pg, pgk = ringC.next()
                                mm_chain(pg[:], [(wcg[p][br][:, k, :], hT[:, k, bsl]) for k in range(8)], pgk, [f"wcg{p}{br}", "hT"])
                                op("scalar", lambda e, pg=pg, j=j: e.activation(out=sg[j][:], in_=pg[:], func=AF.Sigmoid),
                                   reads=[pgk], writes=[f"sg{j}"])
                                op("vector", lambda e, pu=pu, j=j, mm=mm: e.tensor_tensor(out=mm[j][:], in0=pu[:], in1=sg[j][:], op=ALU.mult),
                                   reads=[puk, f"sg{j}"], writes=[f"m{br + 1}{j}"])
                            op("gpsimd", lambda e, j=j, dmc=dmc, bsl=bsl: e.tensor_tensor(out=mT[:, dmc, bsl], in0=m1[j][:], in1=m2[j][:], op=ALU.add),
                               reads=[f"m1{j}", f"m2{j}"], writes=[f"mT{tb}"])
                    for tt in range(16):
                        tb = tt // 4
                        v = tt % 2
                        op("sync", lambda e, v=v, tt=tt: e.dma_start(out=xt[v][:], in_=x_src[b, tt * 128:(tt + 1) * 128, :]),
                           reads=[], writes=[f"cxt{v}"], dma=True)
                        for half in range(2):
                            hs = slice(half * 512, (half + 1) * 512)
                            po, pok = ringC.next()
                            mm_chain(po[:], [(mT[:, k, tt * 128:(tt + 1) * 128], wo[:, k, hs]) for k in range(8)],
                                     pok, [f"mT{tb}", "wo"])
                            op("vector", lambda e, po=po, v=v, hs=hs: e.tensor_tensor(out=xt[v][:, hs], in0=po[:], in1=xt[v][:, hs], op=ALU.add),
                               reads=[pok, f"cxt{v}"], writes=[f"cxt{v}"])
                        op("sync", lambda e, v=v, tt=tt: e.dma_start(out=out[b, tt * 128:(tt + 1) * 128, :], in_=xt[v][:]),
                           reads=[f"cxt{v}"], writes=[], dma=True)
                    sc.flush()
        print("total instructions (incl. waits):", sc.ninstr)
    return nc


_INVF = (1.0 / (2.0 * math.pi) * (10000.0 ** (-(np.arange(128) % 32) / 32.0))).astype(np.float32).reshape(128, 1)
_PROG = {}


def _get_prog(NL):
    if NL not in _PROG:
        _PROG[NL] = build_program(NL)
    return _PROG[NL]


WNAMES = ["norm_g", "w_in", "q_norm_g", "k_norm_g", "cmp_pe", "cmp_w1", "cmp_b1", "cmp_w2", "w_up_a", "w_up_b", "w_out"]


def kernel(x, positions, norm_g, w_in, q_norm_g, k_norm_g, cmp_pe, cmp_w1, cmp_b1, cmp_w2,
           w_up_a, w_up_b, w_out, _layers_per_launch=DEPTH):
    ws = dict(norm_g=norm_g, w_in=w_in, q_norm_g=q_norm_g, k_norm_g=k_norm_g, cmp_pe=cmp_pe, cmp_w1=cmp_w1,
              cmp_b1=cmp_b1, cmp_w2=cmp_w2, w_up_a=w_up_a, w_up_b=w_up_b, w_out=w_out)
    ws = {k: np.ascontiguousarray(np.asarray(v, dtype=np.float32)) for k, v in ws.items()}
    xcur = np.ascontiguousarray(np.asarray(x, dtype=np.float32))
    pos = np.ascontiguousarray(np.asarray(positions, dtype=np.int32))
    NL = _layers_per_launch
    nc = _get_prog(NL)
    for l0 in range(0, DEPTH, NL):
        in_maps = []
        for c in range(NCORES):
            m = {"x": xcur[c * NB:(c + 1) * NB], "positions": pos[c * NB:(c + 1) * NB], "invf": _INVF}
            for k, v in ws.items():
                m[k] = v[l0:l0 + NL]
            in_maps.append(m)
        res = run_bass_kernel_spmd(nc, in_maps, core_ids=list(range(NCORES)))
        xcur = np.concatenate([np.asarray(r["out"]) for r in res.results], axis=0)
    return xcur.astype(np.float32)
```

```python
import contextlib
import math
import numpy as np
import concourse.bass as bass
import concourse.mybir as mybir
from concourse.bass_utils import run_bass_kernel_spmd

F32 = mybir.dt.float32
BF16 = mybir.dt.bfloat16
I32 = mybir.dt.int32
AF = mybir.ActivationFunctionType
ALU = mybir.AluOpType
AX = mybir.AxisListType

S = 2048
D = 1024
NIN = 5912
NCORES = 8
NB = 2
DEPTH = 4
NEG = -30000.0
EPS = 1e-6

ENGS = ["tensor", "vector", "scalar", "gpsimd", "sync"]
EPOCH = 30000
NDMASEM = 12


DEFCOST = {"tensor": 560.0, "scalar": 600.0, "vector": 650.0, "gpsimd": 1200.0, "sync": 60.0}
WINDOW = 64


class Sched:
    def __init__(self, nc, stack):
        self.nc = nc
        self.stack = stack
        self.cnt = {e: 0 for e in ENGS}
        self.esem = {}
        for e in ENGS:
            if e != "sync":
                self.esem[e] = stack.enter_context(nc.semaphore(f"es_{e}_0"))
        self.eepoch = {e: 0 for e in ENGS}
        self.dsem = {}
        self.dcnt = {}
        self.dnext = {}
        for q in ["sync", "gpsimd", "scalar"]:
            self.dsem[q] = [stack.enter_context(nc.semaphore(f"ds_{q}_{i}")) for i in range(NDMASEM)]
            self.dcnt[q] = [0] * NDMASEM
            self.dnext[q] = 0
        self.ninstr = 0
        self.reorder = True
        self._reset()

    def _reset(self):
        self.nodes = []
        self.last_w = {}
        self.last_r = {}

    def op(self, eng, fn, reads=(), writes=(), dma=False, c=None):
        preds = set()
        for b in reads:
            w = self.last_w.get(b)
            if w is not None:
                preds.add(w)
            if b.startswith("bank"):
                for r_ in self.last_r.get(b, ()):
                    if self.nodes[r_][0] != eng:
                        preds.add(r_)
        for b in writes:
            w = self.last_w.get(b)
            if w is not None:
                preds.add(w)
            preds.update(self.last_r.get(b, ()))
        nid = len(self.nodes)
        if c is None:
            c = 3000.0 if dma else DEFCOST[eng]
        self.nodes.append((eng, fn, dma, preds, float(c)))
        for b in reads:
            self.last_r.setdefault(b, []).append(nid)
        for b in writes:
            self.last_w[b] = nid
            self.last_r[b] = []
        return nid

    def _simulate(self):
        import heapq
        nodes = self.nodes
        n = len(nodes)
        order = {e: [] for e in ENGS}
        if not self.reorder:
            for nid, nd in enumerate(nodes):
                order[nd[0]].append(nid)
            return order
        succ = [[] for _ in range(n)]
        indeg = [0] * n
        for nid, nd in enumerate(nodes):
            for p in nd[3]:
                succ[p].append(nid)
            indeg[nid] = len(nd[3])
        ready = {e: [] for e in ENGS}
        for nid, nd in enumerate(nodes):
            if indeg[nid] == 0:
                heapq.heappush(ready[nd[0]], nid)
        free_at = {e: 0.0 for e in ENGS}
        events = []
        now = 0.0
        left = n
        while left:
            progressed = False
            for e in ENGS:
                if free_at[e] <= now and ready[e]:
                    nid = heapq.heappop(ready[e])
                    nd = nodes[nid]
                    if nd[2]:
                        free_at[e] = now + 60.0
                        heapq.heappush(events, (free_at[e], -1))
                    else:
                        free_at[e] = now + nd[4]
                    heapq.heappush(events, (now + nd[4], nid))
                    order[e].append(nid)
                    left -= 1
                    progressed = True
            if not progressed:
                assert events, "scheduler deadlock"
                t, nid = heapq.heappop(events)
                now = max(now, t)
                while True:
                    if nid >= 0:
                        for sx in succ[nid]:
                            indeg[sx] -= 1
                            if indeg[sx] == 0:
                                heapq.heappush(ready[nodes[sx][0]], sx)
                    if events and events[0][0] <= now:
                        t, nid = heapq.heappop(events)
                    else:
                        break
        return order

    def flush(self):
        nc = self.nc
        nodes = self.nodes
        order = self._simulate()
        tok = [None] * len(nodes)
        extra = {}
        for e in ENGS:
            for nid in order[e]:
                if nodes[nid][2]:
                    i = self.dnext[e]
                    self.dnext[e] = (i + 1) % NDMASEM
                    sem = self.dsem[e][i]
                    if self.dcnt[e][i] > 0:
                        extra[nid] = (sem, self.dcnt[e][i])
                    self.dcnt[e][i] += 16
                    tok[nid] = (sem, self.dcnt[e][i], 16)
                else:
                    if self.cnt[e] >= EPOCH:
                        self.eepoch[e] += 1
                        self.esem[e] = self.stack.enter_context(nc.semaphore(f"es_{e}_{self.eepoch[e]}"))
                        self.cnt[e] = 0
                    self.cnt[e] += 1
                    tok[nid] = (self.esem[e], self.cnt[e], 1)
        prog = {}
        for e in ENGS:
            seen = {}
            lst = []
            for nid in order[e]:
                waits = {}
                cands = [tok[p][:2] for p in nodes[nid][3]]
                if nid in extra:
                    cands.append(extra[nid])
                for (sm, v) in cands:
                    k = id(sm)
                    if seen.get(k, 0) >= v:
                        continue
                    if k not in waits or waits[k][1] < v:
                        waits[k] = (sm, v)
                wl = list(waits.values())
                for (sm, v) in wl:
                    seen[id(sm)] = v
                lst.append((wl, nodes[nid][1], tok[nid][0], tok[nid][2]))
                self.ninstr += 1 + len(wl)
            prog[e] = lst
        finals = {}
        for q in ["sync", "gpsimd", "scalar"]:
            finals[q] = [(self.dsem[q][i], self.dcnt[q][i]) for i in range(NDMASEM) if self.dcnt[q][i] > 0]
        with nc.Block() as block:
            def mk(ename):
                def body(eng):
                    for (wl, fn, sem, inc) in prog[ename]:
                        for (sm, v) in wl:
                            eng.wait_ge(sm, v)
                        fn(eng).then_inc(sem, inc)
                    for (sm, v) in finals.get(ename, []):
                        eng.wait_ge(sm, v)
                return body
            block.tensor(mk("tensor"))
            block.vector(mk("vector"))
            block.scalar(mk("scalar"))
            block.gpsimd(mk("gpsimd"))
            block.sync(mk("sync"))
        self._reset()


class Ring:
    def __init__(self, bufs, name):
        self.bufs = bufs
        self.name = name
        self.i = 0

    def next(self):
        j = self.i % len(self.bufs)
        self.i += 1
        return self.bufs[j], f"{self.name}{j}"


def bc4(ap2d, n=4):
    p, f = ap2d.shape
    return ap2d.unsqueeze(1).broadcast_to([p, n, f])


STOPAT = 99
B0PART = 99


def build_program(NL):
    nc = bass.Bass("TRN2", target_bir_lowering=False)
    dt_in = lambda name, shape, dt=F32: nc.dram_tensor(name, shape, dt, kind="ExternalInput").ap()
    x_in = dt_in("x", [NB, S, D])
    pos_in = dt_in("positions", [NB, S], I32)
    norm_g = dt_in("norm_g", [NL, D])
    w_in = dt_in("w_in", [NL, D, NIN])
    q_norm_g = dt_in("q_norm_g", [NL, 64])
    k_norm_g = dt_in("k_norm_g", [NL, 3, 64])
    cmp_pe = dt_in("cmp_pe", [NL, 2, 32, 64])
    cmp_w1 = dt_in("cmp_w1", [NL, 2, 2048, 256])
    cmp_b1 = dt_in("cmp_b1", [NL, 2, 256])
    cmp_w2 = dt_in("cmp_w2", [NL, 2, 256, 64])
    w_up_a = dt_in("w_up_a", [NL, 512, D])
    w_up_b = dt_in("w_up_b", [NL, 512, D])
    w_out = dt_in("w_out", [NL, D, D])
    invf_in = dt_in("invf", [128, 1])
    out = nc.dram_tensor("out", [NB, S, D], F32, kind="ExternalOutput").ap()
    yTa_d = nc.dram_tensor("yTa_scr", [512, S], BF16).ap()
    yTb_d = nc.dram_tensor("yTb_scr", [512, S], BF16).ap()
    nzT_d = nc.dram_tensor("nzT_scr", [512, S], BF16).ap()

    with contextlib.ExitStack() as st:
        sc = Sched(nc, st)
        E = st.enter_context
        op = sc.op

        uid = [0]

        def sb(name, shape, dt, stack=None):
            uid[0] += 1
            return (stack if stack is not None else st).enter_context(nc.sbuf_tensor(f"{name}_{uid[0]}", shape, dt))

        ident_bf = sb("ident_bf", [128, 128], BF16)
        ident_f = sb("ident_f", [128, 128], F32)
        negtri = sb("negtri", [128, 128], BF16)
        negones = sb("negones", [128, 128], BF16)
        ones_blk = sb("ones_blk", [128, 128], BF16)
        rotM_blk = sb("rotM_blk", [128, 128], BF16)
        maskS = sb("maskS", [128, 128], BF16)
        maskC = sb("maskC", [128, 128], BF16)
        maskW = sb("maskW", [128, 128], BF16)
        Eexp = sb("Eexp", [32, S], BF16)
        cmask = sb("cmask", [127, S], BF16)
        bonus = sb("bonus", [128, 16, 32], F32)
        bonusF = sb("bonusF", [128, 16, 32], F32)
        invf = sb("invf_sb", [128, 1], F32)
        hT = sb("hT", [128, 8, S], BF16)
        cosT = sb("cosT", [128, S], F32)
        sinT = sb("sinT", [128, S], F32)
        cosC = sb("cosC", [64, 127], F32)
        sinC = sb("sinC", [64, 127], F32)
        VO = sb("VO", [127, 2, 97], BF16)
        vs_aug = sb("vs_aug", [128, 16, 2, 65], BF16)
        vw_aug = sb("vw_aug", [128, 16, 2, 65], BF16)
        kcT = sb("kcT", [64, 2, 127], BF16)
        ksTa = sb("ksTa", [96, 2, S], BF16)
        gq = sb("gq", [128, 1], F32)
        gk = sb("gk", [128, 3], F32)
        wst = Ring([sb(f"wst{i}", [128, 8, 128], F32) for i in range(2)], "wst")
        banks = [E(nc.psum_tensor(f"bank{i}", [128, 512], F32)) for i in range(8)]


        def sel(t, ap, pattern, cop, fill, base, cm, key):
            op("gpsimd", lambda e: e.affine_select(out=ap, in_=ap, pattern=pattern, compare_op=cop,
                                                   fill=fill, base=base, channel_multiplier=cm),
               reads=[key], writes=[key])

        def mset(ap, val, key):
            op("gpsimd", lambda e: e.memset(ap, val), writes=[key])

        mset(ident_bf[:], 0.0, "ident_bf")
        sel(ident_bf, ident_bf[:], [[-1, 128]], ALU.not_equal, 1.0, 0, 1, "ident_bf")
        mset(ident_f[:], 0.0, "ident_f")
        sel(ident_f, ident_f[:], [[-1, 128]], ALU.not_equal, 1.0, 0, 1, "ident_f")
        mset(negtri[:], -1.0, "negtri")
        sel(negtri, negtri[:], [[-1, 128]], ALU.is_ge, 0.0, 0, 1, "negtri")
        mset(negones[:], -1.0, "negones")
        mset(ones_blk[:], 1.0, "ones_blk")
        mset(ones_blk[0:64, 64:128], 0.0, "ones_blk")
        mset(ones_blk[64:128, 0:64], 0.0, "ones_blk")
        mset(rotM_blk[:], 0.0, "rotM_blk")
        for q0 in (0, 64):
            sel(rotM_blk, rotM_blk[q0:q0 + 64, q0:q0 + 64], [[-1, 64]], ALU.not_equal, -1.0, -32, 1, "rotM_blk")
            sel(rotM_blk, rotM_blk[q0:q0 + 64, q0:q0 + 64], [[-1, 64]], ALU.not_equal, 1.0, 32, 1, "rotM_blk")
        mset(maskS[:], 0.0, "maskS")
        sel(maskS, maskS[:], [[1, 128]], ALU.is_ge, NEG, -1, -1, "maskS")
        mset(maskC[:], 0.0, "maskC")
        sel(maskC, maskC[:], [[1, 128]], ALU.is_ge, NEG, 0, -1, "maskC")
        mset(maskW[:], 0.0, "maskW")
        sel(maskW, maskW[:], [[-1, 128]], ALU.is_ge, NEG, -1, 1, "maskW")
        mset(Eexp[:], 1.0, "Eexp")
        sel(Eexp, Eexp[:], [[1, S]], ALU.is_ge, 0.0, 0, -64, "Eexp")
        sel(Eexp, Eexp[:], [[-1, S]], ALU.is_ge, 0.0, 63, 64, "Eexp")
        for g_ in range(2):
            mset(ksTa[64:96, g_, :], 1.0, "ksTa")
            sel(ksTa, ksTa[64:96, g_, :], [[1, S]], ALU.is_ge, 0.0, 0, -64, "ksTa")
            sel(ksTa, ksTa[64:96, g_, :], [[-1, S]], ALU.is_ge, 0.0, 63, 64, "ksTa")
        mset(cmask[:], 0.0, "cmask")
        sel(cmask, cmask[:], [[1, S]], ALU.is_ge, NEG, -31, -16, "cmask")
        mset(VO[:], 1.0, "VO")
        sel(VO, VO[:, :, 65:97], [[0, 2], [64, 32]], ALU.is_ge, 0.0, 63, -16, "VO")
        sel(VO, VO[:, :, 65:97], [[0, 2], [-64, 32]], ALU.is_ge, 0.0, 31, 16, "VO")
        mset(vs_aug[:], 1.0, "vs_aug")
        mset(vw_aug[:], 1.0, "vw_aug")
        mset(bonus[:], 0.0, "bonus")
        sel(bonus, bonus[:], [[128, 16], [-64, 32]], ALU.is_ge, -1e30, 0, 1, "bonus")
        mset(bonusF[:], 1e4, "bonusF")
        sel(bonusF, bonusF[:], [[128, 16], [-64, 32]], ALU.is_ge, 0.0, 0, 1, "bonusF")
        sel(bonusF, bonusF[:], [[-128, 16], [64, 32]], ALU.is_ge, 0.0, 127, -1, "bonusF")
        op("gpsimd", lambda e: e.tensor_tensor(out=bonus[:], in0=bonus[:], in1=bonusF[:], op=ALU.add),
           reads=["bonus", "bonusF"], writes=["bonus"])
        mset(bonus[:, :, 0:1], 1e4, "bonus")
        op("sync", lambda e: e.dma_start(out=invf[:], in_=invf_in[:, :]), writes=["invf"], dma=True)
        sc.flush()

        def load_w(dst_ap, src_ap, key, np_=128):
            stg, skey = wst.next()
            a, b = src_ap.shape[1], src_ap.shape[2]
            sv = stg[0:np_, 0:a, 0:b]
            op("sync", lambda e: e.dma_start(out=sv, in_=src_ap), writes=[skey], dma=True)
            op("gpsimd", lambda e: e.tensor_copy(out=dst_ap, in_=sv), reads=[skey], writes=[key])

        def win_cols(l, c0, n):
            return w_in[l, :, c0:c0 + n].rearrange("(k p) m -> p k m", p=128)

        def mm_chain(out_ap, pairs, okey, rkeys):
            n = len(pairs)
            for j, (lh, rh) in enumerate(pairs):
                op("tensor", lambda e, lh=lh, rh=rh, j=j: e.matmul(out_ap, lhsT=lh, rhs=rh,
                                                                    start=(j == 0), stop=(j == n - 1)),
                   reads=rkeys, writes=[okey])

        for b in range(NB):
            with contextlib.ExitStack() as ss_:
                posi = sb("posi", [128, S], I32, ss_)
                posf = sb("posf", [128, S], F32, ss_)
                vv = sb("vv", [128, S], F32, ss_)
                uu = sb("uu", [128, S], F32, ss_)
                ui = sb("ui", [128, S], I32, ss_)
                uf = sb("uf", [128, S], F32, ss_)
                gg = sb("gg", [128, S], F32, ss_)
                mm_ = sb("mm_", [128, S], F32, ss_)
                posm = sb("posm", [128, 127], F32, ss_)
                vm = sb("vm", [128, 127], F32, ss_)
                op("sync", lambda e: e.dma_start(out=posi[:], in_=pos_in[b].partition_broadcast(128)),
                   writes=["posi"], dma=True)
                op("vector", lambda e: e.tensor_copy(out=posf[:], in_=posi[:]), reads=["posi"], writes=["posf"])
                op("vector", lambda e: e.tensor_scalar(out=vv[:], in0=posf[:], scalar1=invf[:, 0:1], scalar2=None,
                                                       op0=ALU.mult), reads=["posf", "invf"], writes=["vv"])
                win = bass.AP(posf[:].tensor, posf[:].offset, [[S, 128], [16, 127], [1, 32]])
                op("vector", lambda e: e.reduce_sum(out=posm[:], in_=win, axis=AX.X), reads=["posf"], writes=["posm"])
                op("vector", lambda e: e.tensor_scalar(out=vm[:], in0=posm[:], scalar1=invf[:, 0:1], scalar2=1.0 / 32,
                                                       op0=ALU.mult, op1=ALU.mult),
                   reads=["posm", "invf"], writes=["vm"])

                def table(vsrc, n, add, dst, dkey, skey, P=128):
                    u = uu[0:P, 0:n]
                    op("vector", lambda e: e.tensor_scalar(out=u, in0=vsrc, scalar1=float(add), scalar2=None,
                                                           op0=ALU.add), reads=[skey], writes=["uu"])
                    op("vector", lambda e: e.tensor_copy(out=ui[0:P, 0:n], in_=u), reads=["uu"], writes=["ui"])
                    op("vector", lambda e: e.tensor_copy(out=uf[0:P, 0:n], in_=ui[0:P, 0:n]), reads=["ui"], writes=["uf"])
                    op("vector", lambda e: e.tensor_tensor(out=gg[0:P, 0:n], in0=u, in1=uf[0:P, 0:n], op=ALU.subtract),
                       reads=["uu", "uf"], writes=["gg"])
                    op("vector", lambda e: e.scalar_tensor_tensor(out=mm_[0:P, 0:n], in0=gg[0:P, 0:n], scalar=0.5,
                                                                  in1=gg[0:P, 0:n], op0=ALU.is_gt, op1=ALU.subtract),
                       reads=["gg"], writes=["mm_"])
                    op("scalar", lambda e: e.activation(out=dst, in_=mm_[0:P, 0:n], func=AF.Sin,
                                                        scale=-2.0 * math.pi), reads=["mm_"], writes=[dkey])

                table(vv[:], S, 0.0, sinT[:], "sinT", "vv")
                table(vv[:], S, 0.25, cosT[:], "cosT", "vv")
                table(vm[0:64, :], 127, 0.0, sinC[:], "sinC", "vm", P=64)
                table(vm[0:64, :], 127, 0.25, cosC[:], "cosC", "vm", P=64)
                sc.flush()

            for l in range(NL):
                x_src = x_in if l == 0 else out
                with contextlib.ExitStack() as s1:
                    gqr = sb("gqr", [128, 1], F32, s1)
                    gbc = sb("gbc", [128, D], F32, s1)
                    xt = [sb(f"p1xt{i}", [128, D], F32, s1) for i in range(2)]
                    junk = sb("p1junk", [128, D], BF16, s1)
                    xs = [sb(f"p1xs{i}", [128, D], BF16, s1) for i in range(2)]
                    ssq = [sb(f"p1ss{i}", [128, 1], F32, s1) for i in range(2)]
                    rt_ = [sb(f"p1rt{i}", [128, 1], F32, s1) for i in range(2)]
                    rs_ = [sb(f"p1rs{i}", [128, 1], F32, s1) for i in range(2)]
                    for q0 in (0, 64):
                        op("sync", lambda e, q0=q0: e.dma_start(out=gqr[q0:q0 + 64, :], in_=q_norm_g[l].rearrange("(p o) -> p o", o=1)),
                           writes=["gqr"], dma=True)
                    op("vector", lambda e: e.tensor_scalar(out=gq[:], in0=gqr[:], scalar1=0.125, scalar2=None,
                                                           op0=ALU.mult), reads=["gqr"], writes=["gq"])
                    for j in range(3):
                        for q0 in (0, 64):
                            op("sync", lambda e, j=j, q0=q0: e.dma_start(out=gk[q0:q0 + 64, j:j + 1],
                                                                         in_=k_norm_g[l, j].rearrange("(p o) -> p o", o=1)),
                               writes=["gk"], dma=True)
                    op("sync", lambda e: e.dma_start(out=gbc[:], in_=norm_g[l].partition_broadcast(128)),
                       writes=["gbc"], dma=True)
                    for tt in range(16):
                        i = tt % 2
                        pT = banks[i][:].bitcast(BF16)
                        op("sync", lambda e, tt=tt, i=i: e.dma_start(out=xt[i][:], in_=x_src[b, tt * 128:(tt + 1) * 128, :]),
                           writes=[f"xt{i}"], dma=True)
                        op("scalar", lambda e, i=i: e.activation(out=junk[:], in_=xt[i][:], func=AF.Square,
                                                                 accum_out=ssq[i][:]),
                           reads=[f"xt{i}"], writes=["junk", f"ss{i}"])
                        op("scalar", lambda e, i=i: e.activation(out=rt_[i][:], in_=ssq[i][:], func=AF.Sqrt,
                                                                 scale=1.0 / D, bias=EPS),
                           reads=[f"ss{i}"], writes=[f"rt{i}"])
                        op("vector", lambda e, i=i: e.reciprocal(out=rs_[i][:], in_=rt_[i][:]),
                           reads=[f"rt{i}"], writes=[f"rs{i}"])
                        op("vector", lambda e, i=i: e.scalar_tensor_tensor(out=xs[i][:], in0=xt[i][:], scalar=rs_[i][:],
                                                                           in1=gbc[:], op0=ALU.mult, op1=ALU.mult),
                           reads=[f"xt{i}", f"rs{i}", "gbc"], writes=[f"xs{i}"])
                        for k in range(8):
                            op("tensor", lambda e, i=i, k=k, pT=pT: e.transpose(out=pT[:, k * 128:(k + 1) * 128],
                                                                                in_=xs[i][:, k * 128:(k + 1) * 128],
                                                                                identity=ident_bf[:]),
                               reads=[f"xs{i}"], writes=[f"bank{i}"])
                        op("scalar", lambda e, i=i, tt=tt, pT=pT: e.copy(out=hT[:, :, tt * 128:(tt + 1) * 128],
                                                                         in_=pT.rearrange("p (k t) -> p k t", k=8)),
                           reads=[f"bank{i}"], writes=["hT"])
                    sc.flush()

                if STOPAT <= 1:
                    continue
                with contextlib.ExitStack() as sa:
                    XP = [[sb(f"XP{p}{i}", [128, S], BF16, sa) for i in range(3)] for p in range(2)]
                    XO = [[sb(f"XO{p}{i}", [64, S], BF16, sa) for i in range(3)] for p in range(2)]
                    v_all = sb("v_all", [128, 16, 512], BF16, sa)
                    wv_all = sb("wv_all", [128, 8, 512], BF16, sa)
                    wA2 = [[sb(f"wA{p}{i}", [128, 8, 128], BF16, sa) for i in range(3)] for p in range(2)]
                    e_sb = [sb(f"e_sb{i}", [128, 512], F32, sa) for i in range(2)]
                    sp_bf = [[sb(f"sp_bf{s}{i}", [128, 512], BF16, sa) for i in range(2)] for s in range(2)]
                    R32 = [sb(f"R32{s}", [128, 512], F32, sa) for s in range(2)]
                    Rbf = [[sb(f"Rbf{s}{i}", [128, 512], BF16, sa) for i in range(2)] for s in range(2)]
                    w_bf = [[sb(f"w_bf{s}{i}", [128, 512], BF16, sa) for i in range(2)] for s in range(2)]
                    ya_sb = [sb(f"ya_sb{i}", [64, 512], BF16, sa) for i in range(2)]
                    ringA = Ring(banks[6:8], "bank6")

                    def ringA_next():
                        j = ringA.i % 2
                        ringA.i += 1
                        return banks[6 + j], f"bank{6 + j}"

                    for c in range(4):
                        load_w(wv_all[:, :, c * 128:(c + 1) * 128], win_cols(l, 1024 + c * 128, 128), f"wv_all{c}")
                    for tt in range(16):
                        ps, pk = ringA_next()
                        mm_chain(ps[:], [(hT[:, k, tt * 128:(tt + 1) * 128], wv_all[:, k, :]) for k in range(8)],
                                 pk, ["wv_all0", "wv_all1", "wv_all2", "wv_all3", "hT"])
                        if tt % 2 == 0:
                            op("vector", lambda e, ps=ps, tt=tt: e.tensor_copy(out=v_all[:, tt, :], in_=ps[:]),
                               reads=[pk], writes=["v_all"])
                        else:
                            op("scalar", lambda e, ps=ps, tt=tt: e.copy(out=v_all[:, tt, :], in_=ps[:]),
                               reads=[pk], writes=["v_all"])

                    for hp in range(4):
                        pp = hp % 2
                        wA = wA2[pp]
                        PK = f"p{pp}"
                        qT = [XP[pp][0], XO[pp][0]]
                        kT = [XP[pp][1], XO[pp][1]]
                        szT = [XP[pp][2], XO[pp][2]]
                        for wi, sec in enumerate((0, 1, 3)):
                            load_w(wA[wi][:], win_cols(l, sec * 512 + hp * 128, 128), f"wA{wi}" + PK)
                        for wi in range(3):
                            for tb in range(4):
                                ps, pk = ringA_next()
                                csl = slice(tb * 512, (tb + 1) * 512)
                                mm_chain(ps[:], [(wA[wi][:, k, :], hT[:, k, csl]) for k in range(8)], pk, [f"wA{wi}" + PK, "hT"])
                                d_ap = XP[pp][wi][:, csl]
                                ek = f"XP{wi}t{tb}" + PK
                                if wi == 0:
                                    op("scalar", lambda e, ps=ps, d_ap=d_ap: e.activation(out=d_ap, in_=ps[:], func=AF.Copy, scale=0.125),
                                       reads=[pk], writes=[ek])
                                elif wi == 1:
                                    op("vector", lambda e, ps=ps, d_ap=d_ap: e.tensor_copy(out=d_ap, in_=ps[:]),
                                       reads=[pk], writes=[ek])
                                else:
                                    op("scalar", lambda e, ps=ps, d_ap=d_ap: e.activation(out=d_ap, in_=ps[:], func=AF.Silu),
                                       reads=[pk], writes=[ek])
                                op("sync", lambda e, pp=pp, wi=wi, csl=csl: e.dma_start(out=XO[pp][wi][:, csl], in_=XP[pp][wi][64:128, csl]),
                                   reads=[ek], writes=[f"XO{wi}t{tb}" + PK], dma=True, c=2500)

                        def xkeys(wi, s, qb=None):
                            nm = "XP" if s == 0 else "XO"
                            if qb is None:
                                return [f"{nm}{wi}t{t}" + PK for t in range(4)]
                            return [f"{nm}{wi}t{qb}" + PK]

                        tiles = []
                        for qb in range(4):
                            nt = 4 * qb + 4
                            for kt in range(nt - 1, -1, -1):
                                tiles.append((qb, kt, kt == nt - 1, kt == 0))

                        def c0_of(qb, kt):
                            return max(kt - 4 * qb, 0) * 128

                        def emit_qk(n, s, kT=kT, qT=qT, xkeys=xkeys):
                            qb, kt, first, last = tiles[n]
                            zb = banks[2 * s + (n % 2)]
                            zk = f"bank{2 * s + (n % 2)}"
                            diag = kt >= 4 * qb
                            c0 = c0_of(qb, kt)
                            op("tensor", lambda e: e.matmul(zb[:, c0:512], lhsT=kT[s][0:64, kt * 128:(kt + 1) * 128],
                                                            rhs=qT[s][0:64, qb * 512 + c0:(qb + 1) * 512], start=True, stop=True),
                               reads=xkeys(1, s) + xkeys(0, s, qb), writes=[zk], c=100 + 0.75 * (512 - c0))
                            if diag:
                                op("tensor", lambda e: e.matmul(zb[:, c0:c0 + 128], lhsT=ident_bf[:], rhs=maskS[:],
                                                                start=False, stop=True, skip_group_check=True),
                                   reads=[], writes=[zk], c=200)

                        for s in range(2):
                            emit_qk(0, s)
                        for n in range(len(tiles)):
                            qb, kt, first, last = tiles[n]
                            par = n % 2
                            c0 = c0_of(qb, kt)
                            diag = kt >= 4 * qb
                            c1 = (kt - 4 * qb + 1) * 128 if diag else 0
                            w = 512 - c0
                            if first:
                                for s in range(2):
                                    op("gpsimd", lambda e, s=s: e.memset(R32[s][:], 0.0), writes=[f"R32{s}"], c=600)
                            for s in range(2):
                                zb = banks[2 * s + par]
                                zk = f"bank{2 * s + par}"
                                op("scalar", lambda e, zb=zb, s=s, c0=c0: e.activation(out=e_sb[s][:, c0:512], in_=zb[:, c0:512], func=AF.Exp),
                                   reads=[zk], writes=[f"e_sb{s}"], c=220 + 0.72 * w)
                                op("scalar", lambda e, s=s, par=par, c0=c0: e.activation(out=sp_bf[s][par][:, c0:512], in_=e_sb[s][:, c0:512],
                                                                                         func=AF.Ln, bias=1.0),
                                   reads=[f"e_sb{s}"], writes=[f"sp_bf{s}{par}"], c=250 + 0.75 * w)
                            for s in range(2):
                                zb = banks[2 * s + par]
                                zk = f"bank{2 * s + par}"
                                op("tensor", lambda e, zb=zb, s=s, par=par, first=first, c0=c0: e.matmul(
                                    zb[:, c0:512], lhsT=negtri[:], rhs=sp_bf[s][par][:, c0:512], start=False, stop=first, skip_group_check=True),
                                   reads=[f"sp_bf{s}{par}"], writes=[zk], c=100 + 0.75 * w)
                                if not first:
                                    op("tensor", lambda e, zb=zb, s=s, par=par, c1=c1: e.matmul(
                                        zb[:, c1:512], lhsT=negones[:], rhs=Rbf[s][par][:, c1:512], start=False, stop=True, skip_group_check=True),
                                       reads=[f"Rbf{s}{par}"], writes=[zk], c=100 + 0.75 * (512 - c1))
                                if not last:
                                    op("gpsimd", lambda e, s=s, par=par, c0=c0: e.tensor_tensor(out=R32[s][:, c0:512], in0=R32[s][:, c0:512],
                                                                                                in1=sp_bf[s][par][:, c0:512], op=ALU.add),
                                       reads=[f"sp_bf{s}{par}", f"R32{s}"], writes=[f"R32{s}"], c=200 + 2.0 * w)
                                    op("vector", lambda e, s=s, par=par, c0=c0: e.tensor_copy(out=Rbf[s][1 - par][:, c0:512], in_=R32[s][:, c0:512]),
                                       reads=[f"R32{s}"], writes=[f"Rbf{s}{1 - par}"], c=100 + 1.1 * w)
                            if n + 1 < len(tiles):
                                for s in range(2):
                                    emit_qk(n + 1, s)
                            for s in range(2):
                                zb = banks[2 * s + par]
                                zk = f"bank{2 * s + par}"
                                op("scalar", lambda e, zb=zb, s=s, par=par, c0=c0: e.activation(out=w_bf[s][par][:, c0:512], in_=zb[:, c0:512], func=AF.Exp),
                                   reads=[zk], writes=[f"w_bf{s}{par}"], c=220 + 0.72 * w)
                            for s in range(2):
                                ob = banks[4 + s]
                                h = 2 * hp + s
                                op("tensor", lambda e, ob=ob, s=s, par=par, kt=kt, first=first, last=last, h=h, c0=c0: e.matmul(
                                    ob[0:64, c0:512], lhsT=v_all[:, kt, h * 64:(h + 1) * 64], rhs=w_bf[s][par][:, c0:512],
                                    start=first, stop=last, skip_group_check=True),
                                   reads=[f"w_bf{s}{par}", "v_all"], writes=[f"bank{4 + s}"], c=100 + 0.75 * w)
                                if last:
                                    op("vector", lambda e, ob=ob, s=s, qb=qb, szT=szT: e.tensor_tensor(
                                        out=ya_sb[s][:], in0=ob[0:64, :], in1=szT[s][0:64, qb * 512:(qb + 1) * 512], op=ALU.mult),
                                       reads=[f"bank{4 + s}"] + xkeys(2, s, qb), writes=[f"ya_sb{s}"])
                                    op("sync", lambda e, s=s, h=h, qb=qb: e.dma_start(out=yTa_d[h * 64:(h + 1) * 64, qb * 512:(qb + 1) * 512], in_=ya_sb[s][:]),
                                       reads=[f"ya_sb{s}"], writes=["yTa_d"], dma=True)
                    sc.flush()

                if STOPAT <= 2:
                    continue
                with contextlib.ExitStack() as sB:
                    nqT = sb("nqT", [64, 8, S], BF16, sB)
                    ksT = ksTa[0:64]
                    kwT = sb("kwT", [64, 2, S], BF16, sB)
                    gates = sb("gates", [128, 16, 24], F32, sB)
                    kcr = sb("kcr", [64, 2, S], BF16, sB)
                    vcr = sb("vcr", [64, 2, S], BF16, sB)
                    ringB = Ring(banks[3:8], "bank")

                    def ringB_next():
                        j = 3 + (ringB.i % 5)
                        ringB.i += 1
                        return banks[j], f"bank{j}"

                    nr = {nm: [sb(f"nr_{nm}{i}", [128, 512], dt_, sB) for i in range(2)]
                          for nm, dt_ in (("sq", BF16), ("rt", F32), ("qn", BF16), ("t1", F32), ("t2", F32))}
                    nrc = [0]

                    def normrope(ps, pk, n, P, gcol, gkey, cos_ap, sin_ap, tkeys, out_ap, okey):
                        i = nrc[0] % 2
                        nrc[0] += 1
                        sq, rt, qn, t1, t2 = (nr[nm][i][0:P, 0:n] for nm in ("sq", "rt", "qn", "t1", "t2"))
                        rs = rt
                        kk = lambda nm: f"nr_{nm}{i}"
                        raw = ps[0:P, 0:n]
                        op("scalar", lambda e: e.activation(out=sq, in_=raw, func=AF.Square), reads=[pk], writes=[kk("sq")])
                        p2, p2k = ringB_next()
                        op("tensor", lambda e: e.matmul(p2[0:P, 0:n], lhsT=ones_blk[0:P, 0:P], rhs=sq, start=True, stop=True),
                           reads=[kk("sq")], writes=[p2k])
                        op("scalar", lambda e: e.activation(out=rt, in_=p2[0:P, 0:n], func=AF.Sqrt, scale=1.0 / 64, bias=EPS),
                           reads=[p2k], writes=[kk("rt")])
                        op("vector", lambda e: e.reciprocal(out=rs, in_=rt), reads=[kk("rt")], writes=[kk("rt")], c=3300)
                        op("vector", lambda e: e.scalar_tensor_tensor(out=qn, in0=raw, scalar=gcol, in1=rs,
                                                                      op0=ALU.mult, op1=ALU.mult),
                           reads=[pk, kk("rt"), gkey], writes=[kk("qn")])
                        p3, p3k = ringB_next()
                        op("tensor", lambda e: e.matmul(p3[0:P, 0:n], lhsT=rotM_blk[0:P, 0:P], rhs=qn, start=True, stop=True),
                           reads=[kk("qn")], writes=[p3k])
                        op("gpsimd", lambda e: e.tensor_tensor(out=t1, in0=qn, in1=cos_ap, op=ALU.mult),
                           reads=[kk("qn")] + tkeys, writes=[kk("t1")])
                        op("vector", lambda e: e.tensor_tensor(out=t2, in0=p3[0:P, 0:n], in1=sin_ap, op=ALU.mult),
                           reads=[p3k] + tkeys, writes=[kk("t2")])
                        op("gpsimd", lambda e: e.tensor_tensor(out=out_ap, in0=t1, in1=t2, op=ALU.add),
                           reads=[kk("t1"), kk("t2")], writes=[okey])

                    with contextlib.ExitStack() as sB0:
                        wB = [sb(f"wB{i}", [128, 8, 128], BF16, sB0) for i in range(3)]
                        wBr = Ring(wB, "wB")
                        wv4 = sb("wv4", [128, 8, 512], BF16, sB0)
                        tmpO = Ring([sb(f"tmpO{i}", [128, 512], BF16, sB0) for i in range(3)], "tmpO")

                        def proj128(tb, wt, wk):
                            ps, pk = ringB_next()
                            mm_chain(ps[:], [(wt[:, k, :], hT[:, k, tb * 512:(tb + 1) * 512]) for k in range(8)], pk, [wk, "hT"])
                            return ps, pk

                        def shift2(to, tk, dst0, dst1, kbase):
                            op("sync", lambda e: e.dma_start(out=dst0, in_=to[0:64, :]), reads=[tk], writes=[kbase + "a"], dma=True, c=2500)
                            op("sync", lambda e: e.dma_start(out=dst1, in_=to[64:128, :]), reads=[tk], writes=[kbase + "b"], dma=True, c=2500)

                        for pr in range(4):
                            wt, wk = wBr.next()
                            load_w(wt[:], win_cols(l, 2048 + pr * 128, 128), wk)
                            for tb in range(4):
                                csl = slice(tb * 512, (tb + 1) * 512)
                                ps, pk = proj128(tb, wt, wk)
                                to, tk = tmpO.next()
                                normrope(ps, pk, 512, 128, gq[:, 0:1], "gq", cosT[:, csl], sinT[:, csl], ["cosT", "sinT"], to[:], tk)
                                shift2(to, tk, nqT[:, 2 * pr, csl], nqT[:, 2 * pr + 1, csl], f"nqT{pr}_{tb}")
                        for c0, dst, dkey in ((2560, kcr, "kcr"), (2688, vcr, "vcr")) if B0PART >= 2 else ():
                            wt, wk = wBr.next()
                            load_w(wt[:], win_cols(l, c0, 128), wk)
                            for tb in range(4):
                                csl = slice(tb * 512, (tb + 1) * 512)
                                ps, pk = proj128(tb, wt, wk)
                                to, tk = tmpO.next()
                                op("vector", lambda e, ps=ps, to=to: e.tensor_copy(out=to[:], in_=ps[:]), reads=[pk], writes=[tk])
                                shift2(to, tk, dst[:, 0, csl], dst[:, 1, csl], f"{dkey}_{tb}")
                        for c0, dst, dkey, gi in ((2816, ksT, "ksT", 1), (3072, kwT, "kwT", 2)) if B0PART >= 3 else ():
                            wt, wk = wBr.next()
                            load_w(wt[:], win_cols(l, c0, 128), wk)
                            for tb in range(4):
                                csl = slice(tb * 512, (tb + 1) * 512)
                                ps, pk = proj128(tb, wt, wk)
                                to, tk = tmpO.next()
                                normrope(ps, pk, 512, 128, gk[:, gi:gi + 1], "gk", cosT[:, csl], sinT[:, csl], ["cosT", "sinT"], to[:], tk)
                                shift2(to, tk, dst[:, 0, csl], dst[:, 1, csl], f"{dkey}_{tb}")
                        for c in range(4 if B0PART >= 4 else 0):
                            n_ = 128 if c < 3 else 24
                            load_w(wv4[:, :, c * 128:c * 128 + n_], win_cols(l, 2944 + c * 128, n_), f"wv4_{c}")
                        for tt in range(16 if B0PART >= 4 else 0):
                            ps, pk = ringB_next()
                            mm_chain(ps[:, 0:408], [(hT[:, k, tt * 128:(tt + 1) * 128], wv4[:, k, 0:408]) for k in range(8)],
                                     pk, ["wv4_0", "wv4_1", "wv4_2", "wv4_3", "hT"])
                            op("vector", lambda e, ps=ps, tt=tt: e.tensor_copy(
                                out=vs_aug[:, tt, :, 0:64], in_=ps[:, 0:128].rearrange("p (g d) -> p g d", g=2)),
                               reads=[pk], writes=["vs_aug"], c=300)
                            op("vector", lambda e, ps=ps, tt=tt: e.tensor_copy(
                                out=vw_aug[:, tt, :, 0:64], in_=ps[:, 256:384].rearrange("p (g d) -> p g d", g=2)),
                               reads=[pk], writes=["vw_aug"], c=300)
                            op("scalar", lambda e, ps=ps, tt=tt: e.activation(out=gates[:, tt, :], in_=ps[:, 384:408], func=AF.Sigmoid),
                               reads=[pk], writes=["gates"], c=250)
                        for pr in range(4 if B0PART >= 5 else 0):
                            wt, wk = wBr.next()
                            load_w(wt[:], win_cols(l, 3352 + pr * 128, 128), wk)
                            for tb in range(4):
                                csl = slice(tb * 512, (tb + 1) * 512)
                                ps, pk = proj128(tb, wt, wk)
                                to, tk = tmpO.next()
                                op("scalar", lambda e, ps=ps, to=to: e.activation(out=to[:], in_=ps[:], func=AF.Silu),
                                   reads=[pk], writes=[tk])
                                op("sync", lambda e, to=to, pr=pr, csl=csl: e.dma_start(out=nzT_d[pr * 128:(pr + 1) * 128, csl], in_=to[:]),
                                   reads=[tk], writes=[f"nzT_d{pr}_{tb}"], dma=True)
                        sc.flush()

                    if STOPAT > 3:
                        with contextlib.ExitStack() as sB1:
                            w1_bf = sb("w1_bf", [64, 32, 256], BF16, sB1)
                            w1st = [sb(f"w1st{i}", [64, 4, 256], F32, sB1) for i in range(2)]
                            pe_sb = sb("pe_sb", [32, 64], F32, sB1)
                            peT = sb("peT", [64, 32], BF16, sB1)
                            b1_sb = sb("b1_sb", [128, 2], F32, sB1)
                            w2st = sb("w2st", [128, 2, 64], F32, sB1)
                            w2_bf = sb("w2_bf", [128, 2, 64], BF16, sB1)
                            cvec = sb("cvec", [128, 2], F32, sB1)
                            hid = sb("hid", [128, 2, 2, 127], BF16, sB1)
                            for kv in range(2):
                                raw = kcr if kv == 0 else vcr
                                rkey = "kcr" if kv == 0 else "vcr"
                                for c in range(8):
                                    i = c % 2
                                    src = cmp_w1[l, kv, c * 256:(c + 1) * 256, :].rearrange("(l d) h -> d l h", d=64)
                                    op("sync", lambda e, i=i, src=src: e.dma_start(out=w1st[i][:], in_=src),
                                       writes=[f"w1st{i}"], dma=True)
                                    op("gpsimd", lambda e, i=i, c=c: e.tensor_copy(out=w1_bf[:, c * 4:(c + 1) * 4, :], in_=w1st[i][:]),
                                       reads=[f"w1st{i}"], writes=["w1_bf"])
                                op("sync", lambda e, kv=kv: e.dma_start(out=pe_sb[:], in_=cmp_pe[l, kv]), writes=["pe_sb"], dma=True)
                                for hc in range(2):
                                    op("sync", lambda e, kv=kv, hc=hc: e.dma_start(
                                        out=b1_sb[:, hc:hc + 1],
                                        in_=cmp_b1[l, kv, hc * 128:(hc + 1) * 128].rearrange("(p o) -> p o", o=1)),
                                       writes=["b1_sb"], dma=True)
                                op("sync", lambda e, kv=kv: e.dma_start(
                                    out=w2st[:], in_=cmp_w2[l, kv].rearrange("(c p) d -> p c d", p=128)),
                                   writes=["w2st"], dma=True)
                                op("gpsimd", lambda e: e.tensor_copy(out=w2_bf[:], in_=w2st[:]), reads=["w2st"], writes=["w2_bf"])
                                ps, pk = ringB_next()
                                op("tensor", lambda e, ps=ps: e.transpose(out=ps[0:64, 0:32], in_=pe_sb[:], identity=ident_f[0:32, 0:32]),
                                   reads=["pe_sb"], writes=[pk])
                                op("vector", lambda e, ps=ps: e.tensor_copy(out=peT[:], in_=ps[0:64, 0:32]), reads=[pk], writes=["peT"])
                                for hc in range(2):
                                    ps, pk = ringB_next()
                                    mm_chain(ps[:, 0:1], [(w1_bf[:, li, hc * 128:(hc + 1) * 128], peT[:, li:li + 1]) for li in range(32)],
                                             pk, ["w1_bf", "peT"])
                                    op("vector", lambda e, ps=ps, hc=hc: e.tensor_tensor(out=cvec[:, hc:hc + 1], in0=ps[:, 0:1],
                                                                                         in1=b1_sb[:, hc:hc + 1], op=ALU.add),
                                       reads=[pk, "b1_sb"], writes=["cvec"])
                                for g in range(2):
                                    for hc in range(2):
                                        ps, pk = ringB_next()
                                        mm_chain(ps[:, 0:127], [(w1_bf[:, li, hc * 128:(hc + 1) * 128],
                                                                 raw[:, g, li:li + 16 * 126 + 1:16]) for li in range(32)],
                                                 pk, ["w1_bf", rkey])
                                        op("scalar", lambda e, ps=ps, hc=hc, g=g: e.activation(
                                            out=hid[:, hc, g, :], in_=ps[:, 0:127], func=AF.Silu, bias=cvec[:, hc:hc + 1]),
                                           reads=[pk, "cvec"], writes=["hid"])
                                    ps, pk = ringB_next()
                                    if kv == 0:
                                        mm_chain(ps[0:64, 0:127], [(w2_bf[:, hc, :], hid[:, hc, g, :]) for hc in range(2)],
                                                 pk, ["w2_bf", "hid"])
                                        normrope(ps, pk, 127, 64, gk[0:64, 0:1], "gk", cosC[:], sinC[:], ["cosC", "sinC"],
                                                 kcT[:, g, :], "kcT")
                                    else:
                                        mm_chain(ps[0:127, 0:64], [(hid[:, hc, g, :], w2_bf[:, hc, :]) for hc in range(2)],
                                                 pk, ["w2_bf", "hid"])
                                        op("vector", lambda e, ps=ps, g=g: e.tensor_copy(out=VO[:, g, 0:64], in_=ps[0:127, 0:64]),
                                           reads=[pk], writes=["VO"])
                            sc.flush()

                    with contextlib.ExitStack() as sB2:
                      if STOPAT > 4:
                        NP = 3
                        p_bf = [sb(f"p_bf{i}", [128, 512], BF16, sB2) for i in range(NP)]
                        pR = Ring(p_bf, "p_bf")
                        dn = [sb(f"dn{i}", [128, 3, 4], F32, sB2) for i in range(2)]
                        rdn = [sb(f"rdn{i}", [128, 3, 4], F32, sB2) for i in range(2)]
                        coef = [sb(f"coef{i}", [128, 3, 4], F32, sB2) for i in range(2)]
                        imp = [sb(f"imp{i}", [128, 32], F32, sB2) for i in range(2)]
                        top8 = [sb(f"top8{i}", [128, 8], F32, sB2) for i in range(2)]
                        sbt = [sb(f"sbt{i}", [128, 96], F32, sB2) for i in range(2)]
                        qa = [sb(f"qa{i}", [96, 4, 128], BF16, sB2) for i in range(2)]
                        for i_ in range(2):
                            op("gpsimd", lambda e, i_=i_: e.memset(sbt[i_][:], 0.0), writes=[f"sbt{i_}"])
                        obs = [sb(f"obs{i}", [128, 4, 64], F32, sB2) for i in range(2)]
                        nz_t = [sb(f"nz_t{i}", [64, 4, 128], BF16, sB2) for i in range(2)]
                        yb_sb = [sb(f"yb_sb{i}", [64, 4, 128], BF16, sB2) for i in range(2)]
                        ocb, osb, owb = banks[0], banks[1], banks[2]
                        oTs, oTw = banks[3], banks[4]
                        oT_sb = [[sb(f"oT_sb{a}{i}", [65, 512], F32, sB2) for i in range(2)] for a in range(2)]
                        rb2 = [0]

                        def ringB_next():
                            j = 5 + (rb2[0] % 3)
                            rb2[0] += 1
                            return banks[j], f"bank{j}"

                        oc = ocb[:, 0:388].rearrange("p (r c) -> p r c", r=4)
                        os_ = osb[:, 0:260].rearrange("p (r c) -> p r c", r=4)
                        ow = owb[:, 0:260].rearrange("p (r c) -> p r c", r=4)
                        it = 0
                        for g in range(2):
                            for i in range(16):
                                u = it % 2
                                it += 1
                                tsl = slice(i * 128, (i + 1) * 128)
                                q_ap = nqT[:, 4 * g:4 * g + 4, tsl]
                                op("sync", lambda e, u=u, g=g, tsl=tsl: e.dma_start(out=nz_t[u][:], in_=nzT_d[256 * g:256 * (g + 1), tsl].rearrange("(r d) t -> d r t", d=64)),
                                   reads=[], writes=[f"nz_t{u}"], dma=True)
                                ps, pk = ringB_next()
                                op("tensor", lambda e, ps=ps, g=g, q_ap=q_ap: e.matmul(ps[0:127, :], lhsT=kcT[:, g, :], rhs=q_ap,
                                                                                       start=True, stop=False),
                                   reads=["kcT", "nqT"], writes=[pk])
                                op("tensor", lambda e, ps=ps, tsl=tsl: e.matmul(ps[0:127, :], lhsT=ident_bf[0:127, 0:127],
                                                                                rhs=bc4(cmask[:, tsl]), start=False, stop=True),
                                   reads=[], writes=[pk])
                                pb, pbk = pR.next()
                                op("scalar", lambda e, ps=ps, pb=pb: e.activation(out=pb[0:127, :], in_=ps[0:127, :], func=AF.Exp),
                                   reads=[pk], writes=[pbk])
                                for r in range(4):
                                    op("tensor", lambda e, pb=pb, r=r, g=g: e.matmul(oc[:, r, :], lhsT=pb[0:127, r * 128:(r + 1) * 128],
                                                                                     rhs=VO[:, g, :], start=True, stop=True),
                                       reads=[pbk, "VO"], writes=["bank0"])
                                op("vector", lambda e, u=u: e.tensor_scalar(out=dn[u][:, 0, :], in0=oc[:, :, 64], scalar1=1e-30, scalar2=None,
                                                                            op0=ALU.max), reads=["bank0"], writes=[f"dn{u}"])
                                op("vector", lambda e, u=u: e.reciprocal(out=rdn[u][:, 0, :], in_=dn[u][:, 0, :]),
                                   reads=[f"dn{u}"], writes=[f"rdn{u}"])
                                for r in range(4):
                                    in1 = bonus[:, i, :] if r == 0 else imp[u][:]
                                    op("vector", lambda e, u=u, r=r, in1=in1: e.scalar_tensor_tensor(
                                        out=imp[u][:], in0=oc[:, r, 65:97], scalar=rdn[u][:, 0, r:r + 1], in1=in1,
                                        op0=ALU.mult, op1=ALU.add),
                                       reads=["bank0", f"rdn{u}", f"imp{u}"], writes=[f"imp{u}"])
                                op("vector", lambda e, u=u: e.max(out=top8[u][:], in_=imp[u][:]), reads=[f"imp{u}"], writes=[f"top8{u}"])
                                op("vector", lambda e, u=u: e.tensor_scalar(out=sbt[u][:, 64:96], in0=imp[u][:], scalar1=top8[u][:, 7:8],
                                                                            scalar2=NEG, op0=ALU.is_lt, op1=ALU.mult),
                                   reads=[f"imp{u}", f"top8{u}"], writes=[f"sbt{u}"])
                                ps, pk = ringB_next()
                                op("tensor", lambda e, ps=ps, u=u: e.transpose(out=ps[0:96, 0:128], in_=sbt[u][:], identity=ident_f[:]),
                                   reads=[f"sbt{u}"], writes=[pk])
                                op("vector", lambda e, ps=ps, u=u: e.tensor_copy(out=qa[u][64:96, :, :], in_=bc4(ps[64:96, 0:128])),
                                   reads=[pk], writes=[f"qa{u}"])
                                op("gpsimd", lambda e, u=u, q_ap=q_ap: e.tensor_copy(out=qa[u][0:64, :, :], in_=q_ap),
                                   reads=["nqT", f"qa{u}"], writes=[f"qa{u}"])
                                for kt in range(i + 1):
                                    ksl = slice(kt * 128, (kt + 1) * 128)
                                    ps, pk = ringB_next()
                                    op("tensor", lambda e, ps=ps, g=g, ksl=ksl, u=u, kt=kt, i=i: e.matmul(
                                        ps[:], lhsT=ksTa[0:96, g, ksl], rhs=qa[u][:, :, :], start=True, stop=(kt != i)),
                                       reads=["ksT", f"qa{u}"], writes=[pk])
                                    if kt == i:
                                        op("tensor", lambda e, ps=ps: e.matmul(ps[:], lhsT=ident_bf[:], rhs=bc4(maskC[:]),
                                                                               start=False, stop=True),
                                           reads=[], writes=[pk])
                                    pb, pbk = pR.next()
                                    op("scalar", lambda e, ps=ps, pb=pb: e.activation(out=pb[:], in_=ps[:], func=AF.Exp),
                                       reads=[pk], writes=[pbk])
                                    op("tensor", lambda e, pb=pb, g=g, kt=kt, i=i: e.matmul(
                                        oTs[0:65, :], lhsT=vs_aug[:, kt, g, :], rhs=pb[:], start=(kt == 0), stop=(kt == i)),
                                       reads=[pbk, "vs_aug"], writes=["bank3"])
                                op("scalar", lambda e, u=u: e.copy(out=oT_sb[0][u][:], in_=oTs[0:65, :]),
                                   reads=["bank3"], writes=[f"oT_sb0{u}"])
                                for r in range(4):
                                    op("tensor", lambda e, u=u, r=r: e.transpose(out=os_[:, r, :], in_=oT_sb[0][u][:, r * 128:(r + 1) * 128],
                                                                                 identity=ident_f[0:65, 0:65]),
                                       reads=[f"oT_sb0{u}"], writes=["bank1"], c=300)
                                kts = [kt for kt in range(i - 4, i + 1) if kt >= 0]
                                for kt in kts:
                                    ksl = slice(kt * 128, (kt + 1) * 128)
                                    edge = (kt == i) or (kt == i - 4)
                                    ps, pk = ringB_next()
                                    op("tensor", lambda e, ps=ps, g=g, ksl=ksl, q_ap=q_ap, edge=edge: e.matmul(
                                        ps[:], lhsT=kwT[:, g, ksl], rhs=q_ap, start=True, stop=not edge),
                                       reads=["kwT", "nqT"], writes=[pk])
                                    if edge:
                                        mk_ = maskC if kt == i else maskW
                                        op("tensor", lambda e, ps=ps, mk_=mk_: e.matmul(ps[:], lhsT=ident_bf[:], rhs=bc4(mk_[:]),
                                                                                        start=False, stop=True),
                                           reads=[], writes=[pk])
                                    pb, pbk = pR.next()
                                    op("scalar", lambda e, ps=ps, pb=pb: e.activation(out=pb[:], in_=ps[:], func=AF.Exp),
                                       reads=[pk], writes=[pbk])
                                    op("tensor", lambda e, pb=pb, g=g, kt=kt, kts=kts: e.matmul(
                                        oTw[0:65, :], lhsT=vw_aug[:, kt, g, :], rhs=pb[:], start=(kt == kts[0]), stop=(kt == kts[-1])),
                                       reads=[pbk, "vw_aug"], writes=["bank4"])
                                op("scalar", lambda e, u=u: e.copy(out=oT_sb[1][u][:], in_=oTw[0:65, :]),
                                   reads=["bank4"], writes=[f"oT_sb1{u}"])
                                for r in range(4):
                                    op("tensor", lambda e, u=u, r=r: e.transpose(out=ow[:, r, :], in_=oT_sb[1][u][:, r * 128:(r + 1) * 128],
                                                                                 identity=ident_f[0:65, 0:65]),
                                       reads=[f"oT_sb1{u}"], writes=["bank2"], c=300)
                                op("vector", lambda e, u=u: e.tensor_copy(out=dn[u][:, 1, :], in_=os_[:, :, 64]),
                                   reads=["bank1"], writes=[f"dn{u}"])
                                op("vector", lambda e, u=u: e.tensor_copy(out=dn[u][:, 2, :], in_=ow[:, :, 64]),
                                   reads=["bank2"], writes=[f"dn{u}"])
                                op("vector", lambda e, u=u: e.reciprocal(out=rdn[u][:, 1:3, :], in_=dn[u][:, 1:3, :]),
                                   reads=[f"dn{u}"], writes=[f"rdn{u}"])
                                gview = gates[:, i, :].rearrange("p (b h) -> p b h", b=3)[:, :, 4 * g:4 * g + 4]
                                op("vector", lambda e, u=u, gview=gview: e.tensor_tensor(out=coef[u][:], in0=rdn[u][:], in1=gview, op=ALU.mult),
                                   reads=[f"rdn{u}", "gates"], writes=[f"coef{u}"])
                                for r in range(4):
                                    op("vector", lambda e, u=u, r=r: e.tensor_scalar(out=obs[u][:, r, :], in0=oc[:, r, 0:64],
                                                                                     scalar1=coef[u][:, 0, r:r + 1], scalar2=None, op0=ALU.mult),
                                       reads=["bank0", f"coef{u}"], writes=[f"obs{u}"])
                                    op("vector", lambda e, u=u, r=r: e.scalar_tensor_tensor(
                                        out=obs[u][:, r, :], in0=os_[:, r, 0:64], scalar=coef[u][:, 1, r:r + 1], in1=obs[u][:, r, :],
                                        op0=ALU.mult, op1=ALU.add),
                                       reads=["bank1", f"coef{u}", f"obs{u}"], writes=[f"obs{u}"])
                                    op("vector", lambda e, u=u, r=r: e.scalar_tensor_tensor(
                                        out=obs[u][:, r, :], in0=ow[:, r, 0:64], scalar=coef[u][:, 2, r:r + 1], in1=obs[u][:, r, :],
                                        op0=ALU.mult, op1=ALU.add),
                                       reads=["bank2", f"coef{u}", f"obs{u}"], writes=[f"obs{u}"])
                                ps, pk = ringB_next()
                                tp = ps[0:64, :].rearrange("p (r t) -> p r t", r=4)
                                for r in range(4):
                                    op("tensor", lambda e, tp=tp, u=u, r=r: e.transpose(out=tp[:, r, :], in_=obs[u][:, r, :], identity=ident_f[:]),
                                       reads=[f"obs{u}"], writes=[pk])
                                op("vector", lambda e, tp=tp, u=u: e.tensor_tensor(out=yb_sb[u][:], in0=tp, in1=nz_t[u][:], op=ALU.mult),
                                   reads=[pk, f"nz_t{u}"], writes=[f"yb_sb{u}"])
                                op("sync", lambda e, u=u, g=g, tsl=tsl: e.dma_start(out=yTb_d[256 * g:256 * (g + 1), tsl].rearrange("(r d) t -> d r t", d=64), in_=yb_sb[u][:]),
                                   reads=[f"yb_sb{u}"], writes=["yTb_d"], dma=True)
                        sc.flush()

                with contextlib.ExitStack() as sC:
                    wo = sb("wo", [128, 8, D], BF16, sC)
                    mT = sb("mT", [128, 8, S], BF16, sC)
                    ya_t = sb("ya_t", [128, 4, S], BF16, sC)
                    yb_t = sb("yb_t", [128, 4, S], BF16, sC)
                    wcu = [[sb(f"wcu{p}{i}", [128, 4, 128], BF16, sC) for i in range(2)] for p in range(2)]
                    wcg = [[sb(f"wcg{p}{i}", [128, 8, 128], BF16, sC) for i in range(2)] for p in range(2)]
                    sg = [sb(f"sg{i}", [128, 512], F32, sC) for i in range(2)]
                    m1 = [sb(f"m1{i}", [128, 512], F32, sC) for i in range(2)]
                    m2 = [sb(f"m2{i}", [128, 512], F32, sC) for i in range(2)]
                    xt = [sb(f"cxt{i}", [128, D], F32, sC) for i in range(2)]
                    ringC = Ring(banks, "bank")
                    yav = yTa_d.rearrange("(hp p) s -> p hp s", p=128)
                    ybv = yTb_d.rearrange("(hp p) s -> p hp s", p=128)
                    for tb in range(4):
                        bsl = slice(tb * 512, (tb + 1) * 512)
                        op("sync", lambda e, bsl=bsl: e.dma_start(out=ya_t[:, :, bsl], in_=yav[:, :, bsl]),
                           reads=["yTa_d"], writes=[f"ya_t{tb}"], dma=True)
                        op("sync", lambda e, bsl=bsl: e.dma_start(out=yb_t[:, :, bsl], in_=ybv[:, :, bsl]),
                           reads=["yTb_d"], writes=[f"yb_t{tb}"], dma=True)
                    sgc = 0
                    for dmc in range(8):
                        cs = slice(dmc * 128, (dmc + 1) * 128)
                        p = dmc % 2
                        load_w(wcu[p][0][:], w_up_a[l, :, cs].rearrange("(hp p) m -> p hp m", p=128), f"wcu{p}0")
                        load_w(wcg[p][0][:], win_cols(l, 3864 + dmc * 128, 128), f"wcg{p}0")
                        load_w(wcu[p][1][:], w_up_b[l, :, cs].rearrange("(hp p) m -> p hp m", p=128), f"wcu{p}1")
                        load_w(wcg[p][1][:], win_cols(l, 4888 + dmc * 128, 128), f"wcg{p}1")
                        if dmc >= 1:
                            c = dmc - 1
                            load_w(wo[:, :, c * 128:(c + 1) * 128], w_out[l, :, c * 128:(c + 1) * 128].rearrange("(k p) m -> p k m", p=128), "wo")
                        if dmc == 7:
                            load_w(wo[:, :, 7 * 128:8 * 128], w_out[l, :, 7 * 128:8 * 128].rearrange("(k p) m -> p k m", p=128), "wo")
                        for tb in range(4):
                            bsl = slice(tb * 512, (tb + 1) * 512)
                            j = sgc % 2
                            sgc += 1
                            for br, (yt, ytk, mm) in enumerate(((ya_t, f"ya_t{tb}", m1), (yb_t, f"yb_t{tb}", m2))):
                                pu, puk = ringC.next()
                                mm_chain(pu[:], [(wcu[p][br][:, hp, :], yt[:, hp, bsl]) for hp in range(4)], puk, [f"wcu{p}{br}", ytk])
                                pg, pgk = ringC.next()
                                mm_chain(pg[:], [(wcg[p][br][:, k, :], hT[:, k, bsl]) for k in range(8)], pgk, [f"wcg{p}{br}", "hT"])
                                op("scalar", lambda e, pg=pg, j=j: e.activation(out=sg[j][:], in_=pg[:], func=AF.Sigmoid),
                                   reads=[pgk], writes=[f"sg{j}"])
                                op("vector", lambda e, pu=pu, j=j, mm=mm: e.tensor_tensor(out=mm[j][:], in0=pu[:], in1=sg[j][:], op=ALU.mult),
                                   reads=[puk, f"sg{j}"], writes=[f"m{br + 1}{j}"])
                            op("gpsimd", lambda e, j=j, dmc=dmc, bsl=bsl: e.tensor_tensor(out=mT[:, dmc, bsl], in0=m1[j][:], in1=m2[j][:], op=ALU.add),
                               reads=[f"m1{j}", f"m2{j}"], writes=[f"mT{tb}"])
                    for tt in range(16):
                        tb = tt // 4
                        v = tt % 2
                        op("sync", lambda e, v=v, tt=tt: e.dma_start(out=xt[v][:], in_=x_src[b, tt * 128:(tt + 1) * 128, :]),
                           reads=[], writes=[f"cxt{v}"], dma=True)
                        for half in range(2):
                            hs = slice(half * 512, (half + 1) * 512)
                            po, pok = ringC.next()
                            mm_chain(po[:], [(mT[:, k, tt * 128:(tt + 1) * 128], wo[:, k, hs]) for k in range(8)],
                                     pok, [f"mT{tb}", "wo"])
                            op("vector", lambda e, po=po, v=v, hs=hs: e.tensor_tensor(out=xt[v][:, hs], in0=po[:], in1=xt[v][:, hs], op=ALU.add),
                               reads=[pok, f"cxt{v}"], writes=[f"cxt{v}"])
                        op("sync", lambda e, v=v, tt=tt: e.dma_start(out=out[b, tt * 128:(tt + 1) * 128, :], in_=xt[v][:]),
                           reads=[f"cxt{v}"], writes=[], dma=True)
                    sc.flush()
        print("total instructions (incl. waits):", sc.ninstr)
    return nc


_INVF = (1.0 / (2.0 * math.pi) * (10000.0 ** (-(np.arange(128) % 32) / 32.0))).astype(np.float32).reshape(128, 1)
_PROG = {}


def _get_prog(NL):
    if NL not in _PROG:
        _PROG[NL] = build_program(NL)
    return _PROG[NL]


WNAMES = ["norm_g", "w_in", "q_norm_g", "k_norm_g", "cmp_pe", "cmp_w1", "cmp_b1", "cmp_w2", "w_up_a", "w_up_b", "w_out"]


def kernel(x, positions, norm_g, w_in, q_norm_g, k_norm_g, cmp_pe, cmp_w1, cmp_b1, cmp_w2,
           w_up_a, w_up_b, w_out, _layers_per_launch=DEPTH):
    ws = dict(norm_g=norm_g, w_in=w_in, q_norm_g=q_norm_g, k_norm_g=k_norm_g, cmp_pe=cmp_pe, cmp_w1=cmp_w1,
              cmp_b1=cmp_b1, cmp_w2=cmp_w2, w_up_a=w_up_a, w_up_b=w_up_b, w_out=w_out)
    ws = {k: np.ascontiguousarray(np.asarray(v, dtype=np.float32)) for k, v in ws.items()}
    xcur = np.ascontiguousarray(np.asarray(x, dtype=np.float32))
    pos = np.ascontiguousarray(np.asarray(positions, dtype=np.int32))
    NL = _layers_per_launch
    nc = _get_prog(NL)
    for l0 in range(0, DEPTH, NL):
        in_maps = []
        for c in range(NCORES):
            m = {"x": xcur[c * NB:(c + 1) * NB], "positions": pos[c * NB:(c + 1) * NB], "invf": _INVF}
            for k, v in ws.items():
                m[k] = v[l0:l0 + NL]
            in_maps.append(m)
        res = run_bass_kernel_spmd(nc, in_maps, core_ids=list(range(NCORES)))
        xcur = np.concatenate([np.asarray(r["out"]) for r in res.results], axis=0)
    return xcur.astype(np.float32)
```

```python
import contextlib
import math
import numpy as np
import concourse.bass as bass
import concourse.mybir as mybir
from concourse.bass_utils import run_bass_kernel_spmd

F32 = mybir.dt.float32
BF16 = mybir.dt.bfloat16
I32 = mybir.dt.int32
AF = mybir.ActivationFunctionType
ALU = mybir.AluOpType
AX = mybir.AxisListType

S = 2048
D = 1024
NIN = 5912
NCORES = 8
NB = 2
DEPTH = 4
NEG = -30000.0
EPS = 1e-6

ENGS = ["tensor", "vector", "scalar", "gpsimd", "sync"]
EPOCH = 30000
NDMASEM = 12


DEFCOST = {"tensor": 560.0, "scalar": 600.0, "vector": 650.0, "gpsimd": 1200.0, "sync": 60.0}
WINDOW = 64


class Sched:
    def __init__(self, nc, stack):
        self.nc = nc
        self.stack = stack
        self.cnt = {e: 0 for e in ENGS}
        self.esem = {}
        for e in ENGS:
            if e != "sync":
                self.esem[e] = stack.enter_context(nc.semaphore(f"es_{e}_0"))
        self.eepoch = {e: 0 for e in ENGS}
        self.dsem = {}
        self.dcnt = {}
        self.dnext = {}
        for q in ["sync", "gpsimd", "scalar"]:
            self.dsem[q] = [stack.enter_context(nc.semaphore(f"ds_{q}_{i}")) for i in range(NDMASEM)]
            self.dcnt[q] = [0] * NDMASEM
            self.dnext[q] = 0
        self.ninstr = 0
        self.reorder = True
        self._reset()

    def _reset(self):
        self.nodes = []
        self.last_w = {}
        self.last_r = {}

    def op(self, eng, fn, reads=(), writes=(), dma=False, c=None):
        preds = set()
        for b in reads:
            w = self.last_w.get(b)
            if w is not None:
                preds.add(w)
            if b.startswith("bank"):
                for r_ in self.last_r.get(b, ()):
                    if self.nodes[r_][0] != eng:
                        preds.add(r_)
        for b in writes:
            w = self.last_w.get(b)
            if w is not None:
                preds.add(w)
            preds.update(self.last_r.get(b, ()))
        nid = len(self.nodes)
        if c is None:
            c = 3000.0 if dma else DEFCOST[eng]
        self.nodes.append((eng, fn, dma, preds, float(c)))
        for b in reads:
            self.last_r.setdefault(b, []).append(nid)
        for b in writes:
            self.last_w[b] = nid
            self.last_r[b] = []
        return nid

    def _simulate(self):
        import heapq
        nodes = self.nodes
        n = len(nodes)
        order = {e: [] for e in ENGS}
        if not self.reorder:
            for nid, nd in enumerate(nodes):
                order[nd[0]].append(nid)
            return order
        succ = [[] for _ in range(n)]
        indeg = [0] * n
        for nid, nd in enumerate(nodes):
            for p in nd[3]:
                succ[p].append(nid)
            indeg[nid] = len(nd[3])
        ready = {e: [] for e in ENGS}
        for nid, nd in enumerate(nodes):
            if indeg[nid] == 0:
                heapq.heappush(ready[nd[0]], nid)
        free_at = {e: 0.0 for e in ENGS}
        events = []
        now = 0.0
        left = n
        while left:
            progressed = False
            for e in ENGS:
                if free_at[e] <= now and ready[e]:
                    nid = heapq.heappop(ready[e])
                    nd = nodes[nid]
                    if nd[2]:
                        free_at[e] = now + 60.0
                        heapq.heappush(events, (free_at[e], -1))
                    else:
                        free_at[e] = now + nd[4]
                    heapq.heappush(events, (now + nd[4], nid))
                    order[e].append(nid)
                    left -= 1
                    progressed = True
            if not progressed:
                assert events, "scheduler deadlock"
                t, nid = heapq.heappop(events)
                now = max(now, t)
                while True:
                    if nid >= 0:
                        for sx in succ[nid]:
                            indeg[sx] -= 1
                            if indeg[sx] == 0:
                                heapq.heappush(ready[nodes[sx][0]], sx)
                    if events and events[0][0] <= now:
                        t, nid = heapq.heappop(events)
                    else:
                        break
        return order

    def flush(self):
        nc = self.nc
        nodes = self.nodes
        order = self._simulate()
        tok = [None] * len(nodes)
        extra = {}
        for e in ENGS:
            for nid in order[e]:
                if nodes[nid][2]:
                    i = self.dnext[e]
                    self.dnext[e] = (i + 1) % NDMASEM
                    sem = self.dsem[e][i]
                    if self.dcnt[e][i] > 0:
                        extra[nid] = (sem, self.dcnt[e][i])
                    self.dcnt[e][i] += 16
                    tok[nid] = (sem, self.dcnt[e][i], 16)
                else:
                    if self.cnt[e] >= EPOCH:
                        self.eepoch[e] += 1
                        self.esem[e] = self.stack.enter_context(nc.semaphore(f"es_{e}_{self.eepoch[e]}"))
                        self.cnt[e] = 0
                    self.cnt[e] += 1
                    tok[nid] = (self.esem[e], self.cnt[e], 1)
        prog = {}
        for e in ENGS:
            seen = {}
            lst = []
            for nid in order[e]:
                waits = {}
                cands = [tok[p][:2] for p in nodes[nid][3]]
                if nid in extra:
                    cands.append(extra[nid])
                for (sm, v) in cands:
                    k = id(sm)
                    if seen.get(k, 0) >= v:
                        continue
                    if k not in waits or waits[k][1] < v:
                        waits[k] = (sm, v)
                wl = list(waits.values())
                for (sm, v) in wl:
                    seen[id(sm)] = v
                lst.append((wl, nodes[nid][1], tok[nid][0], tok[nid][2]))
                self.ninstr += 1 + len(wl)
            prog[e] = lst
        finals = {}
        for q in ["sync", "gpsimd", "scalar"]:
            finals[q] = [(self.dsem[q][i], self.dcnt[q][i]) for i in range(NDMASEM) if self.dcnt[q][i] > 0]
        with nc.Block() as block:
            def mk(ename):
                def body(eng):
                    for (wl, fn, sem, inc) in prog[ename]:
                        for (sm, v) in wl:
                            eng.wait_ge(sm, v)
                        fn(eng).then_inc(sem, inc)
                    for (sm, v) in finals.get(ename, []):
                        eng.wait_ge(sm, v)
                return body
            block.tensor(mk("tensor"))
            block.vector(mk("vector"))
            block.scalar(mk("scalar"))
            block.gpsimd(mk("gpsimd"))
            block.sync(mk("sync"))
        self._reset()


class Ring:
    def __init__(self, bufs, name):
        self.bufs = bufs
        self.name = name
        self.i = 0

    def next(self):
        j = self.i % len(self.bufs)
        self.i += 1
        return self.bufs[j], f"{self.name}{j}"


def bc4(ap2d, n=4):
    p, f = ap2d.shape
    return ap2d.unsqueeze(1).broadcast_to([p, n, f])


STOPAT = 99
B0PART = 99


def build_program(NL):
    nc = bass.Bass("TRN2", target_bir_lowering=False)
    dt_in = lambda name, shape, dt=F32: nc.dram_tensor(name, shape, dt, kind="ExternalInput").ap()
    x_in = dt_in("x", [NB, S, D])
    pos_in = dt_in("positions", [NB, S], I32)
    norm_g = dt_in("norm_g", [NL, D])
    w_in = dt_in("w_in", [NL, D, NIN])
    q_norm_g = dt_in("q_norm_g", [NL, 64])
    k_norm_g = dt_in("k_norm_g", [NL, 3, 64])
    cmp_pe = dt_in("cmp_pe", [NL, 2, 32, 64])
    cmp_w1 = dt_in("cmp_w1", [NL, 2, 2048, 256])
    cmp_b1 = dt_in("cmp_b1", [NL, 2, 256])
    cmp_w2 = dt_in("cmp_w2", [NL, 2, 256, 64])
    w_up_a = dt_in("w_up_a", [NL, 512, D])
    w_up_b = dt_in("w_up_b", [NL, 512, D])
    w_out = dt_in("w_out", [NL, D, D])
    invf_in = dt_in("invf", [128, 1])
    out = nc.dram_tensor("out", [NB, S, D], F32, kind="ExternalOutput").ap()
    yTa_d = nc.dram_tensor("yTa_scr", [512, S], BF16).ap()
    yTb_d = nc.dram_tensor("yTb_scr", [512, S], BF16).ap()
    nzT_d = nc.dram_tensor("nzT_scr", [512, S], BF16).ap()

    with contextlib.ExitStack() as st:
        sc = Sched(nc, st)
        E = st.enter_context
        op = sc.op

        uid = [0]

        def sb(name, shape, dt, stack=None):
            uid[0] += 1
            return (stack if stack is not None else st).enter_context(nc.sbuf_tensor(f"{name}_{uid[0]}", shape, dt))

        ident_bf = sb("ident_bf", [128, 128], BF16)
        ident_f = sb("ident_f", [128, 128], F32)
        negtri = sb("negtri", [128, 128], BF16)
        negones = sb("negones", [128, 128], BF16)
        ones_blk = sb("ones_blk", [128, 128], BF16)
        rotM_blk = sb("rotM_blk", [128, 128], BF16)
        maskS = sb("maskS", [128, 128], BF16)
        maskC = sb("maskC", [128, 128], BF16)
        maskW = sb("maskW", [128, 128], BF16)
        Eexp = sb("Eexp", [32, S], BF16)
        cmask = sb("cmask", [127, S], BF16)
        bonus = sb("bonus", [128, 16, 32], F32)
        bonusF = sb("bonusF", [128, 16, 32], F32)
        invf = sb("invf_sb", [128, 1], F32)
        hT = sb("hT", [128, 8, S], BF16)
        cosT = sb("cosT", [128, S], F32)
        sinT = sb("sinT", [128, S], F32)
        cosC = sb("cosC", [64, 127], F32)
        sinC = sb("sinC", [64, 127], F32)
        VO = sb("VO", [127, 2, 97], BF16)
        vs_aug = sb("vs_aug", [128, 16, 2, 65], BF16)
        vw_aug = sb("vw_aug", [128, 16, 2, 65], BF16)
        kcT = sb("kcT", [64, 2, 127], BF16)
        ksTa = sb("ksTa", [96, 2, S], BF16)
        gq = sb("gq", [128, 1], F32)
        gk = sb("gk", [128, 3], F32)
        wst = Ring([sb(f"wst{i}", [128, 8, 128], F32) for i in range(2)], "wst")
        banks = [E(nc.psum_tensor(f"bank{i}", [128, 512], F32)) for i in range(8)]


        def sel(t, ap, pattern, cop, fill, base, cm, key):
            op("gpsimd", lambda e: e.affine_select(out=ap, in_=ap, pattern=pattern, compare_op=cop,
                                                   fill=fill, base=base, channel_multiplier=cm),
               reads=[key], writes=[key])

        def mset(ap, val, key):
            op("gpsimd", lambda e: e.memset(ap, val), writes=[key])

        mset(ident_bf[:], 0.0, "ident_bf")
        sel(ident_bf, ident_bf[:], [[-1, 128]], ALU.not_equal, 1.0, 0, 1, "ident_bf")
        mset(ident_f[:], 0.0, "ident_f")
        sel(ident_f, ident_f[:], [[-1, 128]], ALU.not_equal, 1.0, 0, 1, "ident_f")
        mset(negtri[:], -1.0, "negtri")
        sel(negtri, negtri[:], [[-1, 128]], ALU.is_ge, 0.0, 0, 1, "negtri")
        mset(negones[:], -1.0, "negones")
        mset(ones_blk[:], 1.0, "ones_blk")
        mset(ones_blk[0:64, 64:128], 0.0, "ones_blk")
        mset(ones_blk[64:128, 0:64], 0.0, "ones_blk")
        mset(rotM_blk[:], 0.0, "rotM_blk")
        for q0 in (0, 64):
            sel(rotM_blk, rotM_blk[q0:q0 + 64, q0:q0 + 64], [[-1, 64]], ALU.not_equal, -1.0, -32, 1, "rotM_blk")
            sel(rotM_blk, rotM_blk[q0:q0 + 64, q0:q0 + 64], [[-1, 64]], ALU.not_equal, 1.0, 32, 1, "rotM_blk")
        mset(maskS[:], 0.0, "maskS")
        sel(maskS, maskS[:], [[1, 128]], ALU.is_ge, NEG, -1, -1, "maskS")
        mset(maskC[:], 0.0, "maskC")
        sel(maskC, maskC[:], [[1, 128]], ALU.is_ge, NEG, 0, -1, "maskC")
        mset(maskW[:], 0.0, "maskW")
        sel(maskW, maskW[:], [[-1, 128]], ALU.is_ge, NEG, -1, 1, "maskW")
        mset(Eexp[:], 1.0, "Eexp")
        sel(Eexp, Eexp[:], [[1, S]], ALU.is_ge, 0.0, 0, -64, "Eexp")
        sel(Eexp, Eexp[:], [[-1, S]], ALU.is_ge, 0.0, 63, 64, "Eexp")
        for g_ in range(2):
            mset(ksTa[64:96, g_, :], 1.0, "ksTa")
            sel(ksTa, ksTa[64:96, g_, :], [[1, S]], ALU.is_ge, 0.0, 0, -64, "ksTa")
            sel(ksTa, ksTa[64:96, g_, :], [[-1, S]], ALU.is_ge, 0.0, 63, 64, "ksTa")
        mset(cmask[:], 0.0, "cmask")
        sel(cmask, cmask[:], [[1, S]], ALU.is_ge, NEG, -31, -16, "cmask")
        mset(VO[:], 1.0, "VO")
        sel(VO, VO[:, :, 65:97], [[0, 2], [64, 32]], ALU.is_ge, 0.0, 63, -16, "VO")
        sel(VO, VO[:, :, 65:97], [[0, 2], [-64, 32]], ALU.is_ge, 0.0, 31, 16, "VO")
        mset(vs_aug[:], 1.0, "vs_aug")
        mset(vw_aug[:], 1.0, "vw_aug")
        mset(bonus[:], 0.0, "bonus")
        sel(bonus, bonus[:], [[128, 16], [-64, 32]], ALU.is_ge, -1e30, 0, 1, "bonus")
        mset(bonusF[:], 1e4, "bonusF")
        sel(bonusF, bonusF[:], [[128, 16], [-64, 32]], ALU.is_ge, 0.0, 0, 1, "bonusF")
        sel(bonusF, bonusF[:], [[-128, 16], [64, 32]], ALU.is_ge, 0.0, 127, -1, "bonusF")
        op("gpsimd", lambda e: e.tensor_tensor(out=bonus[:], in0=bonus[:], in1=bonusF[:], op=ALU.add),
           reads=["bonus", "bonusF"], writes=["bonus"])
        mset(bonus[:, :, 0:1], 1e4, "bonus")
        op("sync", lambda e: e.dma_start(out=invf[:], in_=invf_in[:, :]), writes=["invf"], dma=True)
        sc.flush()

        def load_w(dst_ap, src_ap, key, np_=128):
            stg, skey = wst.next()
            a, b = src_ap.shape[1], src_ap.shape[2]
            sv = stg[0:np_, 0:a, 0:b]
            op("sync", lambda e: e.dma_start(out=sv, in_=src_ap), writes=[skey], dma=True)
            op("gpsimd", lambda e: e.tensor_copy(out=dst_ap, in_=sv), reads=[skey], writes=[key])

        def win_cols(l, c0, n):
            return w_in[l, :, c0:c0 + n].rearrange("(k p) m -> p k m", p=128)

        def mm_chain(out_ap, pairs, okey, rkeys):
            n = len(pairs)
            for j, (lh, rh) in enumerate(pairs):
                op("tensor", lambda e, lh=lh, rh=rh, j=j: e.matmul(out_ap, lhsT=lh, rhs=rh,
                                                                    start=(j == 0), stop=(j == n - 1)),
                   reads=rkeys, writes=[okey])

        for b in range(NB):
            with contextlib.ExitStack() as ss_:
                posi = sb("posi", [128, S], I32, ss_)
                posf = sb("posf", [128, S], F32, ss_)
                vv = sb("vv", [128, S], F32, ss_)
                uu = sb("uu", [128, S], F32, ss_)
                ui = sb("ui", [128, S], I32, ss_)
                uf = sb("uf", [128, S], F32, ss_)
                gg = sb("gg", [128, S], F32, ss_)
                mm_ = sb("mm_", [128, S], F32, ss_)
                posm = sb("posm", [128, 127], F32, ss_)
                vm = sb("vm", [128, 127], F32, ss_)
                op("sync", lambda e: e.dma_start(out=posi[:], in_=pos_in[b].partition_broadcast(128)),
                   writes=["posi"], dma=True)
                op("vector", lambda e: e.tensor_copy(out=posf[:], in_=posi[:]), reads=["posi"], writes=["posf"])
                op("vector", lambda e: e.tensor_scalar(out=vv[:], in0=posf[:], scalar1=invf[:, 0:1], scalar2=None,
                                                       op0=ALU.mult), reads=["posf", "invf"], writes=["vv"])
                win = bass.AP(posf[:].tensor, posf[:].offset, [[S, 128], [16, 127], [1, 32]])
                op("vector", lambda e: e.reduce_sum(out=posm[:], in_=win, axis=AX.X), reads=["posf"], writes=["posm"])
                op("vector", lambda e: e.tensor_scalar(out=vm[:], in0=posm[:], scalar1=invf[:, 0:1], scalar2=1.0 / 32,
                                                       op0=ALU.mult, op1=ALU.mult),
                   reads=["posm", "invf"], writes=["vm"])

                def table(vsrc, n, add, dst, dkey, skey, P=128):
                    u = uu[0:P, 0:n]
                    op("vector", lambda e: e.tensor_scalar(out=u, in0=vsrc, scalar1=float(add), scalar2=None,
                                                           op0=ALU.add), reads=[skey], writes=["uu"])
                    op("vector", lambda e: e.tensor_copy(out=ui[0:P, 0:n], in_=u), reads=["uu"], writes=["ui"])
                    op("vector", lambda e: e.tensor_copy(out=uf[0:P, 0:n], in_=ui[0:P, 0:n]), reads=["ui"], writes=["uf"])
                    op("vector", lambda e: e.tensor_tensor(out=gg[0:P, 0:n], in0=u, in1=uf[0:P, 0:n], op=ALU.subtract),
                       reads=["uu", "uf"], writes=["gg"])
                    op("vector", lambda e: e.scalar_tensor_tensor(out=mm_[0:P, 0:n], in0=gg[0:P, 0:n], scalar=0.5,
                                                                  in1=gg[0:P, 0:n], op0=ALU.is_gt, op1=ALU.subtract),
                       reads=["gg"], writes=["mm_"])
                    op("scalar", lambda e: e.activation(out=dst, in_=mm_[0:P, 0:n], func=AF.Sin,
                                                        scale=-2.0 * math.pi), reads=["mm_"], writes=[dkey])

                table(vv[:], S, 0.0, sinT[:], "sinT", "vv")
                table(vv[:], S, 0.25, cosT[:], "cosT", "vv")
                table(vm[0:64, :], 127, 0.0, sinC[:], "sinC", "vm", P=64)
                table(vm[0:64, :], 127, 0.25, cosC[:], "cosC", "vm", P=64)
                sc.flush()

            for l in range(NL):
                x_src = x_in if l == 0 else out
                with contextlib.ExitStack() as s1:
                    gqr = sb("gqr", [128, 1], F32, s1)
                    gbc = sb("gbc", [128, D], F32, s1)
                    xt = [sb(f"p1xt{i}", [128, D], F32, s1) for i in range(2)]
                    junk = sb("p1junk", [128, D], BF16, s1)
                    xs = [sb(f"p1xs{i}", [128, D], BF16, s1) for i in range(2)]
                    ssq = [sb(f"p1ss{i}", [128, 1], F32, s1) for i in range(2)]
                    rt_ = [sb(f"p1rt{i}", [128, 1], F32, s1) for i in range(2)]
                    rs_ = [sb(f"p1rs{i}", [128, 1], F32, s1) for i in range(2)]
                    for q0 in (0, 64):
                        op("sync", lambda e, q0=q0: e.dma_start(out=gqr[q0:q0 + 64, :], in_=q_norm_g[l].rearrange("(p o) -> p o", o=1)),
                           writes=["gqr"], dma=True)
                    op("vector", lambda e: e.tensor_scalar(out=gq[:], in0=gqr[:], scalar1=0.125, scalar2=None,
                                                           op0=ALU.mult), reads=["gqr"], writes=["gq"])
                    for j in range(3):
                        for q0 in (0, 64):
                            op("sync", lambda e, j=j, q0=q0: e.dma_start(out=gk[q0:q0 + 64, j:j + 1],
                                                                         in_=k_norm_g[l, j].rearrange("(p o) -> p o", o=1)),
                               writes=["gk"], dma=True)
                    op("sync", lambda e: e.dma_start(out=gbc[:], in_=norm_g[l].partition_broadcast(128)),
                       writes=["gbc"], dma=True)
                    for tt in range(16):
                        i = tt % 2
                        pT = banks[i][:].bitcast(BF16)
                        op("sync", lambda e, tt=tt, i=i: e.dma_start(out=xt[i][:], in_=x_src[b, tt * 128:(tt + 1) * 128, :]),
                           writes=[f"xt{i}"], dma=True)
                        op("scalar", lambda e, i=i: e.activation(out=junk[:], in_=xt[i][:], func=AF.Square,
                                                                 accum_out=ssq[i][:]),
                           reads=[f"xt{i}"], writes=["junk", f"ss{i}"])
                        op("scalar", lambda e, i=i: e.activation(out=rt_[i][:], in_=ssq[i][:], func=AF.Sqrt,
                                                                 scale=1.0 / D, bias=EPS),
                           reads=[f"ss{i}"], writes=[f"rt{i}"])
                        op("vector", lambda e, i=i: e.reciprocal(out=rs_[i][:], in_=rt_[i][:]),
                           reads=[f"rt{i}"], writes=[f"rs{i}"])
                        op("vector", lambda e, i=i: e.scalar_tensor_tensor(out=xs[i][:], in0=xt[i][:], scalar=rs_[i][:],
                                                                           in1=gbc[:], op0=ALU.mult, op1=ALU.mult),
                           reads=[f"xt{i}", f"rs{i}", "gbc"], writes=[f"xs{i}"])
                        for k in range(8):
                            op("tensor", lambda e, i=i, k=k, pT=pT: e.transpose(out=pT[:, k * 128:(k + 1) * 128],
                                                                                in_=xs[i][:, k * 128:(k + 1) * 128],
                                                                                identity=ident_bf[:]),
                               reads=[f"xs{i}"], writes=[f"bank{i}"])
                        op("scalar", lambda e, i=i, tt=tt, pT=pT: e.copy(out=hT[:, :, tt * 128:(tt + 1) * 128],
                                                                         in_=pT.rearrange("p (k t) -> p k t", k=8)),
                           reads=[f"bank{i}"], writes=["hT"])
                    sc.flush()

                if STOPAT <= 1:
                    continue
                with contextlib.ExitStack() as sa:
                    XP = [[sb(f"XP{p}{i}", [128, S], BF16, sa) for i in range(3)] for p in range(2)]
                    XO = [[sb(f"XO{p}{i}", [64, S], BF16, sa) for i in range(3)] for p in range(2)]
                    v_all = sb("v_all", [128, 16, 512], BF16, sa)
                    wv_all = sb("wv_all", [128, 8, 512], BF16, sa)
                    wA2 = [[sb(f"wA{p}{i}", [128, 8, 128], BF16, sa) for i in range(3)] for p in range(2)]
                    e_sb = [sb(f"e_sb{i}", [128, 512], F32, sa) for i in range(2)]
                    sp_bf = [[sb(f"sp_bf{s}{i}", [128, 512], BF16, sa) for i in range(2)] for s in range(2)]
                    R32 = [sb(f"R32{s}", [128, 512], F32, sa) for s in range(2)]
                    Rbf = [[sb(f"Rbf{s}{i}", [128, 512], BF16, sa) for i in range(2)] for s in range(2)]
                    w_bf = [[sb(f"w_bf{s}{i}", [128, 512], BF16, sa) for i in range(2)] for s in range(2)]
                    ya_sb = [sb(f"ya_sb{i}", [64, 512], BF16, sa) for i in range(2)]
                    ringA = Ring(banks[6:8], "bank6")

                    def ringA_next():
                        j = ringA.i % 2
                        ringA.i += 1
                        return banks[6 + j], f"bank{6 + j}"

                    for c in range(4):
                        load_w(wv_all[:, :, c * 128:(c + 1) * 128], win_cols(l, 1024 + c * 128, 128), f"wv_all{c}")
                    for tt in range(16):
                        ps, pk = ringA_next()
                        mm_chain(ps[:], [(hT[:, k, tt * 128:(tt + 1) * 128], wv_all[:, k, :]) for k in range(8)],
                                 pk, ["wv_all0", "wv_all1", "wv_all2", "wv_all3", "hT"])
                        if tt % 2 == 0:
                            op("vector", lambda e, ps=ps, tt=tt: e.tensor_copy(out=v_all[:, tt, :], in_=ps[:]),
                               reads=[pk], writes=["v_all"])
                        else:
                            op("scalar", lambda e, ps=ps, tt=tt: e.copy(out=v_all[:, tt, :], in_=ps[:]),
                               reads=[pk], writes=["v_all"])

                    for hp in range(4):
                        pp = hp % 2
                        wA = wA2[pp]
                        PK = f"p{pp}"
                        qT = [XP[pp][0], XO[pp][0]]
                        kT = [XP[pp][1], XO[pp][1]]
                        szT = [XP[pp][2], XO[pp][2]]
                        for wi, sec in enumerate((0, 1, 3)):
                            load_w(wA[wi][:], win_cols(l, sec * 512 + hp * 128, 128), f"wA{wi}" + PK)
                        for wi in range(3):
                            for tb in range(4):
                                ps, pk = ringA_next()
                                csl = slice(tb * 512, (tb + 1) * 512)
                                mm_chain(ps[:], [(wA[wi][:, k, :], hT[:, k, csl]) for k in range(8)], pk, [f"wA{wi}" + PK, "hT"])
                                d_ap = XP[pp][wi][:, csl]
                                ek = f"XP{wi}t{tb}" + PK
                                if wi == 0:
                                    op("scalar", lambda e, ps=ps, d_ap=d_ap: e.activation(out=d_ap, in_=ps[:], func=AF.Copy, scale=0.125),
                                       reads=[pk], writes=[ek])
                                elif wi == 1:
                                    op("vector", lambda e, ps=ps, d_ap=d_ap: e.tensor_copy(out=d_ap, in_=ps[:]),
                                       reads=[pk], writes=[ek])
                                else:
                                    op("scalar", lambda e, ps=ps, d_ap=d_ap: e.activation(out=d_ap, in_=ps[:], func=AF.Silu),
                                       reads=[pk], writes=[ek])
                                op("sync", lambda e, pp=pp, wi=wi, csl=csl: e.dma_start(out=XO[pp][wi][:, csl], in_=XP[pp][wi][64:128, csl]),
                                   reads=[ek], writes=[f"XO{wi}t{tb}" + PK], dma=True, c=2500)

                        def xkeys(wi, s, qb=None):
                            nm = "XP" if s == 0 else "XO"
                            if qb is None:
                                return [f"{nm}{wi}t{t}" + PK for t in range(4)]
                            return [f"{nm}{wi}t{qb}" + PK]

                        tiles = []
                        for qb in range(4):
                            nt = 4 * qb + 4
                            for kt in range(nt - 1, -1, -1):
                                tiles.append((qb, kt, kt == nt - 1, kt == 0))

                        def c0_of(qb, kt):
                            return max(kt - 4 * qb, 0) * 128

                        def emit_qk(n, s, kT=kT, qT=qT, xkeys=xkeys):
                            qb, kt, first, last = tiles[n]
                            zb = banks[2 * s + (n % 2)]
                            zk = f"bank{2 * s + (n % 2)}"
                            diag = kt >= 4 * qb
                            c0 = c0_of(qb, kt)
                            op("tensor", lambda e: e.matmul(zb[:, c0:512], lhsT=kT[s][0:64, kt * 128:(kt + 1) * 128],
                                                            rhs=qT[s][0:64, qb * 512 + c0:(qb + 1) * 512], start=True, stop=True),
                               reads=xkeys(1, s) + xkeys(0, s, qb), writes=[zk], c=100 + 0.75 * (512 - c0))
                            if diag:
                                op("tensor", lambda e: e.matmul(zb[:, c0:c0 + 128], lhsT=ident_bf[:], rhs=maskS[:],
                                                                start=False, stop=True, skip_group_check=True),
                                   reads=[], writes=[zk], c=200)

                        for s in range(2):
                            emit_qk(0, s)
                        for n in range(len(tiles)):
                            qb, kt, first, last = tiles[n]
                            par = n % 2
                            c0 = c0_of(qb, kt)
                            diag = kt >= 4 * qb
                            c1 = (kt - 4 * qb + 1) * 128 if diag else 0
                            w = 512 - c0
                            if first:
                                for s in range(2):
                                    op("gpsimd", lambda e, s=s: e.memset(R32[s][:], 0.0), writes=[f"R32{s}"], c=600)
                            for s in range(2):
                                zb = banks[2 * s + par]
                                zk = f"bank{2 * s + par}"
                                op("scalar", lambda e, zb=zb, s=s, c0=c0: e.activation(out=e_sb[s][:, c0:512], in_=zb[:, c0:512], func=AF.Exp),
                                   reads=[zk], writes=[f"e_sb{s}"], c=220 + 0.72 * w)
                                op("scalar", lambda e, s=s, par=par, c0=c0: e.activation(out=sp_bf[s][par][:, c0:512], in_=e_sb[s][:, c0:512],
                                                                                         func=AF.Ln, bias=1.0),
                                   reads=[f"e_sb{s}"], writes=[f"sp_bf{s}{par}"], c=250 + 0.75 * w)
                            for s in range(2):
                                zb = banks[2 * s + par]
                                zk = f"bank{2 * s + par}"
                                op("tensor", lambda e, zb=zb, s=s, par=par, first=first, c0=c0: e.matmul(
                                    zb[:, c0:512], lhsT=negtri[:], rhs=sp_bf[s][par][:, c0:512], start=False, stop=first, skip_group_check=True),
                                   reads=[f"sp_bf{s}{par}"], writes=[zk], c=100 + 0.75 * w)
                                if not first:
                                    op("tensor", lambda e, zb=zb, s=s, par=par, c1=c1: e.matmul(
                                        zb[:, c1:512], lhsT=negones[:], rhs=Rbf[s][par][:, c1:512], start=False, stop=True, skip_group_check=True),
                                       reads=[f"Rbf{s}{par}"], writes=[zk], c=100 + 0.75 * (512 - c1))
                                if not last:
                                    op("gpsimd", lambda e, s=s, par=par, c0=c0: e.tensor_tensor(out=R32[s][:, c0:512], in0=R32[s][:, c0:512],
                                                                                                in1=sp_bf[s][par][:, c0:512], op=ALU.add),
                                       reads=[f"sp_bf{s}{par}", f"R32{s}"], writes=[f"R32{s}"], c=200 + 2.0 * w)
                                    op("vector", lambda e, s=s, par=par, c0=c0: e.tensor_copy(out=Rbf[s][1 - par][:, c0:512], in_=R32[s][:, c0:512]),
                                       reads=[f"R32{s}"], writes=[f"Rbf{s}{1 - par}"], c=100 + 1.1 * w)
                            if n + 1 < len(tiles):
                                for s in range(2):
                                    emit_qk(n + 1, s)
                            for s in range(2):
                                zb = banks[2 * s + par]
                                zk = f"bank{2 * s + par}"
                                op("scalar", lambda e, zb=zb, s=s, par=par, c0=c0: e.activation(out=w_bf[s][par][:, c0:512], in_=zb[:, c0:512], func=AF.Exp),
                                   reads=[zk], writes=[f"w_bf{s}{par}"], c=220 + 0.72 * w)
                            for s in range(2):
                                ob = banks[4 + s]
                                h = 2 * hp + s
                                op("tensor", lambda e, ob=ob, s=s, par=par, kt=kt, first=first, last=last, h=h, c0=c0: e.matmul(
                                    ob[0:64, c0:512], lhsT=v_all[:, kt, h * 64:(h + 1) * 64], rhs=w_bf[s][par][:, c0:512],
                                    start=first, stop=last, skip_group_check=True),
                                   reads=[f"w_bf{s}{par}", "v_all"], writes=[f"bank{4 + s}"], c=100 + 0.75 * w)
                                if last:
                                    op("vector", lambda e, ob=ob, s=s, qb=qb, szT=szT: e.tensor_tensor(
                                        out=ya_sb[s][:], in0=ob[0:64, :], in1=szT[s][0:64, qb * 512:(qb + 1) * 512], op=ALU.mult),
                                       reads=[f"bank{4 + s}"] + xkeys(2, s, qb), writes=[f"ya_sb{s}"])
                                    op("sync", lambda e, s=s, h=h, qb=qb: e.dma_start(out=yTa_d[h * 64:(h + 1) * 64, qb * 512:(qb + 1) * 512], in_=ya_sb[s][:]),
                                       reads=[f"ya_sb{s}"], writes=["yTa_d"], dma=True)
                    sc.flush()

                if STOPAT <= 2:
                    continue
                with contextlib.ExitStack() as sB:
                    nqT = sb("nqT", [64, 8, S], BF16, sB)
                    ksT = ksTa[0:64]
                    kwT = sb("kwT", [64, 2, S], BF16, sB)
                    gates = sb("gates", [128, 16, 24], F32, sB)
                    kcr = sb("kcr", [64, 2, S], BF16, sB)
                    vcr = sb("vcr", [64, 2, S], BF16, sB)
                    ringB = Ring(banks[3:8], "bank")

                    def ringB_next():
                        j = 3 + (ringB.i % 5)
                        ringB.i += 1
                        return banks[j], f"bank{j}"

                    nr = {nm: [sb(f"nr_{nm}{i}", [128, 512], dt_, sB) for i in range(2)]
                          for nm, dt_ in (("sq", BF16), ("rt", F32), ("qn", BF16), ("t1", F32), ("t2", F32))}
                    nrc = [0]

                    def normrope(ps, pk, n, P, gcol, gkey, cos_ap, sin_ap, tkeys, out_ap, okey):
                        i = nrc[0] % 2
                        nrc[0] += 1
                        sq, rt, qn, t1, t2 = (nr[nm][i][0:P, 0:n] for nm in ("sq", "rt", "qn", "t1", "t2"))
                        rs = rt
                        kk = lambda nm: f"nr_{nm}{i}"
                        raw = ps[0:P, 0:n]
                        op("scalar", lambda e: e.activation(out=sq, in_=raw, func=AF.Square), reads=[pk], writes=[kk("sq")])
                        p2, p2k = ringB_next()
                        op("tensor", lambda e: e.matmul(p2[0:P, 0:n], lhsT=ones_blk[0:P, 0:P], rhs=sq, start=True, stop=True),
                           reads=[kk("sq")], writes=[p2k])
                        op("scalar", lambda e: e.activation(out=rt, in_=p2[0:P, 0:n], func=AF.Sqrt, scale=1.0 / 64, bias=EPS),
                           reads=[p2k], writes=[kk("rt")])
                        op("vector", lambda e: e.reciprocal(out=rs, in_=rt), reads=[kk("rt")], writes=[kk("rt")], c=3300)
                        op("vector", lambda e: e.scalar_tensor_tensor(out=qn, in0=raw, scalar=gcol, in1=rs,
                                                                      op0=ALU.mult, op1=ALU.mult),
                           reads=[pk, kk("rt"), gkey], writes=[kk("qn")])
                        p3, p3k = ringB_next()
                        op("tensor", lambda e: e.matmul(p3[0:P, 0:n], lhsT=rotM_blk[0:P, 0:P], rhs=qn, start=True, stop=True),
                           reads=[kk("qn")], writes=[p3k])
                        op("gpsimd", lambda e: e.tensor_tensor(out=t1, in0=qn, in1=cos_ap, op=ALU.mult),
                           reads=[kk("qn")] + tkeys, writes=[kk("t1")])
                        op("vector", lambda e: e.tensor_tensor(out=t2, in0=p3[0:P, 0:n], in1=sin_ap, op=ALU.mult),
                           reads=[p3k] + tkeys, writes=[kk("t2")])
                        op("gpsimd", lambda e: e.tensor_tensor(out=out_ap, in0=t1, in1=t2, op=ALU.add),
                           reads=[kk("t1"), kk("t2")], writes=[okey])

                    with contextlib.ExitStack() as sB0:
                        wB = [sb(f"wB{i}", [128, 8, 128], BF16, sB0) for i in range(3)]
                        wBr = Ring(wB, "wB")
                        wv4 = sb("wv4", [128, 8, 512], BF16, sB0)
                        tmpO = Ring([sb(f"tmpO{i}", [128, 512], BF16, sB0) for i in range(3)], "tmpO")

                        def proj128(tb, wt, wk):
                            ps, pk = ringB_next()
                            mm_chain(ps[:], [(wt[:, k, :], hT[:, k, tb * 512:(tb + 1) * 512]) for k in range(8)], pk, [wk, "hT"])
                            return ps, pk

                        def shift2(to, tk, dst0, dst1, kbase):
                            op("sync", lambda e: e.dma_start(out=dst0, in_=to[0:64, :]), reads=[tk], writes=[kbase + "a"], dma=True, c=2500)
                            op("sync", lambda e: e.dma_start(out=dst1, in_=to[64:128, :]), reads=[tk], writes=[kbase + "b"], dma=True, c=2500)

                        for pr in range(4):
                            wt, wk = wBr.next()
                            load_w(wt[:], win_cols(l, 2048 + pr * 128, 128), wk)
                            for tb in range(4):
                                csl = slice(tb * 512, (tb + 1) * 512)
                                ps, pk = proj128(tb, wt, wk)
                                to, tk = tmpO.next()
                                normrope(ps, pk, 512, 128, gq[:, 0:1], "gq", cosT[:, csl], sinT[:, csl], ["cosT", "sinT"], to[:], tk)
                                shift2(to, tk, nqT[:, 2 * pr, csl], nqT[:, 2 * pr + 1, csl], f"nqT{pr}_{tb}")
                        for c0, dst, dkey in ((2560, kcr, "kcr"), (2688, vcr, "vcr")) if B0PART >= 2 else ():
                            wt, wk = wBr.next()
                            load_w(wt[:], win_cols(l, c0, 128), wk)
                            for tb in range(4):
                                csl = slice(tb * 512, (tb + 1) * 512)
                                ps, pk = proj128(tb, wt, wk)
                                to, tk = tmpO.next()
                                op("vector", lambda e, ps=ps, to=to: e.tensor_copy(out=to[:], in_=ps[:]), reads=[pk], writes=[tk])
                                shift2(to, tk, dst[:, 0, csl], dst[:, 1, csl], f"{dkey}_{tb}")
                        for c0, dst, dkey, gi in ((2816, ksT, "ksT", 1), (3072, kwT, "kwT", 2)) if B0PART >= 3 else ():
                            wt, wk = wBr.next()
                            load_w(wt[:], win_cols(l, c0, 128), wk)
                            for tb in range(4):
                                csl = slice(tb * 512, (tb + 1) * 512)
                                ps, pk = proj128(tb, wt, wk)
                                to, tk = tmpO.next()
                                normrope(ps, pk, 512, 128, gk[:, gi:gi + 1], "gk", cosT[:, csl], sinT[:, csl], ["cosT", "sinT"], to[:], tk)
                                shift2(to, tk, dst[:, 0, csl], dst[:, 1, csl], f"{dkey}_{tb}")
                        for c in range(4 if B0PART >= 4 else 0):
                            n_ = 128 if c < 3 else 24
                            load_w(wv4[:, :, c * 128:c * 128 + n_], win_cols(l, 2944 + c * 128, n_), f"wv4_{c}")
                        for tt in range(16 if B0PART >= 4 else 0):
                            ps, pk = ringB_next()
                            mm_chain(ps[:, 0:408], [(hT[:, k, tt * 128:(tt + 1) * 128], wv4[:, k, 0:408]) for k in range(8)],
                                     pk, ["wv4_0", "wv4_1", "wv4_2", "wv4_3", "hT"])
                            op("vector", lambda e, ps=ps, tt=tt: e.tensor_copy(
                                out=vs_aug[:, tt, :, 0:64], in_=ps[:, 0:128].rearrange("p (g d) -> p g d", g=2)),
                               reads=[pk], writes=["vs_aug"], c=300)
                            op("vector", lambda e, ps=ps, tt=tt: e.tensor_copy(
                                out=vw_aug[:, tt, :, 0:64], in_=ps[:, 256:384].rearrange("p (g d) -> p g d", g=2)),
                               reads=[pk], writes=["vw_aug"], c=300)
                            op("scalar", lambda e, ps=ps, tt=tt: e.activation(out=gates[:, tt, :], in_=ps[:, 384:408], func=AF.Sigmoid),
                               reads=[pk], writes=["gates"], c=250)
                        for pr in range(4 if B0PART >= 5 else 0):
                            wt, wk = wBr.next()
                            load_w(wt[:], win_cols(l, 3352 + pr * 128, 128), wk)
                            for tb in range(4):
                                csl = slice(tb * 512, (tb + 1) * 512)
                                ps, pk = proj128(tb, wt, wk)
                                to, tk = tmpO.next()
                                op("scalar", lambda e, ps=ps, to=to: e.activation(out=to[:], in_=ps[:], func=AF.Silu),
                                   reads=[pk], writes=[tk])
                                op("sync", lambda e, to=to, pr=pr, csl=csl: e.dma_start(out=nzT_d[pr * 128:(pr + 1) * 128, csl], in_=to[:]),
                                   reads=[tk], writes=[f"nzT_d{pr}_{tb}"], dma=True)
                        sc.flush()

                    if STOPAT > 3:
                        with contextlib.ExitStack() as sB1:
                            w1_bf = sb("w1_bf", [64, 32, 256], BF16, sB1)
                            w1st = [sb(f"w1st{i}", [64, 4, 256], F32, sB1) for i in range(2)]
                            pe_sb = sb("pe_sb", [32, 64], F32, sB1)
                            peT = sb("peT", [64, 32], BF16, sB1)
                            b1_sb = sb("b1_sb", [128, 2], F32, sB1)
                            w2st = sb("w2st", [128, 2, 64], F32, sB1)
                            w2_bf = sb("w2_bf", [128, 2, 64], BF16, sB1)
                            cvec = sb("cvec", [128, 2], F32, sB1)
                            hid = sb("hid", [128, 2, 2, 127], BF16, sB1)
                            for kv in range(2):
                                raw = kcr if kv == 0 else vcr
                                rkey = "kcr" if kv == 0 else "vcr"
                                for c in range(8):
                                    i = c % 2
                                    src = cmp_w1[l, kv, c * 256:(c + 1) * 256, :].rearrange("(l d) h -> d l h", d=64)
                                    op("sync", lambda e, i=i, src=src: e.dma_start(out=w1st[i][:], in_=src),
                                       writes=[f"w1st{i}"], dma=True)
                                    op("gpsimd", lambda e, i=i, c=c: e.tensor_copy(out=w1_bf[:, c * 4:(c + 1) * 4, :], in_=w1st[i][:]),
                                       reads=[f"w1st{i}"], writes=["w1_bf"])
                                op("sync", lambda e, kv=kv: e.dma_start(out=pe_sb[:], in_=cmp_pe[l, kv]), writes=["pe_sb"], dma=True)
                                for hc in range(2):
                                    op("sync", lambda e, kv=kv, hc=hc: e.dma_start(
                                        out=b1_sb[:, hc:hc + 1],
                                        in_=cmp_b1[l, kv, hc * 128:(hc + 1) * 128].rearrange("(p o) -> p o", o=1)),
                                       writes=["b1_sb"], dma=True)
                                op("sync", lambda e, kv=kv: e.dma_start(
                                    out=w2st[:], in_=cmp_w2[l, kv].rearrange("(c p) d -> p c d", p=128)),
                                   writes=["w2st"], dma=True)
                                op("gpsimd", lambda e: e.tensor_copy(out=w2_bf[:], in_=w2st[:]), reads=["w2st"], writes=["w2_bf"])
                                ps, pk = ringB_next()
                                op("tensor", lambda e, ps=ps: e.transpose(out=ps[0:64, 0:32], in_=pe_sb[:], identity=ident_f[0:32, 0:32]),
                                   reads=["pe_sb"], writes=[pk])
                                op("vector", lambda e, ps=ps: e.tensor_copy(out=peT[:], in_=ps[0:64, 0:32]), reads=[pk], writes=["peT"])
                                for hc in range(2):
                                    ps, pk = ringB_next()
                                    mm_chain(ps[:, 0:1], [(w1_bf[:, li, hc * 128:(hc + 1) * 128], peT[:, li:li + 1]) for li in range(32)],
                                             pk, ["w1_bf", "peT"])
                                    op("vector", lambda e, ps=ps, hc=hc: e.tensor_tensor(out=cvec[:, hc:hc + 1], in0=ps[:, 0:1],
                                                                                         in1=b1_sb[:, hc:hc + 1], op=ALU.add),
                                       reads=[pk, "b1_sb"], writes=["cvec"])
                                for g in range(2):
                                    for hc in range(2):
                                        ps, pk = ringB_next()
                                        mm_chain(ps[:, 0:127], [(w1_bf[:, li, hc * 128:(hc + 1) * 128],
                                                                 raw[:, g, li:li + 16 * 126 + 1:16]) for li in range(32)],
                                                 pk, ["w1_bf", rkey])
                                        op("scalar", lambda e, ps=ps, hc=hc, g=g: e.activation(
                                            out=hid[:, hc, g, :], in_=ps[:, 0:127], func=AF.Silu, bias=cvec[:, hc:hc + 1]),
                                           reads=[pk, "cvec"], writes=["hid"])
                                    ps, pk = ringB_next()
                                    if kv == 0:
                                        mm_chain(ps[0:64, 0:127], [(w2_bf[:, hc, :], hid[:, hc, g, :]) for hc in range(2)],
                                                 pk, ["w2_bf", "hid"])
                                        normrope(ps, pk, 127, 64, gk[0:64, 0:1], "gk", cosC[:], sinC[:], ["cosC", "sinC"],
                                                 kcT[:, g, :], "kcT")
                                    else:
                                        mm_chain(ps[0:127, 0:64], [(hid[:, hc, g, :], w2_bf[:, hc, :]) for hc in range(2)],
                                                 pk, ["w2_bf", "hid"])
                                        op("vector", lambda e, ps=ps, g=g: e.tensor_copy(out=VO[:, g, 0:64], in_=ps[0:127, 0:64]),
                                           reads=[pk], writes=["VO"])
                            sc.flush()

                    with contextlib.ExitStack() as sB2:
                      if STOPAT > 4:
                        NP = 3
                        p_bf = [sb(f"p_bf{i}", [128, 512], BF16, sB2) for i in range(NP)]
                        pR = Ring(p_bf, "p_bf")
                        dn = [sb(f"dn{i}", [128, 3, 4], F32, sB2) for i in range(2)]
                        rdn = [sb(f"rdn{i}", [128, 3, 4], F32, sB2) for i in range(2)]
                        coef = [sb(f"coef{i}", [128, 3, 4], F32, sB2) for i in range(2)]
                        imp = [sb(f"imp{i}", [128, 32], F32, sB2) for i in range(2)]
                        top8 = [sb(f"top8{i}", [128, 8], F32, sB2) for i in range(2)]
                        sbt = [sb(f"sbt{i}", [128, 96], F32, sB2) for i in range(2)]
                        qa = [sb(f"qa{i}", [96, 4, 128], BF16, sB2) for i in range(2)]
                        oc_sb = [sb(f"oc_sb{i}", [128, 388], F32, sB2) for i in range(2)]
                        for i_ in range(2):
                            op("gpsimd", lambda e, i_=i_: e.memset(sbt[i_][:], 0.0), writes=[f"sbt{i_}"])
                        obs = [sb(f"obs{i}", [128, 4, 64], F32, sB2) for i in range(2)]
                        nz_t = [sb(f"nz_t{i}", [64, 4, 128], BF16, sB2) for i in range(2)]
                        yb_sb = [sb(f"yb_sb{i}", [64, 4, 128], BF16, sB2) for i in range(2)]
                        ocb, osb, owb = banks[0], banks[1], banks[2]
                        oTs, oTw = banks[3], banks[4]
                        oT_sb = [[sb(f"oT_sb{a}{i}", [65, 512], F32, sB2) for i in range(2)] for a in range(2)]
                        rb2 = [0]

                        def ringB_next():
                            j = 5 + (rb2[0] % 3)
                            rb2[0] += 1
                            return banks[j], f"bank{j}"

                        oc = ocb[:, 0:388].rearrange("p (r c) -> p r c", r=4)
                        os_ = osb[:, 0:260].rearrange("p (r c) -> p r c", r=4)
                        ow = owb[:, 0:260].rearrange("p (r c) -> p r c", r=4)
                        it = 0
                        for g in range(2):
                            for i in range(16):
                                u = it % 2
                                it += 1
                                tsl = slice(i * 128, (i + 1) * 128)
                                q_ap = nqT[:, 4 * g:4 * g + 4, tsl]
                                op("sync", lambda e, u=u, g=g, tsl=tsl: e.dma_start(out=nz_t[u][:], in_=nzT_d[256 * g:256 * (g + 1), tsl].rearrange("(r d) t -> d r t", d=64)),
                                   reads=[], writes=[f"nz_t{u}"], dma=True)
                                ps, pk = ringB_next()
                                op("tensor", lambda e, ps=ps, g=g, q_ap=q_ap: e.matmul(ps[0:127, :], lhsT=kcT[:, g, :], rhs=q_ap,
                                                                                       start=True, stop=False),
                                   reads=["kcT", "nqT"], writes=[pk])
                                op("tensor", lambda e, ps=ps, tsl=tsl: e.matmul(ps[0:127, :], lhsT=ident_bf[0:127, 0:127],
                                                                                rhs=bc4(cmask[:, tsl]), start=False, stop=True),
                                   reads=[], writes=[pk])
                                pb, pbk = pR.next()
                                op("scalar", lambda e, ps=ps, pb=pb: e.activation(out=pb[0:127, :], in_=ps[0:127, :], func=AF.Exp),
                                   reads=[pk], writes=[pbk])
                                for r in range(4):
                                    op("tensor", lambda e, pb=pb, r=r, g=g: e.matmul(oc[:, r, :], lhsT=pb[0:127, r * 128:(r + 1) * 128],
                                                                                     rhs=VO[:, g, :], start=True, stop=True),
                                       reads=[pbk, "VO"], writes=["bank0"])
                                op("scalar", lambda e, u=u: e.copy(out=oc_sb[u][:], in_=ocb[:, 0:388]), reads=["bank0"], writes=[f"oc_sb{u}"])
                                ocs = oc_sb[u][:].rearrange("p (r c) -> p r c", r=4)
                                op("vector", lambda e, u=u, ocs=ocs: e.tensor_scalar(out=dn[u][:, 0, :], in0=ocs[:, :, 64], scalar1=1e-30, scalar2=None,
                                                                            op0=ALU.max), reads=[f"oc_sb{u}"], writes=[f"dn{u}"])
                                op("vector", lambda e, u=u: e.reciprocal(out=rdn[u][:, 0, :], in_=dn[u][:, 0, :]),
                                   reads=[f"dn{u}"], writes=[f"rdn{u}"])
                                for r in range(4):
                                    in1 = bonus[:, i, :] if r == 0 else imp[u][:]
                                    op("vector", lambda e, u=u, r=r, in1=in1, ocs=ocs: e.scalar_tensor_tensor(
                                        out=imp[u][:], in0=ocs[:, r, 65:97], scalar=rdn[u][:, 0, r:r + 1], in1=in1,
                                        op0=ALU.mult, op1=ALU.add),
                                       reads=[f"oc_sb{u}", f"rdn{u}", f"imp{u}"], writes=[f"imp{u}"])
                                op("vector", lambda e, u=u: e.max(out=top8[u][:], in_=imp[u][:]), reads=[f"imp{u}"], writes=[f"top8{u}"])
                                op("vector", lambda e, u=u: e.tensor_scalar(out=sbt[u][:, 64:96], in0=imp[u][:], scalar1=top8[u][:, 7:8],
                                                                            scalar2=NEG, op0=ALU.is_lt, op1=ALU.mult),
                                   reads=[f"imp{u}", f"top8{u}"], writes=[f"sbt{u}"])
                                ps, pk = ringB_next()
                                op("tensor", lambda e, ps=ps, u=u: e.transpose(out=ps[0:96, 0:128], in_=sbt[u][:], identity=ident_f[:]),
                                   reads=[f"sbt{u}"], writes=[pk])
                                op("vector", lambda e, ps=ps, u=u: e.tensor_copy(out=qa[u][64:96, :, :], in_=bc4(ps[64:96, 0:128])),
                                   reads=[pk], writes=[f"qa{u}"])
                                op("gpsimd", lambda e, u=u, q_ap=q_ap: e.tensor_copy(out=qa[u][0:64, :, :], in_=q_ap),
                                   reads=["nqT", f"qa{u}"], writes=[f"qa{u}"])
                                for kt in range(i + 1):
                                    ksl = slice(kt * 128, (kt + 1) * 128)
                                    ps, pk = ringB_next()
                                    op("tensor", lambda e, ps=ps, g=g, ksl=ksl, u=u, kt=kt, i=i: e.matmul(
                                        ps[:], lhsT=ksTa[0:96, g, ksl], rhs=qa[u][:, :, :], start=True, stop=(kt != i)),
                                       reads=["ksT", f"qa{u}"], writes=[pk])
                                    if kt == i:
                                        op("tensor", lambda e, ps=ps: e.matmul(ps[:], lhsT=ident_bf[:], rhs=bc4(maskC[:]),
                                                                               start=False, stop=True),
                                           reads=[], writes=[pk])
                                    pb, pbk = pR.next()
                                    op("scalar", lambda e, ps=ps, pb=pb: e.activation(out=pb[:], in_=ps[:], func=AF.Exp),
                                       reads=[pk], writes=[pbk])
                                    op("tensor", lambda e, pb=pb, g=g, kt=kt, i=i: e.matmul(
                                        oTs[0:65, :], lhsT=vs_aug[:, kt, g, :], rhs=pb[:], start=(kt == 0), stop=(kt == i)),
                                       reads=[pbk, "vs_aug"], writes=["bank3"])
                                op("scalar", lambda e, u=u: e.copy(out=oT_sb[0][u][:], in_=oTs[0:65, :]),
                                   reads=["bank3"], writes=[f"oT_sb0{u}"])
                                for r in range(4):
                                    op("tensor", lambda e, u=u, r=r: e.transpose(out=os_[:, r, :], in_=oT_sb[0][u][:, r * 128:(r + 1) * 128],
                                                                                 identity=ident_f[0:65, 0:65]),
                                       reads=[f"oT_sb0{u}"], writes=["bank1"], c=300)
                                kts = [kt for kt in range(i - 4, i + 1) if kt >= 0]
                                for kt in kts:
                                    ksl = slice(kt * 128, (kt + 1) * 128)
                                    edge = (kt == i) or (kt == i - 4)
                                    ps, pk = ringB_next()
                                    op("tensor", lambda e, ps=ps, g=g, ksl=ksl, q_ap=q_ap, edge=edge: e.matmul(
                                        ps[:], lhsT=kwT[:, g, ksl], rhs=q_ap, start=True, stop=not edge),
                                       reads=["kwT", "nqT"], writes=[pk])
                                    if edge:
                                        mk_ = maskC if kt == i else maskW
                                        op("tensor", lambda e, ps=ps, mk_=mk_: e.matmul(ps[:], lhsT=ident_bf[:], rhs=bc4(mk_[:]),
                                                                                        start=False, stop=True),
                                           reads=[], writes=[pk])
                                    pb, pbk = pR.next()
                                    op("scalar", lambda e, ps=ps, pb=pb: e.activation(out=pb[:], in_=ps[:], func=AF.Exp),
                                       reads=[pk], writes=[pbk])
                                    op("tensor", lambda e, pb=pb, g=g, kt=kt, kts=kts: e.matmul(
                                        oTw[0:65, :], lhsT=vw_aug[:, kt, g, :], rhs=pb[:], start=(kt == kts[0]), stop=(kt == kts[-1])),
                                       reads=[pbk, "vw_aug"], writes=["bank4"])
                                op("scalar", lambda e, u=u: e.copy(out=oT_sb[1][u][:], in_=oTw[0:65, :]),
                                   reads=["bank4"], writes=[f"oT_sb1{u}"])
                                for r in range(4):
                                    op("tensor", lambda e, u=u, r=r: e.transpose(out=ow[:, r, :], in_=oT_sb[1][u][:, r * 128:(r + 1) * 128],
                                                                                 identity=ident_f[0:65, 0:65]),
                                       reads=[f"oT_sb1{u}"], writes=["bank2"], c=300)
                                op("vector", lambda e, u=u: e.tensor_copy(out=dn[u][:, 1, :], in_=os_[:, :, 64]),
                                   reads=["bank1"], writes=[f"dn{u}"])
                                op("vector", lambda e, u=u: e.tensor_copy(out=dn[u][:, 2, :], in_=ow[:, :, 64]),
                                   reads=["bank2"], writes=[f"dn{u}"])
                                op("vector", lambda e, u=u: e.reciprocal(out=rdn[u][:, 1:3, :], in_=dn[u][:, 1:3, :]),
                                   reads=[f"dn{u}"], writes=[f"rdn{u}"])
                                gview = gates[:, i, :].rearrange("p (b h) -> p b h", b=3)[:, :, 4 * g:4 * g + 4]
                                op("vector", lambda e, u=u, gview=gview: e.tensor_tensor(out=coef[u][:], in0=rdn[u][:], in1=gview, op=ALU.mult),
                                   reads=[f"rdn{u}", "gates"], writes=[f"coef{u}"])
                                for r in range(4):
                                    op("vector", lambda e, u=u, r=r, ocs=ocs: e.tensor_scalar(out=obs[u][:, r, :], in0=ocs[:, r, 0:64],
                                                                                     scalar1=coef[u][:, 0, r:r + 1], scalar2=None, op0=ALU.mult),
                                       reads=[f"oc_sb{u}", f"coef{u}"], writes=[f"obs{u}"])
                                    op("vector", lambda e, u=u, r=r: e.scalar_tensor_tensor(
                                        out=obs[u][:, r, :], in0=os_[:, r, 0:64], scalar=coef[u][:, 1, r:r + 1], in1=obs[u][:, r, :],
                                        op0=ALU.mult, op1=ALU.add),
                                       reads=["bank1", f"coef{u}", f"obs{u}"], writes=[f"obs{u}"])
                                    op("vector", lambda e, u=u, r=r: e.scalar_tensor_tensor(
                                        out=obs[u][:, r, :], in0=ow[:, r, 0:64], scalar=coef[u][:, 2, r:r + 1], in1=obs[u][:, r, :],
                                        op0=ALU.mult, op1=ALU.add),
                                       reads=["bank2", f"coef{u}", f"obs{u}"], writes=[f"obs{u}"])
                                ps, pk = ringB_next()
                                tp = ps[0:64, :].rearrange("p (r t) -> p r t", r=4)
                                for r in range(4):
                                    op("tensor", lambda e, tp=tp, u=u, r=r: e.transpose(out=tp[:, r, :], in_=obs[u][:, r, :], identity=ident_f[:]),
                                       reads=[f"obs{u}"], writes=[pk])
                                op("vector", lambda e, tp=tp, u=u: e.tensor_tensor(out=yb_sb[u][:], in0=tp, in1=nz_t[u][:], op=ALU.mult),
                                   reads=[pk, f"nz_t{u}"], writes=[f"yb_sb{u}"])
                                op("sync", lambda e, u=u, g=g, tsl=tsl: e.dma_start(out=yTb_d[256 * g:256 * (g + 1), tsl].rearrange("(r d) t -> d r t", d=64), in_=yb_sb[u][:]),
                                   reads=[f"yb_sb{u}"], writes=["yTb_d"], dma=True)
                        sc.flush()

                with contextlib.ExitStack() as sC:
                    wo = sb("wo", [128, 8, D], BF16, sC)
                    mT = sb("mT", [128, 8, S], BF16, sC)
                    ya_t = sb("ya_t", [128, 4, S], BF16, sC)
                    yb_t = sb("yb_t", [128, 4, S], BF16, sC)
                    wcu = [[sb(f"wcu{p}{i}", [128, 4, 128], BF16, sC) for i in range(2)] for p in range(2)]
                    wcg = [[sb(f"wcg{p}{i}", [128, 8, 128], BF16, sC) for i in range(2)] for p in range(2)]
                    sg = [sb(f"sg{i}", [128, 512], F32, sC) for i in range(2)]
                    m1 = [sb(f"m1{i}", [128, 512], F32, sC) for i in range(2)]
                    m2 = [sb(f"m2{i}", [128, 512], F32, sC) for i in range(2)]
                    xt = [sb(f"cxt{i}", [128, D], F32, sC) for i in range(2)]
                    ringC = Ring(banks, "bank")
                    yav = yTa_d.rearrange("(hp p) s -> p hp s", p=128)
                    ybv = yTb_d.rearrange("(hp p) s -> p hp s", p=128)
                    for tb in range(4):
                        bsl = slice(tb * 512, (tb + 1) * 512)
                        op("sync", lambda e, bsl=bsl: e.dma_start(out=ya_t[:, :, bsl], in_=yav[:, :, bsl]),
                           reads=["yTa_d"], writes=[f"ya_t{tb}"], dma=True)
                        op("sync", lambda e, bsl=bsl: e.dma_start(out=yb_t[:, :, bsl], in_=ybv[:, :, bsl]),
                           reads=["yTb_d"], writes=[f"yb_t{tb}"], dma=True)
                    sgc = 0
                    for dmc in range(8):
                        cs = slice(dmc * 128, (dmc + 1) * 128)
                        p = dmc % 2
                        load_w(wcu[p][0][:], w_up_a[l, :, cs].rearrange("(hp p) m -> p hp m", p=128), f"wcu{p}0")
                        load_w(wcg[p][0][:], win_cols(l, 3864 + dmc * 128, 128), f"wcg{p}0")
                        load_w(wcu[p][1][:], w_up_b[l, :, cs].rearrange("(hp p) m -> p hp m", p=128), f"wcu{p}1")
                        load_w(wcg[p][1][:], win_cols(l, 4888 + dmc * 128, 128), f"wcg{p}1")
                        if dmc >= 1:
                            c = dmc - 1
                            load_w(wo[:, :, c * 128:(c + 1) * 128], w_out[l, :, c * 128:(c + 1) * 128].rearrange("(k p) m -> p k m", p=128), "wo")
                        if dmc == 7:
                            load_w(wo[:, :, 7 * 128:8 * 128], w_out[l, :, 7 * 128:8 * 128].rearrange("(k p) m -> p k m", p=128), "wo")
                        for tb in range(4):
                            bsl = slice(tb * 512, (tb + 1) * 512)
                            j = sgc % 2
                            sgc += 1
                            for br, (yt, ytk, mm) in enumerate(((ya_t, f"ya_t{tb}", m1), (yb_t, f"yb_t{tb}", m2))):
                                pu, puk = ringC.next()
                                mm_chain(pu[:], [(wcu[p][br][:, hp, :], yt[:, hp, bsl]) for hp in range(4)], puk, [f"wcu{p}{br}", ytk])
                                pg, pgk = ringC.next()
                                mm_chain(pg[:], [(wcg[p][br][:, k, :], hT[:, k, bsl]) for k in range(8)], pgk, [f"wcg{p}{br}", "hT"])
                                op("scalar", lambda e, pg=pg, j=j: e.activation(out=sg[j][:], in_=pg[:], func=AF.Sigmoid),
                                   reads=[pgk], writes=[f"sg{j}"])
                                op("vector", lambda e, pu=pu, j=j, mm=mm: e.tensor_tensor(out=mm[j][:], in0=pu[:], in1=sg[j][:], op=ALU.mult),
                                   reads=[puk, f"sg{j}"], writes=[f"m{br + 1}{j}"])
                            op("gpsimd", lambda e, j=j, dmc=dmc, bsl=bsl: e.tensor_tensor(out=mT[:, dmc, bsl], in0=m1[j][:], in1=m2[j][:], op=ALU.add),
                               reads=[f"m1{j}", f"m2{j}"], writes=[f"mT{tb}"])
                    for tt in range(16):
                        tb = tt // 4
                        v = tt % 2
                        op("sync", lambda e, v=v, tt=tt: e.dma_start(out=xt[v][:], in_=x_src[b, tt * 128:(tt + 1) * 128, :]),
                           reads=[], writes=[f"cxt{v}"], dma=True)
                        for half in range(2):
                            hs = slice(half * 512, (half + 1) * 512)
                            po, pok = ringC.next()
                            mm_chain(po[:], [(mT[:, k, tt * 128:(tt + 1) * 128], wo[:, k, hs]) for k in range(8)],
                                     pok, [f"mT{tb}", "wo"])
                            op("vector", lambda e, po=po, v=v, hs=hs: e.tensor_tensor(out=xt[v][:, hs], in0=po[:], in1=xt[v][:, hs], op=ALU.add),
                               reads=[pok, f"cxt{v}"], writes=[f"cxt{v}"])
                        op("sync", lambda e, v=v, tt=tt: e.dma_start(out=out[b, tt * 128:(tt + 1) * 128, :], in_=xt[v][:]),
                           reads=[f"cxt{v}"], writes=[], dma=True)
                    sc.flush()
        print("total instructions (incl. waits):", sc.ninstr)
    return nc


_INVF = (1.0 / (2.0 * math.pi) * (10000.0 ** (-(np.arange(128) % 32) / 32.0))).astype(np.float32).reshape(128, 1)
_PROG = {}


def _get_prog(NL):
    if NL not in _PROG:
        _PROG[NL] = build_program(NL)
    return _PROG[NL]


WNAMES = ["norm_g", "w_in", "q_norm_g", "k_norm_g", "cmp_pe", "cmp_w1", "cmp_b1", "cmp_w2", "w_up_a", "w_up_b", "w_out"]


def kernel(x, positions, norm_g, w_in, q_norm_g, k_norm_g, cmp_pe, cmp_w1, cmp_b1, cmp_w2,
           w_up_a, w_up_b, w_out, _layers_per_launch=DEPTH):
    ws = dict(norm_g=norm_g, w_in=w_in, q_norm_g=q_norm_g, k_norm_g=k_norm_g, cmp_pe=cmp_pe, cmp_w1=cmp_w1,
              cmp_b1=cmp_b1, cmp_w2=cmp_w2, w_up_a=w_up_a, w_up_b=w_up_b, w_out=w_out)
    ws = {k: np.ascontiguousarray(np.asarray(v, dtype=np.float32)) for k, v in ws.items()}
    xcur = np.ascontiguousarray(np.asarray(x, dtype=np.float32))
    pos = np.ascontiguousarray(np.asarray(positions, dtype=np.int32))
    NL = _layers_per_launch
    nc = _get_prog(NL)
    for l0 in range(0, DEPTH, NL):
        in_maps = []
        for c in range(NCORES):
            m = {"x": xcur[c * NB:(c + 1) * NB], "positions": pos[c * NB:(c + 1) * NB], "invf": _INVF}
            for k, v in ws.items():
                m[k] = v[l0:l0 + NL]
            in_maps.append(m)
        res = run_bass_kernel_spmd(nc, in_maps, core_ids=list(range(NCORES)))
        xcur = np.concatenate([np.asarray(r["out"]) for r in res.results], axis=0)
    return xcur.astype(np.float32)
```

```python
import contextlib
import math
import numpy as np
import concourse.bass as bass
import concourse.mybir as mybir
from concourse.bass_utils import run_bass_kernel_spmd

F32 = mybir.dt.float32
BF16 = mybir.dt.bfloat16
I32 = mybir.dt.int32
AF = mybir.ActivationFunctionType
ALU = mybir.AluOpType
AX = mybir.AxisListType

S = 2048
D = 1024
NIN = 5912
NCORES = 8
NB = 2
DEPTH = 4
NEG = -30000.0
EPS = 1e-6

ENGS = ["tensor", "vector", "scalar", "gpsimd", "sync"]
EPOCH = 30000
NDMASEM = 12


DEFCOST = {"tensor": 560.0, "scalar": 600.0, "vector": 650.0, "gpsimd": 1200.0, "sync": 60.0}
WINDOW = 64


class Sched:
    def __init__(self, nc, stack):
        self.nc = nc
        self.stack = stack
        self.cnt = {e: 0 for e in ENGS}
        self.esem = {}
        for e in ENGS:
            if e != "sync":
                self.esem[e] = stack.enter_context(nc.semaphore(f"es_{e}_0"))
        self.eepoch = {e: 0 for e in ENGS}
        self.dsem = {}
        self.dcnt = {}
        self.dnext = {}
        for q in ["sync", "gpsimd", "scalar"]:
            self.dsem[q] = [stack.enter_context(nc.semaphore(f"ds_{q}_{i}")) for i in range(NDMASEM)]
            self.dcnt[q] = [0] * NDMASEM
            self.dnext[q] = 0
        self.ninstr = 0
        self.reorder = True
        self._reset()

    def _reset(self):
        self.nodes = []
        self.last_w = {}
        self.last_r = {}

    def op(self, eng, fn, reads=(), writes=(), dma=False, c=None):
        preds = set()
        for b in reads:
            w = self.last_w.get(b)
            if w is not None:
                preds.add(w)
            if b.startswith("bank"):
                for r_ in self.last_r.get(b, ()):
                    if self.nodes[r_][0] != eng:
                        preds.add(r_)
        for b in writes:
            w = self.last_w.get(b)
            if w is not None:
                preds.add(w)
            preds.update(self.last_r.get(b, ()))
        nid = len(self.nodes)
        if c is None:
            c = 3000.0 if dma else DEFCOST[eng]
        self.nodes.append((eng, fn, dma, preds, float(c)))
        for b in reads:
            self.last_r.setdefault(b, []).append(nid)
        for b in writes:
            self.last_w[b] = nid
            self.last_r[b] = []
        return nid

    def _simulate(self):
        import heapq
        nodes = self.nodes
        n = len(nodes)
        order = {e: [] for e in ENGS}
        if not self.reorder:
            for nid, nd in enumerate(nodes):
                order[nd[0]].append(nid)
            return order
        succ = [[] for _ in range(n)]
        indeg = [0] * n
        for nid, nd in enumerate(nodes):
            for p in nd[3]:
                succ[p].append(nid)
            indeg[nid] = len(nd[3])
        ready = {e: [] for e in ENGS}
        for nid, nd in enumerate(nodes):
            if indeg[nid] == 0:
                heapq.heappush(ready[nd[0]], nid)
        free_at = {e: 0.0 for e in ENGS}
        events = []
        now = 0.0
        left = n
        while left:
            progressed = False
            for e in ENGS:
                if free_at[e] <= now and ready[e]:
                    nid = heapq.heappop(ready[e])
                    nd = nodes[nid]
                    if nd[2]:
                        free_at[e] = now + 60.0
                        heapq.heappush(events, (free_at[e], -1))
                    else:
                        free_at[e] = now + nd[4]
                    heapq.heappush(events, (now + nd[4], nid))
                    order[e].append(nid)
                    left -= 1
                    progressed = True
            if not progressed:
                assert events, "scheduler deadlock"
                t, nid = heapq.heappop(events)
                now = max(now, t)
                while True:
                    if nid >= 0:
                        for sx in succ[nid]:
                            indeg[sx] -= 1
                            if indeg[sx] == 0:
                                heapq.heappush(ready[nodes[sx][0]], sx)
                    if events and events[0][0] <= now:
                        t, nid = heapq.heappop(events)
                    else:
                        break
        return order

    def flush(self):
        nc = self.nc
        nodes = self.nodes
        order = self._simulate()
        tok = [None] * len(nodes)
        extra = {}
        for e in ENGS:
            for nid in order[e]:
                if nodes[nid][2]:
                    i = self.dnext[e]
                    self.dnext[e] = (i + 1) % NDMASEM
                    sem = self.dsem[e][i]
                    if self.dcnt[e][i] > 0:
                        extra[nid] = (sem, self.dcnt[e][i])
                    self.dcnt[e][i] += 16
                    tok[nid] = (sem, self.dcnt[e][i], 16)
                else:
                    if self.cnt[e] >= EPOCH:
                        self.eepoch[e] += 1
                        self.esem[e] = self.stack.enter_context(nc.semaphore(f"es_{e}_{self.eepoch[e]}"))
                        self.cnt[e] = 0
                    self.cnt[e] += 1
                    tok[nid] = (self.esem[e], self.cnt[e], 1)
        prog = {}
        for e in ENGS:
            seen = {}
            lst = []
            for nid in order[e]:
                waits = {}
                cands = [tok[p][:2] for p in nodes[nid][3] if not (e == "tensor" and nodes[p][0] == "tensor")]
                if nid in extra:
                    cands.append(extra[nid])
                for (sm, v) in cands:
                    k = id(sm)
                    if seen.get(k, 0) >= v:
                        continue
                    if k not in waits or waits[k][1] < v:
                        waits[k] = (sm, v)
                wl = list(waits.values())
                for (sm, v) in wl:
                    seen[id(sm)] = v
                lst.append((wl, nodes[nid][1], tok[nid][0], tok[nid][2]))
                self.ninstr += 1 + len(wl)
            prog[e] = lst
        finals = {}
        for q in ["sync", "gpsimd", "scalar"]:
            finals[q] = [(self.dsem[q][i], self.dcnt[q][i]) for i in range(NDMASEM) if self.dcnt[q][i] > 0]
        with nc.Block() as block:
            def mk(ename):
                def body(eng):
                    for (wl, fn, sem, inc) in prog[ename]:
                        for (sm, v) in wl:
                            eng.wait_ge(sm, v)
                        fn(eng).then_inc(sem, inc)
                    for (sm, v) in finals.get(ename, []):
                        eng.wait_ge(sm, v)
                return body
            block.tensor(mk("tensor"))
            block.vector(mk("vector"))
            block.scalar(mk("scalar"))
            block.gpsimd(mk("gpsimd"))
            block.sync(mk("sync"))
        self._reset()


class Ring:
    def __init__(self, bufs, name):
        self.bufs = bufs
        self.name = name
        self.i = 0

    def next(self):
        j = self.i % len(self.bufs)
        self.i += 1
        return self.bufs[j], f"{self.name}{j}"


def bc4(ap2d, n=4):
    p, f = ap2d.shape
    return ap2d.unsqueeze(1).broadcast_to([p, n, f])


STOPAT = 99
B0PART = 99


def build_program(NL):
    nc = bass.Bass("TRN2", target_bir_lowering=False)
    dt_in = lambda name, shape, dt=F32: nc.dram_tensor(name, shape, dt, kind="ExternalInput").ap()
    x_in = dt_in("x", [NB, S, D])
    pos_in = dt_in("positions", [NB, S], I32)
    norm_g = dt_in("norm_g", [NL, D])
    w_in = dt_in("w_in", [NL, D, NIN])
    q_norm_g = dt_in("q_norm_g", [NL, 64])
    k_norm_g = dt_in("k_norm_g", [NL, 3, 64])
    cmp_pe = dt_in("cmp_pe", [NL, 2, 32, 64])
    cmp_w1 = dt_in("cmp_w1", [NL, 2, 2048, 256])
    cmp_b1 = dt_in("cmp_b1", [NL, 2, 256])
    cmp_w2 = dt_in("cmp_w2", [NL, 2, 256, 64])
    w_up_a = dt_in("w_up_a", [NL, 512, D])
    w_up_b = dt_in("w_up_b", [NL, 512, D])
    w_out = dt_in("w_out", [NL, D, D])
    invf_in = dt_in("invf", [128, 1])
    out = nc.dram_tensor("out", [NB, S, D], F32, kind="ExternalOutput").ap()
    yTa_d = nc.dram_tensor("yTa_scr", [512, S], BF16).ap()
    yTb_d = nc.dram_tensor("yTb_scr", [512, S], BF16).ap()
    nzT_d = nc.dram_tensor("nzT_scr", [512, S], BF16).ap()

    with contextlib.ExitStack() as st:
        sc = Sched(nc, st)
        E = st.enter_context
        op = sc.op

        uid = [0]

        def sb(name, shape, dt, stack=None):
            uid[0] += 1
            return (stack if stack is not None else st).enter_context(nc.sbuf_tensor(f"{name}_{uid[0]}", shape, dt))

        ident_bf = sb("ident_bf", [128, 128], BF16)
        ident_f = sb("ident_f", [128, 128], F32)
        negtri = sb("negtri", [128, 128], BF16)
        negones = sb("negones", [128, 128], BF16)
        ones_blk = sb("ones_blk", [128, 128], BF16)
        rotM_blk = sb("rotM_blk", [128, 128], BF16)
        maskS = sb("maskS", [128, 128], BF16)
        maskC = sb("maskC", [128, 128], BF16)
        maskW = sb("maskW", [128, 128], BF16)
        Eexp = sb("Eexp", [32, S], BF16)
        cmask = sb("cmask", [127, S], BF16)
        bonus = sb("bonus", [128, 16, 32], F32)
        bonusF = sb("bonusF", [128, 16, 32], F32)
        invf = sb("invf_sb", [128, 1], F32)
        hT = sb("hT", [128, 8, S], BF16)
        cosT = sb("cosT", [128, S], F32)
        sinT = sb("sinT", [128, S], F32)
        cosC = sb("cosC", [64, 127], F32)
        sinC = sb("sinC", [64, 127], F32)
        VO = sb("VO", [127, 2, 97], BF16)
        vs_aug = sb("vs_aug", [128, 16, 2, 65], BF16)
        vw_aug = sb("vw_aug", [128, 16, 2, 65], BF16)
        kcT = sb("kcT", [64, 2, 127], BF16)
        ksTa = sb("ksTa", [96, 2, S], BF16)
        gq = sb("gq", [128, 1], F32)
        gk = sb("gk", [128, 3], F32)
        wst = Ring([sb(f"wst{i}", [128, 8, 128], F32) for i in range(2)], "wst")
        banks = [E(nc.psum_tensor(f"bank{i}", [128, 512], F32)) for i in range(8)]


        def sel(t, ap, pattern, cop, fill, base, cm, key):
            op("gpsimd", lambda e: e.affine_select(out=ap, in_=ap, pattern=pattern, compare_op=cop,
                                                   fill=fill, base=base, channel_multiplier=cm),
               reads=[key], writes=[key])

        def mset(ap, val, key):
            op("gpsimd", lambda e: e.memset(ap, val), writes=[key])

        mset(ident_bf[:], 0.0, "ident_bf")
        sel(ident_bf, ident_bf[:], [[-1, 128]], ALU.not_equal, 1.0, 0, 1, "ident_bf")
        mset(ident_f[:], 0.0, "ident_f")
        sel(ident_f, ident_f[:], [[-1, 128]], ALU.not_equal, 1.0, 0, 1, "ident_f")
        mset(negtri[:], -1.0, "negtri")
        sel(negtri, negtri[:], [[-1, 128]], ALU.is_ge, 0.0, 0, 1, "negtri")
        mset(negones[:], -1.0, "negones")
        mset(ones_blk[:], 1.0, "ones_blk")
        mset(ones_blk[0:64, 64:128], 0.0, "ones_blk")
        mset(ones_blk[64:128, 0:64], 0.0, "ones_blk")
        mset(rotM_blk[:], 0.0, "rotM_blk")
        for q0 in (0, 64):
            sel(rotM_blk, rotM_blk[q0:q0 + 64, q0:q0 + 64], [[-1, 64]], ALU.not_equal, -1.0, -32, 1, "rotM_blk")
            sel(rotM_blk, rotM_blk[q0:q0 + 64, q0:q0 + 64], [[-1, 64]], ALU.not_equal, 1.0, 32, 1, "rotM_blk")
        mset(maskS[:], 0.0, "maskS")
        sel(maskS, maskS[:], [[1, 128]], ALU.is_ge, NEG, -1, -1, "maskS")
        mset(maskC[:], 0.0, "maskC")
        sel(maskC, maskC[:], [[1, 128]], ALU.is_ge, NEG, 0, -1, "maskC")
        mset(maskW[:], 0.0, "maskW")
        sel(maskW, maskW[:], [[-1, 128]], ALU.is_ge, NEG, -1, 1, "maskW")
        mset(Eexp[:], 1.0, "Eexp")
        sel(Eexp, Eexp[:], [[1, S]], ALU.is_ge, 0.0, 0, -64, "Eexp")
        sel(Eexp, Eexp[:], [[-1, S]], ALU.is_ge, 0.0, 63, 64, "Eexp")
        for g_ in range(2):
            mset(ksTa[64:96, g_, :], 1.0, "ksTa")
            sel(ksTa, ksTa[64:96, g_, :], [[1, S]], ALU.is_ge, 0.0, 0, -64, "ksTa")
            sel(ksTa, ksTa[64:96, g_, :], [[-1, S]], ALU.is_ge, 0.0, 63, 64, "ksTa")
        mset(cmask[:], 0.0, "cmask")
        sel(cmask, cmask[:], [[1, S]], ALU.is_ge, NEG, -31, -16, "cmask")
        mset(VO[:], 1.0, "VO")
        sel(VO, VO[:, :, 65:97], [[0, 2], [64, 32]], ALU.is_ge, 0.0, 63, -16, "VO")
        sel(VO, VO[:, :, 65:97], [[0, 2], [-64, 32]], ALU.is_ge, 0.0, 31, 16, "VO")
        mset(vs_aug[:], 1.0, "vs_aug")
        mset(vw_aug[:], 1.0, "vw_aug")
        mset(bonus[:], 0.0, "bonus")
        sel(bonus, bonus[:], [[128, 16], [-64, 32]], ALU.is_ge, -1e30, 0, 1, "bonus")
        mset(bonusF[:], 1e4, "bonusF")
        sel(bonusF, bonusF[:], [[128, 16], [-64, 32]], ALU.is_ge, 0.0, 0, 1, "bonusF")
        sel(bonusF, bonusF[:], [[-128, 16], [64, 32]], ALU.is_ge, 0.0, 127, -1, "bonusF")
        op("gpsimd", lambda e: e.tensor_tensor(out=bonus[:], in0=bonus[:], in1=bonusF[:], op=ALU.add),
           reads=["bonus", "bonusF"], writes=["bonus"])
        mset(bonus[:, :, 0:1], 1e4, "bonus")
        op("sync", lambda e: e.dma_start(out=invf[:], in_=invf_in[:, :]), writes=["invf"], dma=True)
        sc.flush()

        def load_w(dst_ap, src_ap, key, np_=128):
            stg, skey = wst.next()
            a, b = src_ap.shape[1], src_ap.shape[2]
            sv = stg[0:np_, 0:a, 0:b]
            op("sync", lambda e: e.dma_start(out=sv, in_=src_ap), writes=[skey], dma=True)
            op("gpsimd", lambda e: e.tensor_copy(out=dst_ap, in_=sv), reads=[skey], writes=[key])

        def win_cols(l, c0, n):
            return w_in[l, :, c0:c0 + n].rearrange("(k p) m -> p k m", p=128)

        def mm_chain(out_ap, pairs, okey, rkeys):
            n = len(pairs)
            for j, (lh, rh) in enumerate(pairs):
                op("tensor", lambda e, lh=lh, rh=rh, j=j: e.matmul(out_ap, lhsT=lh, rhs=rh,
                                                                    start=(j == 0), stop=(j == n - 1)),
                   reads=rkeys, writes=[okey])

        for b in range(NB):
            with contextlib.ExitStack() as ss_:
                posi = sb("posi", [128, S], I32, ss_)
                posf = sb("posf", [128, S], F32, ss_)
                vv = sb("vv", [128, S], F32, ss_)
                uu = sb("uu", [128, S], F32, ss_)
                ui = sb("ui", [128, S], I32, ss_)
                uf = sb("uf", [128, S], F32, ss_)
                gg = sb("gg", [128, S], F32, ss_)
                mm_ = sb("mm_", [128, S], F32, ss_)
                posm = sb("posm", [128, 127], F32, ss_)
                vm = sb("vm", [128, 127], F32, ss_)
                op("sync", lambda e: e.dma_start(out=posi[:], in_=pos_in[b].partition_broadcast(128)),
                   writes=["posi"], dma=True)
                op("vector", lambda e: e.tensor_copy(out=posf[:], in_=posi[:]), reads=["posi"], writes=["posf"])
                op("vector", lambda e: e.tensor_scalar(out=vv[:], in0=posf[:], scalar1=invf[:, 0:1], scalar2=None,
                                                       op0=ALU.mult), reads=["posf", "invf"], writes=["vv"])
                win = bass.AP(posf[:].tensor, posf[:].offset, [[S, 128], [16, 127], [1, 32]])
                op("vector", lambda e: e.reduce_sum(out=posm[:], in_=win, axis=AX.X), reads=["posf"], writes=["posm"])
                op("vector", lambda e: e.tensor_scalar(out=vm[:], in0=posm[:], scalar1=invf[:, 0:1], scalar2=1.0 / 32,
                                                       op0=ALU.mult, op1=ALU.mult),
                   reads=["posm", "invf"], writes=["vm"])

                def table(vsrc, n, add, dst, dkey, skey, P=128):
                    u = uu[0:P, 0:n]
                    op("vector", lambda e: e.tensor_scalar(out=u, in0=vsrc, scalar1=float(add), scalar2=None,
                                                           op0=ALU.add), reads=[skey], writes=["uu"])
                    op("vector", lambda e: e.tensor_copy(out=ui[0:P, 0:n], in_=u), reads=["uu"], writes=["ui"])
                    op("vector", lambda e: e.tensor_copy(out=uf[0:P, 0:n], in_=ui[0:P, 0:n]), reads=["ui"], writes=["uf"])
                    op("vector", lambda e: e.tensor_tensor(out=gg[0:P, 0:n], in0=u, in1=uf[0:P, 0:n], op=ALU.subtract),
                       reads=["uu", "uf"], writes=["gg"])
                    op("vector", lambda e: e.scalar_tensor_tensor(out=mm_[0:P, 0:n], in0=gg[0:P, 0:n], scalar=0.5,
                                                                  in1=gg[0:P, 0:n], op0=ALU.is_gt, op1=ALU.subtract),
                       reads=["gg"], writes=["mm_"])
                    op("scalar", lambda e: e.activation(out=dst, in_=mm_[0:P, 0:n], func=AF.Sin,
                                                        scale=-2.0 * math.pi), reads=["mm_"], writes=[dkey])

                table(vv[:], S, 0.0, sinT[:], "sinT", "vv")
                table(vv[:], S, 0.25, cosT[:], "cosT", "vv")
                table(vm[0:64, :], 127, 0.0, sinC[:], "sinC", "vm", P=64)
                table(vm[0:64, :], 127, 0.25, cosC[:], "cosC", "vm", P=64)
                sc.flush()

            for l in range(NL):
                x_src = x_in if l == 0 else out
                with contextlib.ExitStack() as s1:
                    gqr = sb("gqr", [128, 1], F32, s1)
                    gbc = sb("gbc", [128, D], F32, s1)
                    xt = [sb(f"p1xt{i}", [128, D], F32, s1) for i in range(2)]
                    junk = sb("p1junk", [128, D], BF16, s1)
                    xs = [sb(f"p1xs{i}", [128, D], BF16, s1) for i in range(2)]
                    ssq = [sb(f"p1ss{i}", [128, 1], F32, s1) for i in range(2)]
                    rt_ = [sb(f"p1rt{i}", [128, 1], F32, s1) for i in range(2)]
                    rs_ = [sb(f"p1rs{i}", [128, 1], F32, s1) for i in range(2)]
                    for q0 in (0, 64):
                        op("sync", lambda e, q0=q0: e.dma_start(out=gqr[q0:q0 + 64, :], in_=q_norm_g[l].rearrange("(p o) -> p o", o=1)),
                           writes=["gqr"], dma=True)
                    op("vector", lambda e: e.tensor_scalar(out=gq[:], in0=gqr[:], scalar1=0.125, scalar2=None,
                                                           op0=ALU.mult), reads=["gqr"], writes=["gq"])
                    for j in range(3):
                        for q0 in (0, 64):
                            op("sync", lambda e, j=j, q0=q0: e.dma_start(out=gk[q0:q0 + 64, j:j + 1],
                                                                         in_=k_norm_g[l, j].rearrange("(p o) -> p o", o=1)),
                               writes=["gk"], dma=True)
                    op("sync", lambda e: e.dma_start(out=gbc[:], in_=norm_g[l].partition_broadcast(128)),
                       writes=["gbc"], dma=True)
                    for tt in range(16):
                        i = tt % 2
                        pT = banks[i][:].bitcast(BF16)
                        op("sync", lambda e, tt=tt, i=i: e.dma_start(out=xt[i][:], in_=x_src[b, tt * 128:(tt + 1) * 128, :]),
                           writes=[f"xt{i}"], dma=True)
                        op("scalar", lambda e, i=i: e.activation(out=junk[:], in_=xt[i][:], func=AF.Square,
                                                                 accum_out=ssq[i][:]),
                           reads=[f"xt{i}"], writes=["junk", f"ss{i}"])
                        op("scalar", lambda e, i=i: e.activation(out=rt_[i][:], in_=ssq[i][:], func=AF.Sqrt,
                                                                 scale=1.0 / D, bias=EPS),
                           reads=[f"ss{i}"], writes=[f"rt{i}"])
                        op("vector", lambda e, i=i: e.reciprocal(out=rs_[i][:], in_=rt_[i][:]),
                           reads=[f"rt{i}"], writes=[f"rs{i}"])
                        op("vector", lambda e, i=i: e.scalar_tensor_tensor(out=xs[i][:], in0=xt[i][:], scalar=rs_[i][:],
                                                                           in1=gbc[:], op0=ALU.mult, op1=ALU.mult),
                           reads=[f"xt{i}", f"rs{i}", "gbc"], writes=[f"xs{i}"])
                        for k in range(8):
                            op("tensor", lambda e, i=i, k=k, pT=pT: e.transpose(out=pT[:, k * 128:(k + 1) * 128],
                                                                                in_=xs[i][:, k * 128:(k + 1) * 128],
                                                                                identity=ident_bf[:]),
                               reads=[f"xs{i}"], writes=[f"bank{i}"])
                        op("scalar", lambda e, i=i, tt=tt, pT=pT: e.copy(out=hT[:, :, tt * 128:(tt + 1) * 128],
                                                                         in_=pT.rearrange("p (k t) -> p k t", k=8)),
                           reads=[f"bank{i}"], writes=["hT"])
                    sc.flush()

                if STOPAT <= 1:
                    continue
                with contextlib.ExitStack() as sa:
                    XP = [[sb(f"XP{p}{i}", [128, S], BF16, sa) for i in range(3)] for p in range(2)]
                    XO = [[sb(f"XO{p}{i}", [64, S], BF16, sa) for i in range(3)] for p in range(2)]
                    v_all = sb("v_all", [128, 16, 512], BF16, sa)
                    wv_all = sb("wv_all", [128, 8, 512], BF16, sa)
                    wA2 = [[sb(f"wA{p}{i}", [128, 8, 128], BF16, sa) for i in range(3)] for p in range(2)]
                    e_sb = [sb(f"e_sb{i}", [128, 512], F32, sa) for i in range(2)]
                    sp_bf = [[sb(f"sp_bf{s}{i}", [128, 512], BF16, sa) for i in range(2)] for s in range(2)]
                    R32 = [sb(f"R32{s}", [128, 512], F32, sa) for s in range(2)]
                    Rbf = [[sb(f"Rbf{s}{i}", [128, 512], BF16, sa) for i in range(2)] for s in range(2)]
                    w_bf = [[sb(f"w_bf{s}{i}", [128, 512], BF16, sa) for i in range(2)] for s in range(2)]
                    ya_sb = [sb(f"ya_sb{i}", [64, 512], BF16, sa) for i in range(2)]
                    ringA = Ring(banks[6:8], "bank6")

                    def ringA_next():
                        j = ringA.i % 2
                        ringA.i += 1
                        return banks[6 + j], f"bank{6 + j}"

                    for c in range(4):
                        load_w(wv_all[:, :, c * 128:(c + 1) * 128], win_cols(l, 1024 + c * 128, 128), f"wv_all{c}")
                    for tt in range(16):
                        ps, pk = ringA_next()
                        mm_chain(ps[:], [(hT[:, k, tt * 128:(tt + 1) * 128], wv_all[:, k, :]) for k in range(8)],
                                 pk, ["wv_all0", "wv_all1", "wv_all2", "wv_all3", "hT"])
                        if tt % 2 == 0:
                            op("vector", lambda e, ps=ps, tt=tt: e.tensor_copy(out=v_all[:, tt, :], in_=ps[:]),
                               reads=[pk], writes=["v_all"])
                        else:
                            op("scalar", lambda e, ps=ps, tt=tt: e.copy(out=v_all[:, tt, :], in_=ps[:]),
                               reads=[pk], writes=["v_all"])

                    for hp in range(4):
                        pp = hp % 2
                        wA = wA2[pp]
                        PK = f"p{pp}"
                        qT = [XP[pp][0], XO[pp][0]]
                        kT = [XP[pp][1], XO[pp][1]]
                        szT = [XP[pp][2], XO[pp][2]]
                        for wi, sec in enumerate((0, 1, 3)):
                            load_w(wA[wi][:], win_cols(l, sec * 512 + hp * 128, 128), f"wA{wi}" + PK)
                        for wi in range(3):
                            for tb in range(4):
                                ps, pk = ringA_next()
                                csl = slice(tb * 512, (tb + 1) * 512)
                                mm_chain(ps[:], [(wA[wi][:, k, :], hT[:, k, csl]) for k in range(8)], pk, [f"wA{wi}" + PK, "hT"])
                                d_ap = XP[pp][wi][:, csl]
                                ek = f"XP{wi}t{tb}" + PK
                                if wi == 0:
                                    op("scalar", lambda e, ps=ps, d_ap=d_ap: e.activation(out=d_ap, in_=ps[:], func=AF.Copy, scale=0.125),
                                       reads=[pk], writes=[ek])
                                elif wi == 1:
                                    op("vector", lambda e, ps=ps, d_ap=d_ap: e.tensor_copy(out=d_ap, in_=ps[:]),
                                       reads=[pk], writes=[ek])
                                else:
                                    op("scalar", lambda e, ps=ps, d_ap=d_ap: e.activation(out=d_ap, in_=ps[:], func=AF.Silu),
                                       reads=[pk], writes=[ek])
                                op("sync", lambda e, pp=pp, wi=wi, csl=csl: e.dma_start(out=XO[pp][wi][:, csl], in_=XP[pp][wi][64:128, csl]),
                                   reads=[ek], writes=[f"XO{wi}t{tb}" + PK], dma=True, c=2500)

                        def xkeys(wi, s, qb=None):
                            nm = "XP" if s == 0 else "XO"
                            if qb is None:
                                return [f"{nm}{wi}t{t}" + PK for t in range(4)]
                            return [f"{nm}{wi}t{qb}" + PK]

                        tiles = []
                        for qb in range(4):
                            nt = 4 * qb + 4
                            for kt in range(nt - 1, -1, -1):
                                tiles.append((qb, kt, kt == nt - 1, kt == 0))

                        def c0_of(qb, kt):
                            return max(kt - 4 * qb, 0) * 128

                        def emit_qk(n, s, kT=kT, qT=qT, xkeys=xkeys):
                            qb, kt, first, last = tiles[n]
                            zb = banks[2 * s + (n % 2)]
                            zk = f"bank{2 * s + (n % 2)}"
                            diag = kt >= 4 * qb
                            c0 = c0_of(qb, kt)
                            op("tensor", lambda e: e.matmul(zb[:, c0:512], lhsT=kT[s][0:64, kt * 128:(kt + 1) * 128],
                                                            rhs=qT[s][0:64, qb * 512 + c0:(qb + 1) * 512], start=True, stop=True),
                               reads=xkeys(1, s) + xkeys(0, s, qb), writes=[zk], c=100 + 0.75 * (512 - c0))
                            if diag:
                                op("tensor", lambda e: e.matmul(zb[:, c0:c0 + 128], lhsT=ident_bf[:], rhs=maskS[:],
                                                                start=False, stop=True, skip_group_check=True),
                                   reads=[], writes=[zk], c=200)

                        for s in range(2):
                            emit_qk(0, s)
                        for n in range(len(tiles)):
                            qb, kt, first, last = tiles[n]
                            par = n % 2
                            c0 = c0_of(qb, kt)
                            diag = kt >= 4 * qb
                            c1 = (kt - 4 * qb + 1) * 128 if diag else 0
                            w = 512 - c0
                            if first:
                                for s in range(2):
                                    op("gpsimd", lambda e, s=s: e.memset(R32[s][:], 0.0), writes=[f"R32{s}"], c=600)
                            for s in range(2):
                                zb = banks[2 * s + par]
                                zk = f"bank{2 * s + par}"
                                op("scalar", lambda e, zb=zb, s=s, c0=c0: e.activation(out=e_sb[s][:, c0:512], in_=zb[:, c0:512], func=AF.Exp),
                                   reads=[zk], writes=[f"e_sb{s}"], c=220 + 0.72 * w)
                                op("scalar", lambda e, s=s, par=par, c0=c0: e.activation(out=sp_bf[s][par][:, c0:512], in_=e_sb[s][:, c0:512],
                                                                                         func=AF.Ln, bias=1.0),
                                   reads=[f"e_sb{s}"], writes=[f"sp_bf{s}{par}"], c=250 + 0.75 * w)
                            for s in range(2):
                                zb = banks[2 * s + par]
                                zk = f"bank{2 * s + par}"
                                op("tensor", lambda e, zb=zb, s=s, par=par, first=first, c0=c0: e.matmul(
                                    zb[:, c0:512], lhsT=negtri[:], rhs=sp_bf[s][par][:, c0:512], start=False, stop=first, skip_group_check=True),
                                   reads=[f"sp_bf{s}{par}"], writes=[zk], c=100 + 0.75 * w)
                                if not first:
                                    op("tensor", lambda e, zb=zb, s=s, par=par, c1=c1: e.matmul(
                                        zb[:, c1:512], lhsT=negones[:], rhs=Rbf[s][par][:, c1:512], start=False, stop=True, skip_group_check=True),
                                       reads=[f"Rbf{s}{par}"], writes=[zk], c=100 + 0.75 * (512 - c1))
                                if not last:
                                    op("gpsimd", lambda e, s=s, par=par, c0=c0: e.tensor_tensor(out=R32[s][:, c0:512], in0=R32[s][:, c0:512],
                                                                                                in1=sp_bf[s][par][:, c0:512], op=ALU.add),
                                       reads=[f"sp_bf{s}{par}", f"R32{s}"], writes=[f"R32{s}"], c=200 + 2.0 * w)
                                    op("vector", lambda e, s=s, par=par, c0=c0: e.tensor_copy(out=Rbf[s][1 - par][:, c0:512], in_=R32[s][:, c0:512]),
                                       reads=[f"R32{s}"], writes=[f"Rbf{s}{1 - par}"], c=100 + 1.1 * w)
                            if n + 1 < len(tiles):
                                for s in range(2):
                                    emit_qk(n + 1, s)
                            for s in range(2):
                                zb = banks[2 * s + par]
                                zk = f"bank{2 * s + par}"
                                op("scalar", lambda e, zb=zb, s=s, par=par, c0=c0: e.activation(out=w_bf[s][par][:, c0:512], in_=zb[:, c0:512], func=AF.Exp),
                                   reads=[zk], writes=[f"w_bf{s}{par}"], c=220 + 0.72 * w)
                            for s in range(2):
                                ob = banks[4 + s]
                                h = 2 * hp + s
                                op("tensor", lambda e, ob=ob, s=s, par=par, kt=kt, first=first, last=last, h=h, c0=c0: e.matmul(
                                    ob[0:64, c0:512], lhsT=v_all[:, kt, h * 64:(h + 1) * 64], rhs=w_bf[s][par][:, c0:512],
                                    start=first, stop=last, skip_group_check=True),
                                   reads=[f"w_bf{s}{par}", "v_all"], writes=[f"bank{4 + s}"], c=100 + 0.75 * w)
                                if last:
                                    op("vector", lambda e, ob=ob, s=s, qb=qb, szT=szT: e.tensor_tensor(
                                        out=ya_sb[s][:], in0=ob[0:64, :], in1=szT[s][0:64, qb * 512:(qb + 1) * 512], op=ALU.mult),
                                       reads=[f"bank{4 + s}"] + xkeys(2, s, qb), writes=[f"ya_sb{s}"])
                                    op("sync", lambda e, s=s, h=h, qb=qb: e.dma_start(out=yTa_d[h * 64:(h + 1) * 64, qb * 512:(qb + 1) * 512], in_=ya_sb[s][:]),
                                       reads=[f"ya_sb{s}"], writes=["yTa_d"], dma=True)
                    sc.flush()

                if STOPAT <= 2:
                    continue
                with contextlib.ExitStack() as sB:
                    nqT = sb("nqT", [64, 8, S], BF16, sB)
                    ksT = ksTa[0:64]
                    kwT = sb("kwT", [64, 2, S], BF16, sB)
                    gates = sb("gates", [128, 16, 24], F32, sB)
                    kcr = sb("kcr", [64, 2, S], BF16, sB)
                    vcr = sb("vcr", [64, 2, S], BF16, sB)
                    ringB = Ring(banks[3:8], "bank")

                    def ringB_next():
                        j = 3 + (ringB.i % 5)
                        ringB.i += 1
                        return banks[j], f"bank{j}"

                    nr = {nm: [sb(f"nr_{nm}{i}", [128, 512], dt_, sB) for i in range(2)]
                          for nm, dt_ in (("sq", BF16), ("rt", F32), ("qn", BF16), ("t1", F32), ("t2", F32))}
                    nrc = [0]

                    def normrope(ps, pk, n, P, gcol, gkey, cos_ap, sin_ap, tkeys, out_ap, okey):
                        i = nrc[0] % 2
                        nrc[0] += 1
                        sq, rt, qn, t1, t2 = (nr[nm][i][0:P, 0:n] for nm in ("sq", "rt", "qn", "t1", "t2"))
                        rs = rt
                        kk = lambda nm: f"nr_{nm}{i}"
                        raw = ps[0:P, 0:n]
                        op("scalar", lambda e: e.activation(out=sq, in_=raw, func=AF.Square), reads=[pk], writes=[kk("sq")])
                        p2, p2k = ringB_next()
                        op("tensor", lambda e: e.matmul(p2[0:P, 0:n], lhsT=ones_blk[0:P, 0:P], rhs=sq, start=True, stop=True),
                           reads=[kk("sq")], writes=[p2k])
                        op("scalar", lambda e: e.activation(out=rt, in_=p2[0:P, 0:n], func=AF.Sqrt, scale=1.0 / 64, bias=EPS),
                           reads=[p2k], writes=[kk("rt")])
                        op("vector", lambda e: e.reciprocal(out=rs, in_=rt), reads=[kk("rt")], writes=[kk("rt")], c=3300)
                        op("vector", lambda e: e.scalar_tensor_tensor(out=qn, in0=raw, scalar=gcol, in1=rs,
                                                                      op0=ALU.mult, op1=ALU.mult),
                           reads=[pk, kk("rt"), gkey], writes=[kk("qn")])
                        p3, p3k = ringB_next()
                        op("tensor", lambda e: e.matmul(p3[0:P, 0:n], lhsT=rotM_blk[0:P, 0:P], rhs=qn, start=True, stop=True),
                           reads=[kk("qn")], writes=[p3k])
                        op("gpsimd", lambda e: e.tensor_tensor(out=t1, in0=qn, in1=cos_ap, op=ALU.mult),
                           reads=[kk("qn")] + tkeys, writes=[kk("t1")])
                        op("vector", lambda e: e.tensor_tensor(out=t2, in0=p3[0:P, 0:n], in1=sin_ap, op=ALU.mult),
                           reads=[p3k] + tkeys, writes=[kk("t2")])
                        op("gpsimd", lambda e: e.tensor_tensor(out=out_ap, in0=t1, in1=t2, op=ALU.add),
                           reads=[kk("t1"), kk("t2")], writes=[okey])

                    with contextlib.ExitStack() as sB0:
                        wB = [sb(f"wB{i}", [128, 8, 128], BF16, sB0) for i in range(3)]
                        wBr = Ring(wB, "wB")
                        wv4 = sb("wv4", [128, 8, 512], BF16, sB0)
                        tmpO = Ring([sb(f"tmpO{i}", [128, 512], BF16, sB0) for i in range(3)], "tmpO")

                        def proj128(tb, wt, wk):
                            ps, pk = ringB_next()
                            mm_chain(ps[:], [(wt[:, k, :], hT[:, k, tb * 512:(tb + 1) * 512]) for k in range(8)], pk, [wk, "hT"])
                            return ps, pk

                        def shift2(to, tk, dst0, dst1, kbase):
                            op("sync", lambda e: e.dma_start(out=dst0, in_=to[0:64, :]), reads=[tk], writes=[kbase + "a"], dma=True, c=2500)
                            op("sync", lambda e: e.dma_start(out=dst1, in_=to[64:128, :]), reads=[tk], writes=[kbase + "b"], dma=True, c=2500)

                        for pr in range(4):
                            wt, wk = wBr.next()
                            load_w(wt[:], win_cols(l, 2048 + pr * 128, 128), wk)
                            for tb in range(4):
                                csl = slice(tb * 512, (tb + 1) * 512)
                                ps, pk = proj128(tb, wt, wk)
                                to, tk = tmpO.next()
                                normrope(ps, pk, 512, 128, gq[:, 0:1], "gq", cosT[:, csl], sinT[:, csl], ["cosT", "sinT"], to[:], tk)
                                shift2(to, tk, nqT[:, 2 * pr, csl], nqT[:, 2 * pr + 1, csl], f"nqT{pr}_{tb}")
                        for c0, dst, dkey in ((2560, kcr, "kcr"), (2688, vcr, "vcr")) if B0PART >= 2 else ():
                            wt, wk = wBr.next()
                            load_w(wt[:], win_cols(l, c0, 128), wk)
                            for tb in range(4):
                                csl = slice(tb * 512, (tb + 1) * 512)
                                ps, pk = proj128(tb, wt, wk)
                                to, tk = tmpO.next()
                                op("vector", lambda e, ps=ps, to=to: e.tensor_copy(out=to[:], in_=ps[:]), reads=[pk], writes=[tk])
                                shift2(to, tk, dst[:, 0, csl], dst[:, 1, csl], f"{dkey}_{tb}")
                        for c0, dst, dkey, gi in ((2816, ksT, "ksT", 1), (3072, kwT, "kwT", 2)) if B0PART >= 3 else ():
                            wt, wk = wBr.next()
                            load_w(wt[:], win_cols(l, c0, 128), wk)
                            for tb in range(4):
                                csl = slice(tb * 512, (tb + 1) * 512)
                                ps, pk = proj128(tb, wt, wk)
                                to, tk = tmpO.next()
                                normrope(ps, pk, 512, 128, gk[:, gi:gi + 1], "gk", cosT[:, csl], sinT[:, csl], ["cosT", "sinT"], to[:], tk)
                                shift2(to, tk, dst[:, 0, csl], dst[:, 1, csl], f"{dkey}_{tb}")
                        for c in range(4 if B0PART >= 4 else 0):
                            n_ = 128 if c < 3 else 24
                            load_w(wv4[:, :, c * 128:c * 128 + n_], win_cols(l, 2944 + c * 128, n_), f"wv4_{c}")
                        for tt in range(16 if B0PART >= 4 else 0):
                            ps, pk = ringB_next()
                            mm_chain(ps[:, 0:408], [(hT[:, k, tt * 128:(tt + 1) * 128], wv4[:, k, 0:408]) for k in range(8)],
                                     pk, ["wv4_0", "wv4_1", "wv4_2", "wv4_3", "hT"])
                            op("vector", lambda e, ps=ps, tt=tt: e.tensor_copy(
                                out=vs_aug[:, tt, :, 0:64], in_=ps[:, 0:128].rearrange("p (g d) -> p g d", g=2)),
                               reads=[pk], writes=["vs_aug"], c=300)
                            op("vector", lambda e, ps=ps, tt=tt: e.tensor_copy(
                                out=vw_aug[:, tt, :, 0:64], in_=ps[:, 256:384].rearrange("p (g d) -> p g d", g=2)),
                               reads=[pk], writes=["vw_aug"], c=300)
                            op("scalar", lambda e, ps=ps, tt=tt: e.activation(out=gates[:, tt, :], in_=ps[:, 384:408], func=AF.Sigmoid),
                               reads=[pk], writes=["gates"], c=250)
                        for pr in range(4 if B0PART >= 5 else 0):
                            wt, wk = wBr.next()
                            load_w(wt[:], win_cols(l, 3352 + pr * 128, 128), wk)
                            for tb in range(4):
                                csl = slice(tb * 512, (tb + 1) * 512)
                                ps, pk = proj128(tb, wt, wk)
                                to, tk = tmpO.next()
                                op("scalar", lambda e, ps=ps, to=to: e.activation(out=to[:], in_=ps[:], func=AF.Silu),
                                   reads=[pk], writes=[tk])
                                op("sync", lambda e, to=to, pr=pr, csl=csl: e.dma_start(out=nzT_d[pr * 128:(pr + 1) * 128, csl], in_=to[:]),
                                   reads=[tk], writes=[f"nzT_d{pr}_{tb}"], dma=True)
                        sc.flush()

                    if STOPAT > 3:
                        with contextlib.ExitStack() as sB1:
                            w1_bf = sb("w1_bf", [64, 32, 256], BF16, sB1)
                            w1st = [sb(f"w1st{i}", [64, 4, 256], F32, sB1) for i in range(2)]
                            pe_sb = sb("pe_sb", [32, 64], F32, sB1)
                            peT = sb("peT", [64, 32], BF16, sB1)
                            b1_sb = sb("b1_sb", [128, 2], F32, sB1)
                            w2st = sb("w2st", [128, 2, 64], F32, sB1)
                            w2_bf = sb("w2_bf", [128, 2, 64], BF16, sB1)
                            cvec = sb("cvec", [128, 2], F32, sB1)
                            hid = sb("hid", [128, 2, 2, 127], BF16, sB1)
                            for kv in range(2):
                                raw = kcr if kv == 0 else vcr
                                rkey = "kcr" if kv == 0 else "vcr"
                                for c in range(8):
                                    i = c % 2
                                    src = cmp_w1[l, kv, c * 256:(c + 1) * 256, :].rearrange("(l d) h -> d l h", d=64)
                                    op("sync", lambda e, i=i, src=src: e.dma_start(out=w1st[i][:], in_=src),
                                       writes=[f"w1st{i}"], dma=True)
                                    op("gpsimd", lambda e, i=i, c=c: e.tensor_copy(out=w1_bf[:, c * 4:(c + 1) * 4, :], in_=w1st[i][:]),
                                       reads=[f"w1st{i}"], writes=["w1_bf"])
                                op("sync", lambda e, kv=kv: e.dma_start(out=pe_sb[:], in_=cmp_pe[l, kv]), writes=["pe_sb"], dma=True)
                                for hc in range(2):
                                    op("sync", lambda e, kv=kv, hc=hc: e.dma_start(
                                        out=b1_sb[:, hc:hc + 1],
                                        in_=cmp_b1[l, kv, hc * 128:(hc + 1) * 128].rearrange("(p o) -> p o", o=1)),
                                       writes=["b1_sb"], dma=True)
                                op("sync", lambda e, kv=kv: e.dma_start(
                                    out=w2st[:], in_=cmp_w2[l, kv].rearrange("(c p) d -> p c d", p=128)),
                                   writes=["w2st"], dma=True)
                                op("gpsimd", lambda e: e.tensor_copy(out=w2_bf[:], in_=w2st[:]), reads=["w2st"], writes=["w2_bf"])
                                ps, pk = ringB_next()
                                op("tensor", lambda e, ps=ps: e.transpose(out=ps[0:64, 0:32], in_=pe_sb[:], identity=ident_f[0:32, 0:32]),
                                   reads=["pe_sb"], writes=[pk])
                                op("vector", lambda e, ps=ps: e.tensor_copy(out=peT[:], in_=ps[0:64, 0:32]), reads=[pk], writes=["peT"])
                                for hc in range(2):
                                    ps, pk = ringB_next()
                                    mm_chain(ps[:, 0:1], [(w1_bf[:, li, hc * 128:(hc + 1) * 128], peT[:, li:li + 1]) for li in range(32)],
                                             pk, ["w1_bf", "peT"])
                                    op("vector", lambda e, ps=ps, hc=hc: e.tensor_tensor(out=cvec[:, hc:hc + 1], in0=ps[:, 0:1],
                                                                                         in1=b1_sb[:, hc:hc + 1], op=ALU.add),
                                       reads=[pk, "b1_sb"], writes=["cvec"])
                                for g in range(2):
                                    for hc in range(2):
                                        ps, pk = ringB_next()
                                        mm_chain(ps[:, 0:127], [(w1_bf[:, li, hc * 128:(hc + 1) * 128],
                                                                 raw[:, g, li:li + 16 * 126 + 1:16]) for li in range(32)],
                                                 pk, ["w1_bf", rkey])
                                        op("scalar", lambda e, ps=ps, hc=hc, g=g: e.activation(
                                            out=hid[:, hc, g, :], in_=ps[:, 0:127], func=AF.Silu, bias=cvec[:, hc:hc + 1]),
                                           reads=[pk, "cvec"], writes=["hid"])
                                    ps, pk = ringB_next()
                                    if kv == 0:
                                        mm_chain(ps[0:64, 0:127], [(w2_bf[:, hc, :], hid[:, hc, g, :]) for hc in range(2)],
                                                 pk, ["w2_bf", "hid"])
                                        normrope(ps, pk, 127, 64, gk[0:64, 0:1], "gk", cosC[:], sinC[:], ["cosC", "sinC"],
                                                 kcT[:, g, :], "kcT")
                                    else:
                                        mm_chain(ps[0:127, 0:64], [(hid[:, hc, g, :], w2_bf[:, hc, :]) for hc in range(2)],
                                                 pk, ["w2_bf", "hid"])
                                        op("vector", lambda e, ps=ps, g=g: e.tensor_copy(out=VO[:, g, 0:64], in_=ps[0:127, 0:64]),
                                           reads=[pk], writes=["VO"])
                            sc.flush()

                    with contextlib.ExitStack() as sB2:
                      if STOPAT > 4:
                        NP = 3
                        p_bf = [sb(f"p_bf{i}", [128, 512], BF16, sB2) for i in range(NP)]
                        pR = Ring(p_bf, "p_bf")
                        dn = [sb(f"dn{i}", [128, 3, 4], F32, sB2) for i in range(2)]
                        rdn = [sb(f"rdn{i}", [128, 3, 4], F32, sB2) for i in range(2)]
                        coef = [sb(f"coef{i}", [128, 3, 4], F32, sB2) for i in range(2)]
                        imp = [sb(f"imp{i}", [128, 32], F32, sB2) for i in range(2)]
                        top8 = [sb(f"top8{i}", [128, 8], F32, sB2) for i in range(2)]
                        sbt = [sb(f"sbt{i}", [128, 96], F32, sB2) for i in range(2)]
                        qa = [sb(f"qa{i}", [96, 4, 128], BF16, sB2) for i in range(2)]
                        oc_sb = [sb(f"oc_sb{i}", [128, 388], F32, sB2) for i in range(2)]
                        for i_ in range(2):
                            op("gpsimd", lambda e, i_=i_: e.memset(sbt[i_][:], 0.0), writes=[f"sbt{i_}"])
                        obs = [sb(f"obs{i}", [128, 4, 64], F32, sB2) for i in range(2)]
                        nz_t = [sb(f"nz_t{i}", [64, 4, 128], BF16, sB2) for i in range(2)]
                        yb_sb = [sb(f"yb_sb{i}", [64, 4, 128], BF16, sB2) for i in range(2)]
                        ocb, osb, owb = banks[0], banks[1], banks[2]
                        oTs, oTw = banks[3], banks[4]
                        oT_sb = [[sb(f"oT_sb{a}{i}", [65, 512], F32, sB2) for i in range(2)] for a in range(2)]
                        rb2 = [0]

                        def ringB_next():
                            j = 5 + (rb2[0] % 3)
                            rb2[0] += 1
                            return banks[j], f"bank{j}"

                        oc = ocb[:, 0:388].rearrange("p (r c) -> p r c", r=4)
                        os_ = osb[:, 0:260].rearrange("p (r c) -> p r c", r=4)
                        ow = owb[:, 0:260].rearrange("p (r c) -> p r c", r=4)
                        it = 0
                        for g in range(2):
                            for i in range(16):
                                u = it % 2
                                it += 1
                                tsl = slice(i * 128, (i + 1) * 128)
                                q_ap = nqT[:, 4 * g:4 * g + 4, tsl]
                                op("sync", lambda e, u=u, g=g, tsl=tsl: e.dma_start(out=nz_t[u][:], in_=nzT_d[256 * g:256 * (g + 1), tsl].rearrange("(r d) t -> d r t", d=64)),
                                   reads=[], writes=[f"nz_t{u}"], dma=True)
                                ps, pk = ringB_next()
                                op("tensor", lambda e, ps=ps, g=g, q_ap=q_ap: e.matmul(ps[0:127, :], lhsT=kcT[:, g, :], rhs=q_ap,
                                                                                       start=True, stop=False),
                                   reads=["kcT", "nqT"], writes=[pk])
                                op("tensor", lambda e, ps=ps, tsl=tsl: e.matmul(ps[0:127, :], lhsT=ident_bf[0:127, 0:127],
                                                                                rhs=bc4(cmask[:, tsl]), start=False, stop=True),
                                   reads=[], writes=[pk])
                                pb, pbk = pR.next()
                                op("scalar", lambda e, ps=ps, pb=pb: e.activation(out=pb[0:127, :], in_=ps[0:127, :], func=AF.Exp),
                                   reads=[pk], writes=[pbk])
                                for r in range(4):
                                    op("tensor", lambda e, pb=pb, r=r, g=g: e.matmul(oc[:, r, :], lhsT=pb[0:127, r * 128:(r + 1) * 128],
                                                                                     rhs=VO[:, g, :], start=True, stop=True),
                                       reads=[pbk, "VO"], writes=["bank0"])
                                op("scalar", lambda e, u=u: e.copy(out=oc_sb[u][:], in_=ocb[:, 0:388]), reads=["bank0"], writes=[f"oc_sb{u}"])
                                ocs = oc_sb[u][:].rearrange("p (r c) -> p r c", r=4)
                                op("vector", lambda e, u=u, ocs=ocs: e.tensor_scalar(out=dn[u][:, 0, :], in0=ocs[:, :, 64], scalar1=1e-30, scalar2=None,
                                                                            op0=ALU.max), reads=[f"oc_sb{u}"], writes=[f"dn{u}"])
                                op("vector", lambda e, u=u: e.reciprocal(out=rdn[u][:, 0, :], in_=dn[u][:, 0, :]),
                                   reads=[f"dn{u}"], writes=[f"rdn{u}"])
                                for r in range(4):
                                    in1 = bonus[:, i, :] if r == 0 else imp[u][:]
                                    op("vector", lambda e, u=u, r=r, in1=in1, ocs=ocs: e.scalar_tensor_tensor(
                                        out=imp[u][:], in0=ocs[:, r, 65:97], scalar=rdn[u][:, 0, r:r + 1], in1=in1,
                                        op0=ALU.mult, op1=ALU.add),
                                       reads=[f"oc_sb{u}", f"rdn{u}", f"imp{u}"], writes=[f"imp{u}"])
                                op("vector", lambda e, u=u: e.max(out=top8[u][:], in_=imp[u][:]), reads=[f"imp{u}"], writes=[f"top8{u}"])
                                op("vector", lambda e, u=u: e.tensor_scalar(out=sbt[u][:, 64:96], in0=imp[u][:], scalar1=top8[u][:, 7:8],
                                                                            scalar2=NEG, op0=ALU.is_lt, op1=ALU.mult),
                                   reads=[f"imp{u}", f"top8{u}"], writes=[f"sbt{u}"])
                                ps, pk = ringB_next()
                                op("tensor", lambda e, ps=ps, u=u: e.transpose(out=ps[0:96, 0:128], in_=sbt[u][:], identity=ident_f[:]),
                                   reads=[f"sbt{u}"], writes=[pk])
                                op("vector", lambda e, ps=ps, u=u: e.tensor_copy(out=qa[u][64:96, :, :], in_=bc4(ps[64:96, 0:128])),
                                   reads=[pk], writes=[f"qa{u}"])
                                op("gpsimd", lambda e, u=u, q_ap=q_ap: e.tensor_copy(out=qa[u][0:64, :, :], in_=q_ap),
                                   reads=["nqT", f"qa{u}"], writes=[f"qa{u}"])
                                for kt in range(i + 1):
                                    ksl = slice(kt * 128, (kt + 1) * 128)
                                    ps, pk = ringB_next()
                                    op("tensor", lambda e, ps=ps, g=g, ksl=ksl, u=u, kt=kt, i=i: e.matmul(
                                        ps[:], lhsT=ksTa[0:96, g, ksl], rhs=qa[u][:, :, :], start=True, stop=(kt != i)),
                                       reads=["ksT", f"qa{u}"], writes=[pk])
                                    if kt == i:
                                        op("tensor", lambda e, ps=ps: e.matmul(ps[:], lhsT=ident_bf[:], rhs=bc4(maskC[:]),
                                                                               start=False, stop=True),
                                           reads=[], writes=[pk])
                                    pb, pbk = pR.next()
                                    op("scalar", lambda e, ps=ps, pb=pb: e.activation(out=pb[:], in_=ps[:], func=AF.Exp),
                                       reads=[pk], writes=[pbk])
                                    op("tensor", lambda e, pb=pb, g=g, kt=kt, i=i: e.matmul(
                                        oTs[0:65, :], lhsT=vs_aug[:, kt, g, :], rhs=pb[:], start=(kt == 0), stop=(kt == i)),
                                       reads=[pbk, "vs_aug"], writes=["bank3"])
                                op("scalar", lambda e, u=u: e.copy(out=oT_sb[0][u][:], in_=oTs[0:65, :]),
                                   reads=["bank3"], writes=[f"oT_sb0{u}"])
                                for r in range(4):
                                    op("tensor", lambda e, u=u, r=r: e.transpose(out=os_[:, r, :], in_=oT_sb[0][u][:, r * 128:(r + 1) * 128],
                                                                                 identity=ident_f[0:65, 0:65]),
                                       reads=[f"oT_sb0{u}"], writes=["bank1"], c=300)
                                kts = [kt for kt in range(i - 4, i + 1) if kt >= 0]
                                for kt in kts:
                                    ksl = slice(kt * 128, (kt + 1) * 128)
                                    edge = (kt == i) or (kt == i - 4)
                                    ps, pk = ringB_next()
                                    op("tensor", lambda e, ps=ps, g=g, ksl=ksl, q_ap=q_ap, edge=edge: e.matmul(
                                        ps[:], lhsT=kwT[:, g, ksl], rhs=q_ap, start=True, stop=not edge),
                                       reads=["kwT", "nqT"], writes=[pk])
                                    if edge:
                                        mk_ = maskC if kt == i else maskW
                                        op("tensor", lambda e, ps=ps, mk_=mk_: e.matmul(ps[:], lhsT=ident_bf[:], rhs=bc4(mk_[:]),
                                                                                        start=False, stop=True),
                                           reads=[], writes=[pk])
                                    pb, pbk = pR.next()
                                    op("scalar", lambda e, ps=ps, pb=pb: e.activation(out=pb[:], in_=ps[:], func=AF.Exp),
                                       reads=[pk], writes=[pbk])
                                    op("tensor", lambda e, pb=pb, g=g, kt=kt, kts=kts: e.matmul(
                                        oTw[0:65, :], lhsT=vw_aug[:, kt, g, :], rhs=pb[:], start=(kt == kts[0]), stop=(kt == kts[-1])),
                                       reads=[pbk, "vw_aug"], writes=["bank4"])
                                op("scalar", lambda e, u=u: e.copy(out=oT_sb[1][u][:], in_=oTw[0:65, :]),
                                   reads=["bank4"], writes=[f"oT_sb1{u}"])
                                for r in range(4):
                                    op("tensor", lambda e, u=u, r=r: e.transpose(out=ow[:, r, :], in_=oT_sb[1][u][:, r * 128:(r + 1) * 128],
                                                                                 identity=ident_f[0:65, 0:65]),
                                       reads=[f"oT_sb1{u}"], writes=["bank2"], c=300)
                                op("vector", lambda e, u=u: e.tensor_copy(out=dn[u][:, 1, :], in_=os_[:, :, 64]),
                                   reads=["bank1"], writes=[f"dn{u}"])
                                op("vector", lambda e, u=u: e.tensor_copy(out=dn[u][:, 2, :], in_=ow[:, :, 64]),
                                   reads=["bank2"], writes=[f"dn{u}"])
                                op("vector", lambda e, u=u: e.reciprocal(out=rdn[u][:, 1:3, :], in_=dn[u][:, 1:3, :]),
                                   reads=[f"dn{u}"], writes=[f"rdn{u}"])
                                gview = gates[:, i, :].rearrange("p (b h) -> p b h", b=3)[:, :, 4 * g:4 * g + 4]
                                op("vector", lambda e, u=u, gview=gview: e.tensor_tensor(out=coef[u][:], in0=rdn[u][:], in1=gview, op=ALU.mult),
                                   reads=[f"rdn{u}", "gates"], writes=[f"coef{u}"])
                                for r in range(4):
                                    op("vector", lambda e, u=u, r=r, ocs=ocs: e.tensor_scalar(out=obs[u][:, r, :], in0=ocs[:, r, 0:64],
                                                                                     scalar1=coef[u][:, 0, r:r + 1], scalar2=None, op0=ALU.mult),
                                       reads=[f"oc_sb{u}", f"coef{u}"], writes=[f"obs{u}"])
                                    op("vector", lambda e, u=u, r=r: e.scalar_tensor_tensor(
                                        out=obs[u][:, r, :], in0=os_[:, r, 0:64], scalar=coef[u][:, 1, r:r + 1], in1=obs[u][:, r, :],
                                        op0=ALU.mult, op1=ALU.add),
                                       reads=["bank1", f"coef{u}", f"obs{u}"], writes=[f"obs{u}"])
                                    op("vector", lambda e, u=u, r=r: e.scalar_tensor_tensor(
                                        out=obs[u][:, r, :], in0=ow[:, r, 0:64], scalar=coef[u][:, 2, r:r + 1], in1=obs[u][:, r, :],
                                        op0=ALU.mult, op1=ALU.add),
                                       reads=["bank2", f"coef{u}", f"obs{u}"], writes=[f"obs{u}"])
                                ps, pk = ringB_next()
                                tp = ps[0:64, :].rearrange("p (r t) -> p r t", r=4)
                                for r in range(4):
                                    op("tensor", lambda e, tp=tp, u=u, r=r: e.transpose(out=tp[:, r, :], in_=obs[u][:, r, :], identity=ident_f[:]),
                                       reads=[f"obs{u}"], writes=[pk])
                                op("vector", lambda e, tp=tp, u=u: e.tensor_tensor(out=yb_sb[u][:], in0=tp, in1=nz_t[u][:], op=ALU.mult),
                                   reads=[pk, f"nz_t{u}"], writes=[f"yb_sb{u}"])
                                op("sync", lambda e, u=u, g=g, tsl=tsl: e.dma_start(out=yTb_d[256 * g:256 * (g + 1), tsl].rearrange("(r d) t -> d r t", d=64), in_=yb_sb[u][:]),
                                   reads=[f"yb_sb{u}"], writes=["yTb_d"], dma=True)
                        sc.flush()

                with contextlib.ExitStack() as sC:
                    wo = sb("wo", [128, 8, D], BF16, sC)
                    mT = sb("mT", [128, 8, S], BF16, sC)
                    ya_t = sb("ya_t", [128, 4, S], BF16, sC)
                    yb_t = sb("yb_t", [128, 4, S], BF16, sC)
                    wcu = [[sb(f"wcu{p}{i}", [128, 4, 128], BF16, sC) for i in range(2)] for p in range(2)]
                    wcg = [[sb(f"wcg{p}{i}", [128, 8, 128], BF16, sC) for i in range(2)] for p in range(2)]
                    sg = [sb(f"sg{i}", [128, 512], F32, sC) for i in range(2)]
                    m1 = [sb(f"m1{i}", [128, 512], F32, sC) for i in range(2)]
                    m2 = [sb(f"m2{i}", [128, 512], F32, sC) for i in range(2)]
                    xt = [sb(f"cxt{i}", [128, D], F32, sC) for i in range(2)]
                    ringC = Ring(banks, "bank")
                    yav = yTa_d.rearrange("(hp p) s -> p hp s", p=128)
                    ybv = yTb_d.rearrange("(hp p) s -> p hp s", p=128)
                    for tb in range(4):
                        bsl = slice(tb * 512, (tb + 1) * 512)
                        op("sync", lambda e, bsl=bsl: e.dma_start(out=ya_t[:, :, bsl], in_=yav[:, :, bsl]),
                           reads=["yTa_d"], writes=[f"ya_t{tb}"], dma=True)
                        op("sync", lambda e, bsl=bsl: e.dma_start(out=yb_t[:, :, bsl], in_=ybv[:, :, bsl]),
                           reads=["yTb_d"], writes=[f"yb_t{tb}"], dma=True)
                    sgc = 0
                    for dmc in range(8):
                        cs = slice(dmc * 128, (dmc + 1) * 128)
                        p = dmc % 2
                        load_w(wcu[p][0][:], w_up_a[l, :, cs].rearrange("(hp p) m -> p hp m", p=128), f"wcu{p}0")
                        load_w(wcg[p][0][:], win_cols(l, 3864 + dmc * 128, 128), f"wcg{p}0")
                        load_w(wcu[p][1][:], w_up_b[l, :, cs].rearrange("(hp p) m -> p hp m", p=128), f"wcu{p}1")
                        load_w(wcg[p][1][:], win_cols(l, 4888 + dmc * 128, 128), f"wcg{p}1")
                        if dmc >= 1:
                            c = dmc - 1
                            load_w(wo[:, :, c * 128:(c + 1) * 128], w_out[l, :, c * 128:(c + 1) * 128].rearrange("(k p) m -> p k m", p=128), "wo")
                        if dmc == 7:
                            load_w(wo[:, :, 7 * 128:8 * 128], w_out[l, :, 7 * 128:8 * 128].rearrange("(k p) m -> p k m", p=128), "wo")
                        for tb in range(4):
                            bsl = slice(tb * 512, (tb + 1) * 512)
                            j = sgc % 2
                            sgc += 1
                            for br, (yt, ytk, mm) in enumerate(((ya_t, f"ya_t{tb}", m1), (yb_t, f"yb_t{tb}", m2))):
                                pu, puk = ringC.next()
                                mm_chain(pu[:], [(wcu[p][br][:, hp, :], yt[:, hp, bsl]) for hp in range(4)], puk, [f"wcu{p}{br}", ytk])
                                pg, pgk = ringC.next()
                                mm_chain(pg[:], [(wcg[p][br][:, k, :], hT[:, k, bsl]) for k in range(8)], pgk, [f"wcg{p}{br}", "hT"])
                                op("scalar", lambda e, pg=pg, j=j: e.activation(out=sg[j][:], in_=pg[:], func=AF.Sigmoid),
                                   reads=[pgk], writes=[f"sg{j}"])
                                op("vector", lambda e, pu=pu, j=j, mm=mm: e.tensor_tensor(out=mm[j][:], in0=pu[:], in1=sg[j][:], op=ALU.mult),
                                   reads=[puk, f"sg{j}"], writes=[f"m{br + 1}{j}"])
                            op("gpsimd", lambda e, j=j, dmc=dmc, bsl=bsl: e.tensor_tensor(out=mT[:, dmc, bsl], in0=m1[j][:], in1=m2[j][:], op=ALU.add),
                               reads=[f"m1{j}", f"m2{j}"], writes=[f"mT{tb}"])
                    for tt in range(16):
                        tb = tt // 4
                        v = tt % 2
                        op("sync", lambda e, v=v, tt=tt: e.dma_start(out=xt[v][:], in_=x_src[b, tt * 128:(tt + 1) * 128, :]),
                           reads=[], writes=[f"cxt{v}"], dma=True)
                        for half in range(2):
                            hs = slice(half * 512, (half + 1) * 512)
                            po, pok = ringC.next()
                            mm_chain(po[:], [(mT[:, k, tt * 128:(tt + 1) * 128], wo[:, k, hs]) for k in range(8)],
                                     pok, [f"mT{tb}", "wo"])
                            op("vector", lambda e, po=po, v=v, hs=hs: e.tensor_tensor(out=xt[v][:, hs], in0=po[:], in1=xt[v][:, hs], op=ALU.add),
                               reads=[pok, f"cxt{v}"], writes=[f"cxt{v}"])
                        op("sync", lambda e, v=v, tt=tt: e.dma_start(out=out[b, tt * 128:(tt + 1) * 128, :], in_=xt[v][:]),
                           reads=[f"cxt{v}"], writes=[], dma=True)
                    sc.flush()
        print("total instructions (incl. waits):", sc.ninstr)
    return nc


_INVF = (1.0 / (2.0 * math.pi) * (10000.0 ** (-(np.arange(128) % 32) / 32.0))).astype(np.float32).reshape(128, 1)
_PROG = {}


def _get_prog(NL):
    if NL not in _PROG:
        _PROG[NL] = build_program(NL)
    return _PROG[NL]


WNAMES = ["norm_g", "w_in", "q_norm_g", "k_norm_g", "cmp_pe", "cmp_w1", "cmp_b1", "cmp_w2", "w_up_a", "w_up_b", "w_out"]


def kernel(x, positions, norm_g, w_in, q_norm_g, k_norm_g, cmp_pe, cmp_w1, cmp_b1, cmp_w2,
           w_up_a, w_up_b, w_out, _layers_per_launch=DEPTH):
    ws = dict(norm_g=norm_g, w_in=w_in, q_norm_g=q_norm_g, k_norm_g=k_norm_g, cmp_pe=cmp_pe, cmp_w1=cmp_w1,
              cmp_b1=cmp_b1, cmp_w2=cmp_w2, w_up_a=w_up_a, w_up_b=w_up_b, w_out=w_out)
    ws = {k: np.ascontiguousarray(np.asarray(v, dtype=np.float32)) for k, v in ws.items()}
    xcur = np.ascontiguousarray(np.asarray(x, dtype=np.float32))
    pos = np.ascontiguousarray(np.asarray(positions, dtype=np.int32))
    NL = _layers_per_launch
    nc = _get_prog(NL)
    for l0 in range(0, DEPTH, NL):
        in_maps = []
        for c in range(NCORES):
            m = {"x": xcur[c * NB:(c + 1) * NB], "positions": pos[c * NB:(c + 1) * NB], "invf": _INVF}
            for k, v in ws.items():
                m[k] = v[l0:l0 + NL]
            in_maps.append(m)
        res = run_bass_kernel_spmd(nc, in_maps, core_ids=list(range(NCORES)))
        xcur = np.concatenate([np.asarray(r["out"]) for r in res.results], axis=0)
    return xcur.astype(np.float32)
```

```python
import contextlib
import math
import numpy as np
import concourse.bass as bass
import concourse.mybir as mybir
from concourse.bass_utils import run_bass_kernel_spmd

F32 = mybir.dt.float32
BF16 = mybir.dt.bfloat16
I32 = mybir.dt.int32
AF = mybir.ActivationFunctionType
ALU = mybir.AluOpType
AX = mybir.AxisListType

S = 2048
D = 1024
NIN = 5912
NCORES = 8
NB = 2
DEPTH = 4
NEG = -30000.0
EPS = 1e-6

ENGS = ["tensor", "vector", "scalar", "gpsimd", "sync"]
EPOCH = 30000
NDMASEM = 12


DEFCOST = {"tensor": 450.0, "scalar": 600.0, "vector": 650.0, "gpsimd": 1200.0, "sync": 60.0}
WINDOW = 64


class Sched:
    def __init__(self, nc, stack):
        self.nc = nc
        self.stack = stack
        self.cnt = {e: 0 for e in ENGS}
        self.esem = {}
        for e in ENGS:
            if e != "sync":
                self.esem[e] = stack.enter_context(nc.semaphore(f"es_{e}_0"))
        self.eepoch = {e: 0 for e in ENGS}
        self.dsem = {}
        self.dcnt = {}
        self.dnext = {}
        for q in ["sync", "gpsimd", "scalar"]:
            self.dsem[q] = [stack.enter_context(nc.semaphore(f"ds_{q}_{i}")) for i in range(NDMASEM)]
            self.dcnt[q] = [0] * NDMASEM
            self.dnext[q] = 0
        self.ninstr = 0
        self.reorder = True
        self._reset()

    def _reset(self):
        self.nodes = []
        self.last_w = {}
        self.last_r = {}

    def op(self, eng, fn, reads=(), writes=(), dma=False, c=None):
        preds = set()
        for b in reads:
            w = self.last_w.get(b)
            if w is not None:
                preds.add(w)
            if b.startswith("bank"):
                for r_ in self.last_r.get(b, ()):
                    if self.nodes[r_][0] != eng:
                        preds.add(r_)
        for b in writes:
            w = self.last_w.get(b)
            if w is not None:
                preds.add(w)
            preds.update(self.last_r.get(b, ()))
        nid = len(self.nodes)
        if c is None:
            c = 3000.0 if dma else DEFCOST[eng]
        self.nodes.append((eng, fn, dma, preds, float(c)))
        for b in reads:
            self.last_r.setdefault(b, []).append(nid)
        for b in writes:
            self.last_w[b] = nid
            self.last_r[b] = []
        return nid

    def _simulate(self):
        import heapq
        nodes = self.nodes
        n = len(nodes)
        order = {e: [] for e in ENGS}
        if not self.reorder:
            for nid, nd in enumerate(nodes):
                order[nd[0]].append(nid)
            return order
        succ = [[] for _ in range(n)]
        indeg = [0] * n
        for nid, nd in enumerate(nodes):
            for p in nd[3]:
                succ[p].append(nid)
            indeg[nid] = len(nd[3])
        ready = {e: [] for e in ENGS}
        for nid, nd in enumerate(nodes):
            if indeg[nid] == 0:
                heapq.heappush(ready[nd[0]], nid)
        free_at = {e: 0.0 for e in ENGS}
        events = []
        now = 0.0
        left = n
        while left:
            progressed = False
            for e in ENGS:
                if free_at[e] <= now and ready[e]:
                    nid = heapq.heappop(ready[e])
                    nd = nodes[nid]
                    if nd[2]:
                        free_at[e] = now + 60.0
                        heapq.heappush(events, (free_at[e], -1))
                    else:
                        free_at[e] = now + nd[4]
                    heapq.heappush(events, (now + nd[4], nid))
                    order[e].append(nid)
                    left -= 1
                    progressed = True
            if not progressed:
                assert events, "scheduler deadlock"
                t, nid = heapq.heappop(events)
                now = max(now, t)
                while True:
                    if nid >= 0:
                        for sx in succ[nid]:
                            indeg[sx] -= 1
                            if indeg[sx] == 0:
                                heapq.heappush(ready[nodes[sx][0]], sx)
                    if events and events[0][0] <= now:
                        t, nid = heapq.heappop(events)
                    else:
                        break
        return order

    def flush(self):
        nc = self.nc
        nodes = self.nodes
        order = self._simulate()
        tok = [None] * len(nodes)
        extra = {}
        for e in ENGS:
            for nid in order[e]:
                if nodes[nid][2]:
                    i = self.dnext[e]
                    self.dnext[e] = (i + 1) % NDMASEM
                    sem = self.dsem[e][i]
                    if self.dcnt[e][i] > 0:
                        extra[nid] = (sem, self.dcnt[e][i])
                    self.dcnt[e][i] += 16
                    tok[nid] = (sem, self.dcnt[e][i], 16)
                else:
                    if self.cnt[e] >= EPOCH:
                        self.eepoch[e] += 1
                        self.esem[e] = self.stack.enter_context(nc.semaphore(f"es_{e}_{self.eepoch[e]}"))
                        self.cnt[e] = 0
                    self.cnt[e] += 1
                    tok[nid] = (self.esem[e], self.cnt[e], 1)
        prog = {}
        for e in ENGS:
            seen = {}
            lst = []
            for nid in order[e]:
                waits = {}
                cands = [tok[p][:2] for p in nodes[nid][3] if not (e == "tensor" and nodes[p][0] == "tensor")]
                if nid in extra:
                    cands.append(extra[nid])
                for (sm, v) in cands:
                    k = id(sm)
                    if seen.get(k, 0) >= v:
                        continue
                    if k not in waits or waits[k][1] < v:
                        waits[k] = (sm, v)
                wl = list(waits.values())
                for (sm, v) in wl:
                    seen[id(sm)] = v
                lst.append((wl, nodes[nid][1], tok[nid][0], tok[nid][2]))
                self.ninstr += 1 + len(wl)
            prog[e] = lst
        finals = {}
        for q in ["sync", "gpsimd", "scalar"]:
            finals[q] = [(self.dsem[q][i], self.dcnt[q][i]) for i in range(NDMASEM) if self.dcnt[q][i] > 0]
        with nc.Block() as block:
            def mk(ename):
                def body(eng):
                    for (wl, fn, sem, inc) in prog[ename]:
                        for (sm, v) in wl:
                            eng.wait_ge(sm, v)
                        fn(eng).then_inc(sem, inc)
                    for (sm, v) in finals.get(ename, []):
                        eng.wait_ge(sm, v)
                return body
            block.tensor(mk("tensor"))
            block.vector(mk("vector"))
            block.scalar(mk("scalar"))
            block.gpsimd(mk("gpsimd"))
            block.sync(mk("sync"))
        self._reset()


class Ring:
    def __init__(self, bufs, name):
        self.bufs = bufs
        self.name = name
        self.i = 0

    def next(self):
        j = self.i % len(self.bufs)
        self.i += 1
        return self.bufs[j], f"{self.name}{j}"


def bc4(ap2d, n=4):
    p, f = ap2d.shape
    return ap2d.unsqueeze(1).broadcast_to([p, n, f])


STOPAT = 99
B0PART = 99


def build_program(NL):
    nc = bass.Bass("TRN2", target_bir_lowering=False)
    dt_in = lambda name, shape, dt=F32: nc.dram_tensor(name, shape, dt, kind="ExternalInput").ap()
    x_in = dt_in("x", [NB, S, D])
    pos_in = dt_in("positions", [NB, S], I32)
    norm_g = dt_in("norm_g", [NL, D])
    w_in = dt_in("w_in", [NL, D, NIN])
    q_norm_g = dt_in("q_norm_g", [NL, 64])
    k_norm_g = dt_in("k_norm_g", [NL, 3, 64])
    cmp_pe = dt_in("cmp_pe", [NL, 2, 32, 64])
    cmp_w1 = dt_in("cmp_w1", [NL, 2, 2048, 256])
    cmp_b1 = dt_in("cmp_b1", [NL, 2, 256])
    cmp_w2 = dt_in("cmp_w2", [NL, 2, 256, 64])
    w_up_a = dt_in("w_up_a", [NL, 512, D])
    w_up_b = dt_in("w_up_b", [NL, 512, D])
    w_out = dt_in("w_out", [NL, D, D])
    invf_in = dt_in("invf", [128, 1])
    out = nc.dram_tensor("out", [NB, S, D], F32, kind="ExternalOutput").ap()
    yTa_d = nc.dram_tensor("yTa_scr", [512, S], BF16).ap()
    yTb_d = nc.dram_tensor("yTb_scr", [512, S], BF16).ap()
    nzT_d = nc.dram_tensor("nzT_scr", [512, S], BF16).ap()

    with contextlib.ExitStack() as st:
        sc = Sched(nc, st)
        E = st.enter_context
        op = sc.op

        uid = [0]

        def sb(name, shape, dt, stack=None):
            uid[0] += 1
            return (stack if stack is not None else st).enter_context(nc.sbuf_tensor(f"{name}_{uid[0]}", shape, dt))

        ident_bf = sb("ident_bf", [128, 128], BF16)
        ident_f = sb("ident_f", [128, 128], F32)
        negtri = sb("negtri", [128, 128], BF16)
        negones = sb("negones", [128, 128], BF16)
        ones_blk = sb("ones_blk", [128, 128], BF16)
        rotM_blk = sb("rotM_blk", [128, 128], BF16)
        maskS = sb("maskS", [128, 128], BF16)
        maskC = sb("maskC", [128, 128], BF16)
        maskW = sb("maskW", [128, 128], BF16)
        Eexp = sb("Eexp", [32, S], BF16)
        cmask = sb("cmask", [127, S], BF16)
        bonus = sb("bonus", [128, 16, 32], F32)
        bonusF = sb("bonusF", [128, 16, 32], F32)
        invf = sb("invf_sb", [128, 1], F32)
        hT = sb("hT", [128, 8, S], BF16)
        cosT = sb("cosT", [128, S], F32)
        sinT = sb("sinT", [128, S], F32)
        cosC = sb("cosC", [64, 127], F32)
        sinC = sb("sinC", [64, 127], F32)
        VO = sb("VO", [127, 2, 97], BF16)
        vs_aug = sb("vs_aug", [128, 16, 2, 65], BF16)
        vw_aug = sb("vw_aug", [128, 16, 2, 65], BF16)
        kcT = sb("kcT", [64, 2, 127], BF16)
        ksTa = sb("ksTa", [96, 2, S], BF16)
        gq = sb("gq", [128, 1], F32)
        gk = sb("gk", [128, 3], F32)
        wst = Ring([sb(f"wst{i}", [128, 8, 128], F32) for i in range(2)], "wst")
        banks = [E(nc.psum_tensor(f"bank{i}", [128, 512], F32)) for i in range(8)]


        def sel(t, ap, pattern, cop, fill, base, cm, key):
            op("gpsimd", lambda e: e.affine_select(out=ap, in_=ap, pattern=pattern, compare_op=cop,
                                                   fill=fill, base=base, channel_multiplier=cm),
               reads=[key], writes=[key])

        def mset(ap, val, key):
            op("gpsimd", lambda e: e.memset(ap, val), writes=[key])

        mset(ident_bf[:], 0.0, "ident_bf")
        sel(ident_bf, ident_bf[:], [[-1, 128]], ALU.not_equal, 1.0, 0, 1, "ident_bf")
        mset(ident_f[:], 0.0, "ident_f")
        sel(ident_f, ident_f[:], [[-1, 128]], ALU.not_equal, 1.0, 0, 1, "ident_f")
        mset(negtri[:], -1.0, "negtri")
        sel(negtri, negtri[:], [[-1, 128]], ALU.is_ge, 0.0, 0, 1, "negtri")
        mset(negones[:], -1.0, "negones")
        mset(ones_blk[:], 1.0, "ones_blk")
        mset(ones_blk[0:64, 64:128], 0.0, "ones_blk")
        mset(ones_blk[64:128, 0:64], 0.0, "ones_blk")
        mset(rotM_blk[:], 0.0, "rotM_blk")
        for q0 in (0, 64):
            sel(rotM_blk, rotM_blk[q0:q0 + 64, q0:q0 + 64], [[-1, 64]], ALU.not_equal, -1.0, -32, 1, "rotM_blk")
            sel(rotM_blk, rotM_blk[q0:q0 + 64, q0:q0 + 64], [[-1, 64]], ALU.not_equal, 1.0, 32, 1, "rotM_blk")
        mset(maskS[:], 0.0, "maskS")
        sel(maskS, maskS[:], [[1, 128]], ALU.is_ge, NEG, -1, -1, "maskS")
        mset(maskC[:], 0.0, "maskC")
        sel(maskC, maskC[:], [[1, 128]], ALU.is_ge, NEG, 0, -1, "maskC")
        mset(maskW[:], 0.0, "maskW")
        sel(maskW, maskW[:], [[-1, 128]], ALU.is_ge, NEG, -1, 1, "maskW")
        mset(Eexp[:], 1.0, "Eexp")
        sel(Eexp, Eexp[:], [[1, S]], ALU.is_ge, 0.0, 0, -64, "Eexp")
        sel(Eexp, Eexp[:], [[-1, S]], ALU.is_ge, 0.0, 63, 64, "Eexp")
        for g_ in range(2):
            mset(ksTa[64:96, g_, :], 1.0, "ksTa")
            sel(ksTa, ksTa[64:96, g_, :], [[1, S]], ALU.is_ge, 0.0, 0, -64, "ksTa")
            sel(ksTa, ksTa[64:96, g_, :], [[-1, S]], ALU.is_ge, 0.0, 63, 64, "ksTa")
        mset(cmask[:], 0.0, "cmask")
        sel(cmask, cmask[:], [[1, S]], ALU.is_ge, NEG, -31, -16, "cmask")
        mset(VO[:], 1.0, "VO")
        sel(VO, VO[:, :, 65:97], [[0, 2], [64, 32]], ALU.is_ge, 0.0, 63, -16, "VO")
        sel(VO, VO[:, :, 65:97], [[0, 2], [-64, 32]], ALU.is_ge, 0.0, 31, 16, "VO")
        mset(vs_aug[:], 1.0, "vs_aug")
        mset(vw_aug[:], 1.0, "vw_aug")
        mset(bonus[:], 0.0, "bonus")
        sel(bonus, bonus[:], [[128, 16], [-64, 32]], ALU.is_ge, -1e30, 0, 1, "bonus")
        mset(bonusF[:], 1e4, "bonusF")
        sel(bonusF, bonusF[:], [[128, 16], [-64, 32]], ALU.is_ge, 0.0, 0, 1, "bonusF")
        sel(bonusF, bonusF[:], [[-128, 16], [64, 32]], ALU.is_ge, 0.0, 127, -1, "bonusF")
        op("gpsimd", lambda e: e.tensor_tensor(out=bonus[:], in0=bonus[:], in1=bonusF[:], op=ALU.add),
           reads=["bonus", "bonusF"], writes=["bonus"])
        mset(bonus[:, :, 0:1], 1e4, "bonus")
        op("sync", lambda e: e.dma_start(out=invf[:], in_=invf_in[:, :]), writes=["invf"], dma=True)
        sc.flush()

        def load_w(dst_ap, src_ap, key, np_=128):
            stg, skey = wst.next()
            a, b = src_ap.shape[1], src_ap.shape[2]
            sv = stg[0:np_, 0:a, 0:b]
            op("sync", lambda e: e.dma_start(out=sv, in_=src_ap), writes=[skey], dma=True)
            op("gpsimd", lambda e: e.tensor_copy(out=dst_ap, in_=sv), reads=[skey], writes=[key])

        def win_cols(l, c0, n):
            return w_in[l, :, c0:c0 + n].rearrange("(k p) m -> p k m", p=128)

        def mm_chain(out_ap, pairs, okey, rkeys):
            n = len(pairs)
            for j, (lh, rh) in enumerate(pairs):
                op("tensor", lambda e, lh=lh, rh=rh, j=j: e.matmul(out_ap, lhsT=lh, rhs=rh,
                                                                    start=(j == 0), stop=(j == n - 1)),
                   reads=rkeys, writes=[okey])

        for b in range(NB):
            with contextlib.ExitStack() as ss_:
                posi = sb("posi", [128, S], I32, ss_)
                posf = sb("posf", [128, S], F32, ss_)
                vv = sb("vv", [128, S], F32, ss_)
                uu = sb("uu", [128, S], F32, ss_)
                ui = sb("ui", [128, S], I32, ss_)
                uf = sb("uf", [128, S], F32, ss_)
                gg = sb("gg", [128, S], F32, ss_)
                mm_ = sb("mm_", [128, S], F32, ss_)
                posm = sb("posm", [128, 127], F32, ss_)
                vm = sb("vm", [128, 127], F32, ss_)
                op("sync", lambda e: e.dma_start(out=posi[:], in_=pos_in[b].partition_broadcast(128)),
                   writes=["posi"], dma=True)
                op("vector", lambda e: e.tensor_copy(out=posf[:], in_=posi[:]), reads=["posi"], writes=["posf"])
                op("vector", lambda e: e.tensor_scalar(out=vv[:], in0=posf[:], scalar1=invf[:, 0:1], scalar2=None,
                                                       op0=ALU.mult), reads=["posf", "invf"], writes=["vv"])
                win = bass.AP(posf[:].tensor, posf[:].offset, [[S, 128], [16, 127], [1, 32]])
                op("vector", lambda e: e.reduce_sum(out=posm[:], in_=win, axis=AX.X), reads=["posf"], writes=["posm"])
                op("vector", lambda e: e.tensor_scalar(out=vm[:], in0=posm[:], scalar1=invf[:, 0:1], scalar2=1.0 / 32,
                                                       op0=ALU.mult, op1=ALU.mult),
                   reads=["posm", "invf"], writes=["vm"])

                def table(vsrc, n, add, dst, dkey, skey, P=128):
                    u = uu[0:P, 0:n]
                    op("vector", lambda e: e.tensor_scalar(out=u, in0=vsrc, scalar1=float(add), scalar2=None,
                                                           op0=ALU.add), reads=[skey], writes=["uu"])
                    op("vector", lambda e: e.tensor_copy(out=ui[0:P, 0:n], in_=u), reads=["uu"], writes=["ui"])
                    op("vector", lambda e: e.tensor_copy(out=uf[0:P, 0:n], in_=ui[0:P, 0:n]), reads=["ui"], writes=["uf"])
                    op("vector", lambda e: e.tensor_tensor(out=gg[0:P, 0:n], in0=u, in1=uf[0:P, 0:n], op=ALU.subtract),
                       reads=["uu", "uf"], writes=["gg"])
                    op("vector", lambda e: e.scalar_tensor_tensor(out=mm_[0:P, 0:n], in0=gg[0:P, 0:n], scalar=0.5,
                                                                  in1=gg[0:P, 0:n], op0=ALU.is_gt, op1=ALU.subtract),
                       reads=["gg"], writes=["mm_"])
                    op("scalar", lambda e: e.activation(out=dst, in_=mm_[0:P, 0:n], func=AF.Sin,
                                                        scale=-2.0 * math.pi), reads=["mm_"], writes=[dkey])

                table(vv[:], S, 0.0, sinT[:], "sinT", "vv")
                table(vv[:], S, 0.25, cosT[:], "cosT", "vv")
                table(vm[0:64, :], 127, 0.0, sinC[:], "sinC", "vm", P=64)
                table(vm[0:64, :], 127, 0.25, cosC[:], "cosC", "vm", P=64)
                sc.flush()

            for l in range(NL):
                x_src = x_in if l == 0 else out
                with contextlib.ExitStack() as s1:
                    gqr = sb("gqr", [128, 1], F32, s1)
                    gbc = sb("gbc", [128, D], F32, s1)
                    xt = [sb(f"p1xt{i}", [128, D], F32, s1) for i in range(2)]
                    junk = sb("p1junk", [128, D], BF16, s1)
                    xs = [sb(f"p1xs{i}", [128, D], BF16, s1) for i in range(2)]
                    ssq = [sb(f"p1ss{i}", [128, 1], F32, s1) for i in range(2)]
                    rt_ = [sb(f"p1rt{i}", [128, 1], F32, s1) for i in range(2)]
                    rs_ = [sb(f"p1rs{i}", [128, 1], F32, s1) for i in range(2)]
                    for q0 in (0, 64):
                        op("sync", lambda e, q0=q0: e.dma_start(out=gqr[q0:q0 + 64, :], in_=q_norm_g[l].rearrange("(p o) -> p o", o=1)),
                           writes=["gqr"], dma=True)
                    op("vector", lambda e: e.tensor_scalar(out=gq[:], in0=gqr[:], scalar1=0.125, scalar2=None,
                                                           op0=ALU.mult), reads=["gqr"], writes=["gq"])
                    for j in range(3):
                        for q0 in (0, 64):
                            op("sync", lambda e, j=j, q0=q0: e.dma_start(out=gk[q0:q0 + 64, j:j + 1],
                                                                         in_=k_norm_g[l, j].rearrange("(p o) -> p o", o=1)),
                               writes=["gk"], dma=True)
                    op("sync", lambda e: e.dma_start(out=gbc[:], in_=norm_g[l].partition_broadcast(128)),
                       writes=["gbc"], dma=True)
                    for tt in range(16):
                        i = tt % 2
                        pT = banks[i][:].bitcast(BF16)
                        op("sync", lambda e, tt=tt, i=i: e.dma_start(out=xt[i][:], in_=x_src[b, tt * 128:(tt + 1) * 128, :]),
                           writes=[f"xt{i}"], dma=True)
                        op("scalar", lambda e, i=i: e.activation(out=junk[:], in_=xt[i][:], func=AF.Square,
                                                                 accum_out=ssq[i][:]),
                           reads=[f"xt{i}"], writes=["junk", f"ss{i}"])
                        op("scalar", lambda e, i=i: e.activation(out=rt_[i][:], in_=ssq[i][:], func=AF.Sqrt,
                                                                 scale=1.0 / D, bias=EPS),
                           reads=[f"ss{i}"], writes=[f"rt{i}"])
                        op("vector", lambda e, i=i: e.reciprocal(out=rs_[i][:], in_=rt_[i][:]),
                           reads=[f"rt{i}"], writes=[f"rs{i}"])
                        op("vector", lambda e, i=i: e.scalar_tensor_tensor(out=xs[i][:], in0=xt[i][:], scalar=rs_[i][:],
                                                                           in1=gbc[:], op0=ALU.mult, op1=ALU.mult),
                           reads=[f"xt{i}", f"rs{i}", "gbc"], writes=[f"xs{i}"])
                        for k in range(8):
                            op("tensor", lambda e, i=i, k=k, pT=pT: e.transpose(out=pT[:, k * 128:(k + 1) * 128],
                                                                                in_=xs[i][:, k * 128:(k + 1) * 128],
                                                                                identity=ident_bf[:]),
                               reads=[f"xs{i}"], writes=[f"bank{i}"])
                        op("scalar", lambda e, i=i, tt=tt, pT=pT: e.copy(out=hT[:, :, tt * 128:(tt + 1) * 128],
                                                                         in_=pT.rearrange("p (k t) -> p k t", k=8)),
                           reads=[f"bank{i}"], writes=["hT"])
                    sc.flush()

                if STOPAT <= 1:
                    continue
                with contextlib.ExitStack() as sa:
                    XP = [[sb(f"XP{p}{i}", [128, S], BF16, sa) for i in range(3)] for p in range(2)]
                    XO = [[sb(f"XO{p}{i}", [64, S], BF16, sa) for i in range(3)] for p in range(2)]
                    v_all = sb("v_all", [128, 16, 512], BF16, sa)
                    wv_all = sb("wv_all", [128, 8, 512], BF16, sa)
                    wA2 = [[sb(f"wA{p}{i}", [128, 8, 128], BF16, sa) for i in range(3)] for p in range(2)]
                    e_sb = [sb(f"e_sb{i}", [128, 512], F32, sa) for i in range(2)]
                    sp_bf = [[sb(f"sp_bf{s}{i}", [128, 512], BF16, sa) for i in range(2)] for s in range(2)]
                    R32 = [sb(f"R32{s}", [128, 512], F32, sa) for s in range(2)]
                    Rbf = [[sb(f"Rbf{s}{i}", [128, 512], BF16, sa) for i in range(2)] for s in range(2)]
                    w_bf = [[sb(f"w_bf{s}{i}", [128, 512], BF16, sa) for i in range(2)] for s in range(2)]
                    ya_sb = [sb(f"ya_sb{i}", [64, 512], BF16, sa) for i in range(2)]
                    ringA = Ring(banks[6:8], "bank6")

                    def ringA_next():
                        j = ringA.i % 2
                        ringA.i += 1
                        return banks[6 + j], f"bank{6 + j}"

                    for c in range(4):
                        load_w(wv_all[:, :, c * 128:(c + 1) * 128], win_cols(l, 1024 + c * 128, 128), f"wv_all{c}")
                    for tt in range(16):
                        ps, pk = ringA_next()
                        mm_chain(ps[:], [(hT[:, k, tt * 128:(tt + 1) * 128], wv_all[:, k, :]) for k in range(8)],
                                 pk, ["wv_all0", "wv_all1", "wv_all2", "wv_all3", "hT"])
                        if tt % 2 == 0:
                            op("vector", lambda e, ps=ps, tt=tt: e.tensor_copy(out=v_all[:, tt, :], in_=ps[:]),
                               reads=[pk], writes=["v_all"])
                        else:
                            op("scalar", lambda e, ps=ps, tt=tt: e.copy(out=v_all[:, tt, :], in_=ps[:]),
                               reads=[pk], writes=["v_all"])

                    for hp in range(4):
                        pp = hp % 2
                        wA = wA2[pp]
                        PK = f"p{pp}"
                        qT = [XP[pp][0], XO[pp][0]]
                        kT = [XP[pp][1], XO[pp][1]]
                        szT = [XP[pp][2], XO[pp][2]]
                        for wi, sec in enumerate((0, 1, 3)):
                            load_w(wA[wi][:], win_cols(l, sec * 512 + hp * 128, 128), f"wA{wi}" + PK)
                        for wi in range(3):
                            for tb in range(4):
                                ps, pk = ringA_next()
                                csl = slice(tb * 512, (tb + 1) * 512)
                                mm_chain(ps[:], [(wA[wi][:, k, :], hT[:, k, csl]) for k in range(8)], pk, [f"wA{wi}" + PK, "hT"])
                                d_ap = XP[pp][wi][:, csl]
                                ek = f"XP{wi}t{tb}" + PK
                                if wi == 0:
                                    op("scalar", lambda e, ps=ps, d_ap=d_ap: e.activation(out=d_ap, in_=ps[:], func=AF.Copy, scale=0.125),
                                       reads=[pk], writes=[ek])
                                elif wi == 1:
                                    op("vector", lambda e, ps=ps, d_ap=d_ap: e.tensor_copy(out=d_ap, in_=ps[:]),
                                       reads=[pk], writes=[ek])
                                else:
                                    op("scalar", lambda e, ps=ps, d_ap=d_ap: e.activation(out=d_ap, in_=ps[:], func=AF.Silu),
                                       reads=[pk], writes=[ek])
                                op("sync", lambda e, pp=pp, wi=wi, csl=csl: e.dma_start(out=XO[pp][wi][:, csl], in_=XP[pp][wi][64:128, csl]),
                                   reads=[ek], writes=[f"XO{wi}t{tb}" + PK], dma=True, c=2500)

                        def xkeys(wi, s, qb=None):
                            nm = "XP" if s == 0 else "XO"
                            if qb is None:
                                return [f"{nm}{wi}t{t}" + PK for t in range(4)]
                            return [f"{nm}{wi}t{qb}" + PK]

                        tiles = []
                        for qb in range(4):
                            nt = 4 * qb + 4
                            for kt in range(nt - 1, -1, -1):
                                tiles.append((qb, kt, kt == nt - 1, kt == 0))

                        def c0_of(qb, kt):
                            return max(kt - 4 * qb, 0) * 128

                        def emit_qk(n, s, kT=kT, qT=qT, xkeys=xkeys):
                            qb, kt, first, last = tiles[n]
                            zb = banks[2 * s + (n % 2)]
                            zk = f"bank{2 * s + (n % 2)}"
                            diag = kt >= 4 * qb
                            c0 = c0_of(qb, kt)
                            op("tensor", lambda e: e.matmul(zb[:, c0:512], lhsT=kT[s][0:64, kt * 128:(kt + 1) * 128],
                                                            rhs=qT[s][0:64, qb * 512 + c0:(qb + 1) * 512], start=True, stop=True),
                               reads=xkeys(1, s) + xkeys(0, s, qb), writes=[zk], c=100 + 0.75 * (512 - c0))
                            if diag:
                                op("tensor", lambda e: e.matmul(zb[:, c0:c0 + 128], lhsT=ident_bf[:], rhs=maskS[:],
                                                                start=False, stop=True, skip_group_check=True),
                                   reads=[], writes=[zk], c=200)

                        for s in range(2):
                            emit_qk(0, s)
                        for n in range(len(tiles)):
                            qb, kt, first, last = tiles[n]
                            par = n % 2
                            c0 = c0_of(qb, kt)
                            diag = kt >= 4 * qb
                            c1 = (kt - 4 * qb + 1) * 128 if diag else 0
                            w = 512 - c0
                            if first:
                                for s in range(2):
                                    op("gpsimd", lambda e, s=s: e.memset(R32[s][:], 0.0), writes=[f"R32{s}"], c=600)
                            for s in range(2):
                                zb = banks[2 * s + par]
                                zk = f"bank{2 * s + par}"
                                op("scalar", lambda e, zb=zb, s=s, c0=c0: e.activation(out=e_sb[s][:, c0:512], in_=zb[:, c0:512], func=AF.Exp),
                                   reads=[zk], writes=[f"e_sb{s}"], c=220 + 0.72 * w)
                                op("scalar", lambda e, s=s, par=par, c0=c0: e.activation(out=sp_bf[s][par][:, c0:512], in_=e_sb[s][:, c0:512],
                                                                                         func=AF.Ln, bias=1.0),
                                   reads=[f"e_sb{s}"], writes=[f"sp_bf{s}{par}"], c=250 + 0.75 * w)
                            for s in range(2):
                                zb = banks[2 * s + par]
                                zk = f"bank{2 * s + par}"
                                op("tensor", lambda e, zb=zb, s=s, par=par, first=first, c0=c0: e.matmul(
                                    zb[:, c0:512], lhsT=negtri[:], rhs=sp_bf[s][par][:, c0:512], start=False, stop=first, skip_group_check=True),
                                   reads=[f"sp_bf{s}{par}"], writes=[zk], c=100 + 0.75 * w)
                                if not first:
                                    op("tensor", lambda e, zb=zb, s=s, par=par, c1=c1: e.matmul(
                                        zb[:, c1:512], lhsT=negones[:], rhs=Rbf[s][par][:, c1:512], start=False, stop=True, skip_group_check=True),
                                       reads=[f"Rbf{s}{par}"], writes=[zk], c=100 + 0.75 * (512 - c1))
                                if not last:
                                    op("gpsimd", lambda e, s=s, par=par, c0=c0: e.tensor_tensor(out=R32[s][:, c0:512], in0=R32[s][:, c0:512],
                                                                                                in1=sp_bf[s][par][:, c0:512], op=ALU.add),
                                       reads=[f"sp_bf{s}{par}", f"R32{s}"], writes=[f"R32{s}"], c=200 + 2.0 * w)
                                    op("vector", lambda e, s=s, par=par, c0=c0: e.tensor_copy(out=Rbf[s][1 - par][:, c0:512], in_=R32[s][:, c0:512]),
                                       reads=[f"R32{s}"], writes=[f"Rbf{s}{1 - par}"], c=100 + 1.1 * w)
                            if n + 1 < len(tiles):
                                for s in range(2):
                                    emit_qk(n + 1, s)
                            for s in range(2):
                                zb = banks[2 * s + par]
                                zk = f"bank{2 * s + par}"
                                op("scalar", lambda e, zb=zb, s=s, par=par, c0=c0: e.activation(out=w_bf[s][par][:, c0:512], in_=zb[:, c0:512], func=AF.Exp),
                                   reads=[zk], writes=[f"w_bf{s}{par}"], c=220 + 0.72 * w)
                            for s in range(2):
                                ob = banks[4 + s]
                                h = 2 * hp + s
                                op("tensor", lambda e, ob=ob, s=s, par=par, kt=kt, first=first, last=last, h=h, c0=c0: e.matmul(
                                    ob[0:64, c0:512], lhsT=v_all[:, kt, h * 64:(h + 1) * 64], rhs=w_bf[s][par][:, c0:512],
                                    start=first, stop=last, skip_group_check=True),
                                   reads=[f"w_bf{s}{par}", "v_all"], writes=[f"bank{4 + s}"], c=100 + 0.75 * w)
                                if last:
                                    op("vector", lambda e, ob=ob, s=s, qb=qb, szT=szT: e.tensor_tensor(
                                        out=ya_sb[s][:], in0=ob[0:64, :], in1=szT[s][0:64, qb * 512:(qb + 1) * 512], op=ALU.mult),
                                       reads=[f"bank{4 + s}"] + xkeys(2, s, qb), writes=[f"ya_sb{s}"])
                                    op("sync", lambda e, s=s, h=h, qb=qb: e.dma_start(out=yTa_d[h * 64:(h + 1) * 64, qb * 512:(qb + 1) * 512], in_=ya_sb[s][:]),
                                       reads=[f"ya_sb{s}"], writes=["yTa_d"], dma=True)
                    sc.flush()

                if STOPAT <= 2:
                    continue
                with contextlib.ExitStack() as sB:
                    nqT = sb("nqT", [64, 8, S], BF16, sB)
                    ksT = ksTa[0:64]
                    kwT = sb("kwT", [64, 2, S], BF16, sB)
                    gates = sb("gates", [128, 16, 24], F32, sB)
                    kcr = sb("kcr", [64, 2, S], BF16, sB)
                    vcr = sb("vcr", [64, 2, S], BF16, sB)
                    ringB = Ring(banks[3:8], "bank")

                    def ringB_next():
                        j = 3 + (ringB.i % 5)
                        ringB.i += 1
                        return banks[j], f"bank{j}"

                    nr = {nm: [sb(f"nr_{nm}{i}", [128, 512], dt_, sB) for i in range(2)]
                          for nm, dt_ in (("sq", BF16), ("rt", F32), ("qn", BF16), ("t1", F32), ("t2", F32))}
                    nrc = [0]

                    def normrope(ps, pk, n, P, gcol, gkey, cos_ap, sin_ap, tkeys, out_ap, okey):
                        i = nrc[0] % 2
                        nrc[0] += 1
                        sq, rt, qn, t1, t2 = (nr[nm][i][0:P, 0:n] for nm in ("sq", "rt", "qn", "t1", "t2"))
                        rs = rt
                        kk = lambda nm: f"nr_{nm}{i}"
                        raw = ps[0:P, 0:n]
                        op("scalar", lambda e: e.activation(out=sq, in_=raw, func=AF.Square), reads=[pk], writes=[kk("sq")])
                        p2, p2k = ringB_next()
                        op("tensor", lambda e: e.matmul(p2[0:P, 0:n], lhsT=ones_blk[0:P, 0:P], rhs=sq, start=True, stop=True),
                           reads=[kk("sq")], writes=[p2k])
                        op("scalar", lambda e: e.activation(out=rt, in_=p2[0:P, 0:n], func=AF.Sqrt, scale=1.0 / 64, bias=EPS),
                           reads=[p2k], writes=[kk("rt")])
                        op("vector", lambda e: e.reciprocal(out=rs, in_=rt), reads=[kk("rt")], writes=[kk("rt")], c=3300)
                        op("vector", lambda e: e.scalar_tensor_tensor(out=qn, in0=raw, scalar=gcol, in1=rs,
                                                                      op0=ALU.mult, op1=ALU.mult),
                           reads=[pk, kk("rt"), gkey], writes=[kk("qn")])
                        p3, p3k = ringB_next()
                        op("tensor", lambda e: e.matmul(p3[0:P, 0:n], lhsT=rotM_blk[0:P, 0:P], rhs=qn, start=True, stop=True),
                           reads=[kk("qn")], writes=[p3k])
                        op("gpsimd", lambda e: e.tensor_tensor(out=t1, in0=qn, in1=cos_ap, op=ALU.mult),
                           reads=[kk("qn")] + tkeys, writes=[kk("t1")])
                        op("vector", lambda e: e.tensor_tensor(out=t2, in0=p3[0:P, 0:n], in1=sin_ap, op=ALU.mult),
                           reads=[p3k] + tkeys, writes=[kk("t2")])
                        op("gpsimd", lambda e: e.tensor_tensor(out=out_ap, in0=t1, in1=t2, op=ALU.add),
                           reads=[kk("t1"), kk("t2")], writes=[okey])

                    with contextlib.ExitStack() as sB0:
                        wB = [sb(f"wB{i}", [128, 8, 128], BF16, sB0) for i in range(3)]
                        wBr = Ring(wB, "wB")
                        wv4 = sb("wv4", [128, 8, 512], BF16, sB0)
                        tmpO = Ring([sb(f"tmpO{i}", [128, 512], BF16, sB0) for i in range(3)], "tmpO")

                        def proj128(tb, wt, wk):
                            ps, pk = ringB_next()
                            mm_chain(ps[:], [(wt[:, k, :], hT[:, k, tb * 512:(tb + 1) * 512]) for k in range(8)], pk, [wk, "hT"])
                            return ps, pk

                        def shift2(to, tk, dst0, dst1, kbase):
                            op("sync", lambda e: e.dma_start(out=dst0, in_=to[0:64, :]), reads=[tk], writes=[kbase + "a"], dma=True, c=2500)
                            op("sync", lambda e: e.dma_start(out=dst1, in_=to[64:128, :]), reads=[tk], writes=[kbase + "b"], dma=True, c=2500)

                        for pr in range(4):
                            wt, wk = wBr.next()
                            load_w(wt[:], win_cols(l, 2048 + pr * 128, 128), wk)
                            for tb in range(4):
                                csl = slice(tb * 512, (tb + 1) * 512)
                                ps, pk = proj128(tb, wt, wk)
                                to, tk = tmpO.next()
                                normrope(ps, pk, 512, 128, gq[:, 0:1], "gq", cosT[:, csl], sinT[:, csl], ["cosT", "sinT"], to[:], tk)
                                shift2(to, tk, nqT[:, 2 * pr, csl], nqT[:, 2 * pr + 1, csl], f"nqT{pr}_{tb}")
                        for c0, dst, dkey in ((2560, kcr, "kcr"), (2688, vcr, "vcr")) if B0PART >= 2 else ():
                            wt, wk = wBr.next()
                            load_w(wt[:], win_cols(l, c0, 128), wk)
                            for tb in range(4):
                                csl = slice(tb * 512, (tb + 1) * 512)
                                ps, pk = proj128(tb, wt, wk)
                                to, tk = tmpO.next()
                                op("vector", lambda e, ps=ps, to=to: e.tensor_copy(out=to[:], in_=ps[:]), reads=[pk], writes=[tk])
                                shift2(to, tk, dst[:, 0, csl], dst[:, 1, csl], f"{dkey}_{tb}")
                        for c0, dst, dkey, gi in ((2816, ksT, "ksT", 1), (3072, kwT, "kwT", 2)) if B0PART >= 3 else ():
                            wt, wk = wBr.next()
                            load_w(wt[:], win_cols(l, c0, 128), wk)
                            for tb in range(4):
                                csl = slice(tb * 512, (tb + 1) * 512)
                                ps, pk = proj128(tb, wt, wk)
                                to, tk = tmpO.next()
                                normrope(ps, pk, 512, 128, gk[:, gi:gi + 1], "gk", cosT[:, csl], sinT[:, csl], ["cosT", "sinT"], to[:], tk)
                                shift2(to, tk, dst[:, 0, csl], dst[:, 1, csl], f"{dkey}_{tb}")
                        for c in range(4 if B0PART >= 4 else 0):
                            n_ = 128 if c < 3 else 24
                            load_w(wv4[:, :, c * 128:c * 128 + n_], win_cols(l, 2944 + c * 128, n_), f"wv4_{c}")
                        for tt in range(16 if B0PART >= 4 else 0):
                            ps, pk = ringB_next()
                            mm_chain(ps[:, 0:408], [(hT[:, k, tt * 128:(tt + 1) * 128], wv4[:, k, 0:408]) for k in range(8)],
                                     pk, ["wv4_0", "wv4_1", "wv4_2", "wv4_3", "hT"])
                            op("vector", lambda e, ps=ps, tt=tt: e.tensor_copy(
                                out=vs_aug[:, tt, :, 0:64], in_=ps[:, 0:128].rearrange("p (g d) -> p g d", g=2)),
                               reads=[pk], writes=["vs_aug"], c=300)
                            op("vector", lambda e, ps=ps, tt=tt: e.tensor_copy(
                                out=vw_aug[:, tt, :, 0:64], in_=ps[:, 256:384].rearrange("p (g d) -> p g d", g=2)),
                               reads=[pk], writes=["vw_aug"], c=300)
                            op("scalar", lambda e, ps=ps, tt=tt: e.activation(out=gates[:, tt, :], in_=ps[:, 384:408], func=AF.Sigmoid),
                               reads=[pk], writes=["gates"], c=250)
                        for pr in range(4 if B0PART >= 5 else 0):
                            wt, wk = wBr.next()
                            load_w(wt[:], win_cols(l, 3352 + pr * 128, 128), wk)
                            for tb in range(4):
                                csl = slice(tb * 512, (tb + 1) * 512)
                                ps, pk = proj128(tb, wt, wk)
                                to, tk = tmpO.next()
                                op("scalar", lambda e, ps=ps, to=to: e.activation(out=to[:], in_=ps[:], func=AF.Silu),
                                   reads=[pk], writes=[tk])
                                op("sync", lambda e, to=to, pr=pr, csl=csl: e.dma_start(out=nzT_d[pr * 128:(pr + 1) * 128, csl], in_=to[:]),
                                   reads=[tk], writes=[f"nzT_d{pr}_{tb}"], dma=True)
                        sc.flush()

                    if STOPAT > 3:
                        with contextlib.ExitStack() as sB1:
                            w1_bf = sb("w1_bf", [64, 32, 256], BF16, sB1)
                            w1st = [sb(f"w1st{i}", [64, 4, 256], F32, sB1) for i in range(2)]
                            pe_sb = sb("pe_sb", [32, 64], F32, sB1)
                            peT = sb("peT", [64, 32], BF16, sB1)
                            b1_sb = sb("b1_sb", [128, 2], F32, sB1)
                            w2st = sb("w2st", [128, 2, 64], F32, sB1)
                            w2_bf = sb("w2_bf", [128, 2, 64], BF16, sB1)
                            cvec = sb("cvec", [128, 2], F32, sB1)
                            hid = sb("hid", [128, 2, 2, 127], BF16, sB1)
                            for kv in range(2):
                                raw = kcr if kv == 0 else vcr
                                rkey = "kcr" if kv == 0 else "vcr"
                                for c in range(8):
                                    i = c % 2
                                    src = cmp_w1[l, kv, c * 256:(c + 1) * 256, :].rearrange("(l d) h -> d l h", d=64)
                                    op("sync", lambda e, i=i, src=src: e.dma_start(out=w1st[i][:], in_=src),
                                       writes=[f"w1st{i}"], dma=True)
                                    op("gpsimd", lambda e, i=i, c=c: e.tensor_copy(out=w1_bf[:, c * 4:(c + 1) * 4, :], in_=w1st[i][:]),
                                       reads=[f"w1st{i}"], writes=["w1_bf"])
                                op("sync", lambda e, kv=kv: e.dma_start(out=pe_sb[:], in_=cmp_pe[l, kv]), writes=["pe_sb"], dma=True)
                                for hc in range(2):
                                    op("sync", lambda e, kv=kv, hc=hc: e.dma_start(
                                        out=b1_sb[:, hc:hc + 1],
                                        in_=cmp_b1[l, kv, hc * 128:(hc + 1) * 128].rearrange("(p o) -> p o", o=1)),
                                       writes=["b1_sb"], dma=True)
                                op("sync", lambda e, kv=kv: e.dma_start(
                                    out=w2st[:], in_=cmp_w2[l, kv].rearrange("(c p) d -> p c d", p=128)),
                                   writes=["w2st"], dma=True)
                                op("gpsimd", lambda e: e.tensor_copy(out=w2_bf[:], in_=w2st[:]), reads=["w2st"], writes=["w2_bf"])
                                ps, pk = ringB_next()
                                op("tensor", lambda e, ps=ps: e.transpose(out=ps[0:64, 0:32], in_=pe_sb[:], identity=ident_f[0:32, 0:32]),
                                   reads=["pe_sb"], writes=[pk])
                                op("vector", lambda e, ps=ps: e.tensor_copy(out=peT[:], in_=ps[0:64, 0:32]), reads=[pk], writes=["peT"])
                                for hc in range(2):
                                    ps, pk = ringB_next()
                                    mm_chain(ps[:, 0:1], [(w1_bf[:, li, hc * 128:(hc + 1) * 128], peT[:, li:li + 1]) for li in range(32)],
                                             pk, ["w1_bf", "peT"])
                                    op("vector", lambda e, ps=ps, hc=hc: e.tensor_tensor(out=cvec[:, hc:hc + 1], in0=ps[:, 0:1],
                                                                                         in1=b1_sb[:, hc:hc + 1], op=ALU.add),
                                       reads=[pk, "b1_sb"], writes=["cvec"])
                                for g in range(2):
                                    for hc in range(2):
                                        ps, pk = ringB_next()
                                        mm_chain(ps[:, 0:127], [(w1_bf[:, li, hc * 128:(hc + 1) * 128],
                                                                 raw[:, g, li:li + 16 * 126 + 1:16]) for li in range(32)],
                                                 pk, ["w1_bf", rkey])
                                        op("scalar", lambda e, ps=ps, hc=hc, g=g: e.activation(
                                            out=hid[:, hc, g, :], in_=ps[:, 0:127], func=AF.Silu, bias=cvec[:, hc:hc + 1]),
                                           reads=[pk, "cvec"], writes=["hid"])
                                    ps, pk = ringB_next()
                                    if kv == 0:
                                        mm_chain(ps[0:64, 0:127], [(w2_bf[:, hc, :], hid[:, hc, g, :]) for hc in range(2)],
                                                 pk, ["w2_bf", "hid"])
                                        normrope(ps, pk, 127, 64, gk[0:64, 0:1], "gk", cosC[:], sinC[:], ["cosC", "sinC"],
                                                 kcT[:, g, :], "kcT")
                                    else:
                                        mm_chain(ps[0:127, 0:64], [(hid[:, hc, g, :], w2_bf[:, hc, :]) for hc in range(2)],
                                                 pk, ["w2_bf", "hid"])
                                        op("vector", lambda e, ps=ps, g=g: e.tensor_copy(out=VO[:, g, 0:64], in_=ps[0:127, 0:64]),
                                           reads=[pk], writes=["VO"])
                            sc.flush()

                    with contextlib.ExitStack() as sB2:
                      if STOPAT > 4:
                        NP = 4
                        p_bf = [sb(f"p_bf{i}", [128, 512], BF16, sB2) for i in range(NP)]
                        pR = Ring(p_bf, "p_bf")
                        dn = [sb(f"dn{i}", [128, 3, 4], F32, sB2) for i in range(2)]
                        rdn = [sb(f"rdn{i}", [128, 3, 4], F32, sB2) for i in range(2)]
                        coef = [sb(f"coef{i}", [128, 3, 4], F32, sB2) for i in range(2)]
                        imp = [sb(f"imp{i}", [128, 32], F32, sB2) for i in range(2)]
                        top8 = [sb(f"top8{i}", [128, 8], F32, sB2) for i in range(2)]
                        sbt = [sb(f"sbt{i}", [128, 96], F32, sB2) for i in range(2)]
                        qa = [sb(f"qa{i}", [96, 4, 128], BF16, sB2) for i in range(2)]
                        oc_sb = [sb(f"oc_sb{i}", [128, 388], F32, sB2) for i in range(2)]
                        for i_ in range(2):
                            op("gpsimd", lambda e, i_=i_: e.memset(sbt[i_][:], 0.0), writes=[f"sbt{i_}"])
                        obs = [sb(f"obs{i}", [128, 4, 64], F32, sB2) for i in range(2)]
                        nz_t = [sb(f"nz_t{i}", [64, 4, 128], BF16, sB2) for i in range(2)]
                        yb_sb = [sb(f"yb_sb{i}", [64, 4, 128], BF16, sB2) for i in range(2)]
                        ocb, osb, owb = banks[0], banks[1], banks[2]
                        oTs, oTw = banks[3], banks[4]
                        oT_sb = [[sb(f"oT_sb{a}{i}", [65, 512], F32, sB2) for i in range(2)] for a in range(2)]
                        rb2 = [0]

                        def ringB_next():
                            j = 5 + (rb2[0] % 3)
                            rb2[0] += 1
                            return banks[j], f"bank{j}"

                        oc = ocb[:, 0:388].rearrange("p (r c) -> p r c", r=4)
                        os_ = osb[:, 0:260].rearrange("p (r c) -> p r c", r=4)
                        ow = owb[:, 0:260].rearrange("p (r c) -> p r c", r=4)
                        it = 0
                        for g in range(2):
                            for i in range(16):
                                u = it % 2
                                it += 1
                                tsl = slice(i * 128, (i + 1) * 128)
                                q_ap = nqT[:, 4 * g:4 * g + 4, tsl]
                                op("sync", lambda e, u=u, g=g, tsl=tsl: e.dma_start(out=nz_t[u][:], in_=nzT_d[256 * g:256 * (g + 1), tsl].rearrange("(r d) t -> d r t", d=64)),
                                   reads=[], writes=[f"nz_t{u}"], dma=True)
                                ps, pk = ringB_next()
                                op("tensor", lambda e, ps=ps, g=g, q_ap=q_ap: e.matmul(ps[0:127, :], lhsT=kcT[:, g, :], rhs=q_ap,
                                                                                       start=True, stop=False),
                                   reads=["kcT", "nqT"], writes=[pk])
                                op("tensor", lambda e, ps=ps, tsl=tsl: e.matmul(ps[0:127, :], lhsT=ident_bf[0:127, 0:127],
                                                                                rhs=bc4(cmask[:, tsl]), start=False, stop=True),
                                   reads=[], writes=[pk])
                                pb, pbk = pR.next()
                                op("scalar", lambda e, ps=ps, pb=pb: e.activation(out=pb[0:127, :], in_=ps[0:127, :], func=AF.Exp),
                                   reads=[pk], writes=[pbk])
                                for r in range(4):
                                    op("tensor", lambda e, pb=pb, r=r, g=g: e.matmul(oc[:, r, :], lhsT=pb[0:127, r * 128:(r + 1) * 128],
                                                                                     rhs=VO[:, g, :], start=True, stop=True),
                                       reads=[pbk, "VO"], writes=["bank0"])
                                op("scalar", lambda e, u=u: e.copy(out=oc_sb[u][:], in_=ocb[:, 0:388]), reads=["bank0"], writes=[f"oc_sb{u}"])
                                ocs = oc_sb[u][:].rearrange("p (r c) -> p r c", r=4)
                                op("vector", lambda e, u=u, ocs=ocs: e.tensor_scalar(out=dn[u][:, 0, :], in0=ocs[:, :, 64], scalar1=1e-30, scalar2=None,
                                                                            op0=ALU.max), reads=[f"oc_sb{u}"], writes=[f"dn{u}"])
                                op("vector", lambda e, u=u: e.reciprocal(out=rdn[u][:, 0, :], in_=dn[u][:, 0, :]),
                                   reads=[f"dn{u}"], writes=[f"rdn{u}"])
                                for r in range(4):
                                    in1 = bonus[:, i, :] if r == 0 else imp[u][:]
                                    op("vector", lambda e, u=u, r=r, in1=in1, ocs=ocs: e.scalar_tensor_tensor(
                                        out=imp[u][:], in0=ocs[:, r, 65:97], scalar=rdn[u][:, 0, r:r + 1], in1=in1,
                                        op0=ALU.mult, op1=ALU.add),
                                       reads=[f"oc_sb{u}", f"rdn{u}", f"imp{u}"], writes=[f"imp{u}"])
                                op("vector", lambda e, u=u: e.max(out=top8[u][:], in_=imp[u][:]), reads=[f"imp{u}"], writes=[f"top8{u}"])
                                op("vector", lambda e, u=u: e.tensor_scalar(out=sbt[u][:, 64:96], in0=imp[u][:], scalar1=top8[u][:, 7:8],
                                                                            scalar2=NEG, op0=ALU.is_lt, op1=ALU.mult),
                                   reads=[f"imp{u}", f"top8{u}"], writes=[f"sbt{u}"])
                                ps, pk = ringB_next()
                                op("tensor", lambda e, ps=ps, u=u: e.transpose(out=ps[0:96, 0:128], in_=sbt[u][:], identity=ident_f[:]),
                                   reads=[f"sbt{u}"], writes=[pk])
                                op("vector", lambda e, ps=ps, u=u: e.tensor_copy(out=qa[u][64:96, :, :], in_=bc4(ps[64:96, 0:128])),
                                   reads=[pk], writes=[f"qa{u}"])
                                op("gpsimd", lambda e, u=u, q_ap=q_ap: e.tensor_copy(out=qa[u][0:64, :, :], in_=q_ap),
                                   reads=["nqT", f"qa{u}"], writes=[f"qa{u}"])
                                for kt in range(i + 1):
                                    ksl = slice(kt * 128, (kt + 1) * 128)
                                    ps, pk = ringB_next()
                                    op("tensor", lambda e, ps=ps, g=g, ksl=ksl, u=u, kt=kt, i=i: e.matmul(
                                        ps[:], lhsT=ksTa[0:96, g, ksl], rhs=qa[u][:, :, :], start=True, stop=(kt != i)),
                                       reads=["ksT", f"qa{u}"], writes=[pk])
                                    if kt == i:
                                        op("tensor", lambda e, ps=ps: e.matmul(ps[:], lhsT=ident_bf[:], rhs=bc4(maskC[:]),
                                                                               start=False, stop=True),
                                           reads=[], writes=[pk])
                                    pb, pbk = pR.next()
                                    op("scalar", lambda e, ps=ps, pb=pb: e.activation(out=pb[:], in_=ps[:], func=AF.Exp),
                                       reads=[pk], writes=[pbk])
                                    op("tensor", lambda e, pb=pb, g=g, kt=kt, i=i: e.matmul(
                                        oTs[0:65, :], lhsT=vs_aug[:, kt, g, :], rhs=pb[:], start=(kt == 0), stop=(kt == i)),
                                       reads=[pbk, "vs_aug"], writes=["bank3"])
                                op("scalar", lambda e, u=u: e.copy(out=oT_sb[0][u][:], in_=oTs[0:65, :]),
                                   reads=["bank3"], writes=[f"oT_sb0{u}"])
                                for r in range(4):
                                    op("tensor", lambda e, u=u, r=r: e.transpose(out=os_[:, r, :], in_=oT_sb[0][u][:, r * 128:(r + 1) * 128],
                                                                                 identity=ident_f[0:65, 0:65]),
                                       reads=[f"oT_sb0{u}"], writes=["bank1"], c=300)
                                kts = [kt for kt in range(i - 4, i + 1) if kt >= 0]
                                for kt in kts:
                                    ksl = slice(kt * 128, (kt + 1) * 128)
                                    edge = (kt == i) or (kt == i - 4)
                                    ps, pk = ringB_next()
                                    op("tensor", lambda e, ps=ps, g=g, ksl=ksl, q_ap=q_ap, edge=edge: e.matmul(
                                        ps[:], lhsT=kwT[:, g, ksl], rhs=q_ap, start=True, stop=not edge),
                                       reads=["kwT", "nqT"], writes=[pk])
                                    if edge:
                                        mk_ = maskC if kt == i else maskW
                                        op("tensor", lambda e, ps=ps, mk_=mk_: e.matmul(ps[:], lhsT=ident_bf[:], rhs=bc4(mk_[:]),
                                                                                        start=False, stop=True),
                                           reads=[], writes=[pk])
                                    pb, pbk = pR.next()
                                    op("scalar", lambda e, ps=ps, pb=pb: e.activation(out=pb[:], in_=ps[:], func=AF.Exp),
                                       reads=[pk], writes=[pbk])
                                    op("tensor", lambda e, pb=pb, g=g, kt=kt, kts=kts: e.matmul(
                                        oTw[0:65, :], lhsT=vw_aug[:, kt, g, :], rhs=pb[:], start=(kt == kts[0]), stop=(kt == kts[-1])),
                                       reads=[pbk, "vw_aug"], writes=["bank4"])
                                op("scalar", lambda e, u=u: e.copy(out=oT_sb[1][u][:], in_=oTw[0:65, :]),
                                   reads=["bank4"], writes=[f"oT_sb1{u}"])
                                for r in range(4):
                                    op("tensor", lambda e, u=u, r=r: e.transpose(out=ow[:, r, :], in_=oT_sb[1][u][:, r * 128:(r + 1) * 128],
                                                                                 identity=ident_f[0:65, 0:65]),
                                       reads=[f"oT_sb1{u}"], writes=["bank2"], c=300)
                                op("vector", lambda e, u=u: e.tensor_copy(out=dn[u][:, 1, :], in_=os_[:, :, 64]),
                                   reads=["bank1"], writes=[f"dn{u}"])
                                op("vector", lambda e, u=u: e.tensor_copy(out=dn[u][:, 2, :], in_=ow[:, :, 64]),
                                   reads=["bank2"], writes=[f"dn{u}"])
                                op("vector", lambda e, u=u: e.reciprocal(out=rdn[u][:, 1:3, :], in_=dn[u][:, 1:3, :]),
                                   reads=[f"dn{u}"], writes=[f"rdn{u}"])
                                gview = gates[:, i, :].rearrange("p (b h) -> p b h", b=3)[:, :, 4 * g:4 * g + 4]
                                op("vector", lambda e, u=u, gview=gview: e.tensor_tensor(out=coef[u][:], in0=rdn[u][:], in1=gview, op=ALU.mult),
                                   reads=[f"rdn{u}", "gates"], writes=[f"coef{u}"])
                                for r in range(4):
                                    op("vector", lambda e, u=u, r=r, ocs=ocs: e.tensor_scalar(out=obs[u][:, r, :], in0=ocs[:, r, 0:64],
                                                                                     scalar1=coef[u][:, 0, r:r + 1], scalar2=None, op0=ALU.mult),
                                       reads=[f"oc_sb{u}", f"coef{u}"], writes=[f"obs{u}"])
                                    op("vector", lambda e, u=u, r=r: e.scalar_tensor_tensor(
                                        out=obs[u][:, r, :], in0=os_[:, r, 0:64], scalar=coef[u][:, 1, r:r + 1], in1=obs[u][:, r, :],
                                        op0=ALU.mult, op1=ALU.add),
                                       reads=["bank1", f"coef{u}", f"obs{u}"], writes=[f"obs{u}"])
                                    op("vector", lambda e, u=u, r=r: e.scalar_tensor_tensor(
                                        out=obs[u][:, r, :], in0=ow[:, r, 0:64], scalar=coef[u][:, 2, r:r + 1], in1=obs[u][:, r, :],
                                        op0=ALU.mult, op1=ALU.add),
                                       reads=["bank2", f"coef{u}", f"obs{u}"], writes=[f"obs{u}"])
                                ps, pk = ringB_next()
                                tp = ps[0:64, :].rearrange("p (r t) -> p r t", r=4)
                                for r in range(4):
                                    op("tensor", lambda e, tp=tp, u=u, r=r: e.transpose(out=tp[:, r, :], in_=obs[u][:, r, :], identity=ident_f[:]),
                                       reads=[f"obs{u}"], writes=[pk])
                                op("vector", lambda e, tp=tp, u=u: e.tensor_tensor(out=yb_sb[u][:], in0=tp, in1=nz_t[u][:], op=ALU.mult),
                                   reads=[pk, f"nz_t{u}"], writes=[f"yb_sb{u}"])
                                op("sync", lambda e, u=u, g=g, tsl=tsl: e.dma_start(out=yTb_d[256 * g:256 * (g + 1), tsl].rearrange("(r d) t -> d r t", d=64), in_=yb_sb[u][:]),
                                   reads=[f"yb_sb{u}"], writes=["yTb_d"], dma=True)
                        sc.flush()

                with contextlib.ExitStack() as sC:
                    wo = sb("wo", [128, 8, D], BF16, sC)
                    mT = sb("mT", [128, 8, S], BF16, sC)
                    ya_t = sb("ya_t", [128, 4, S], BF16, sC)
                    yb_t = sb("yb_t", [128, 4, S], BF16, sC)
                    wcu = [[sb(f"wcu{p}{i}", [128, 4, 128], BF16, sC) for i in range(2)] for p in range(2)]
                    wcg = [[sb(f"wcg{p}{i}", [128, 8, 128], BF16, sC) for i in range(2)] for p in range(2)]
                    sg = [sb(f"sg{i}", [128, 512], F32, sC) for i in range(2)]
                    m1 = [sb(f"m1{i}", [128, 512], F32, sC) for i in range(2)]
                    m2 = [sb(f"m2{i}", [128, 512], F32, sC) for i in range(2)]
                    xt = [sb(f"cxt{i}", [128, D], F32, sC) for i in range(2)]
                    ringC = Ring(banks, "bank")
                    yav = yTa_d.rearrange("(hp p) s -> p hp s", p=128)
                    ybv = yTb_d.rearrange("(hp p) s -> p hp s", p=128)
                    for tb in range(4):
                        bsl = slice(tb * 512, (tb + 1) * 512)
                        op("sync", lambda e, bsl=bsl: e.dma_start(out=ya_t[:, :, bsl], in_=yav[:, :, bsl]),
                           reads=["yTa_d"], writes=[f"ya_t{tb}"], dma=True)
                        op("sync", lambda e, bsl=bsl: e.dma_start(out=yb_t[:, :, bsl], in_=ybv[:, :, bsl]),
                           reads=["yTb_d"], writes=[f"yb_t{tb}"], dma=True)
                    sgc = 0
                    for dmc in range(8):
                        cs = slice(dmc * 128, (dmc + 1) * 128)
                        p = dmc % 2
                        load_w(wcu[p][0][:], w_up_a[l, :, cs].rearrange("(hp p) m -> p hp m", p=128), f"wcu{p}0")
                        load_w(wcg[p][0][:], win_cols(l, 3864 + dmc * 128, 128), f"wcg{p}0")
                        load_w(wcu[p][1][:], w_up_b[l, :, cs].rearrange("(hp p) m -> p hp m", p=128), f"wcu{p}1")
                        load_w(wcg[p][1][:], win_cols(l, 4888 + dmc * 128, 128), f"wcg{p}1")
                        if dmc >= 1:
                            c = dmc - 1
                            load_w(wo[:, :, c * 128:(c + 1) * 128], w_out[l, :, c * 128:(c + 1) * 128].rearrange("(k p) m -> p k m", p=128), "wo")
                        if dmc == 7:
                            load_w(wo[:, :, 7 * 128:8 * 128], w_out[l, :, 7 * 128:8 * 128].rearrange("(k p) m -> p k m", p=128), "wo")
                        for tb in range(4):
                            bsl = slice(tb * 512, (tb + 1) * 512)
                            j = sgc % 2
                            sgc += 1
                            for br, (yt, ytk, mm) in enumerate(((ya_t, f"ya_t{tb}", m1), (yb_t, f"yb_t{tb}", m2))):
                                pu, puk = ringC.next()
                                mm_chain(pu[:], [(wcu[p][br][:, hp, :], yt[:, hp, bsl]) for hp in range(4)], puk, [f"wcu{p}{br}", ytk])
                                pg, pgk = ringC.next()
                                mm_chain(pg[:], [(wcg[p][br][:, k, :], hT[:, k, bsl]) for k in range(8)], pgk, [f"wcg{p}{br}", "hT"])
                                op("scalar", lambda e, pg=pg, j=j: e.activation(out=sg[j][:], in_=pg[:], func=AF.Sigmoid),
                                   reads=[pgk], writes=[f"sg{j}"])
                                op("vector", lambda e, pu=pu, j=j, mm=mm: e.tensor_tensor(out=mm[j][:], in0=pu[:], in1=sg[j][:], op=ALU.mult),
                                   reads=[puk, f"sg{j}"], writes=[f"m{br + 1}{j}"])
                            op("gpsimd", lambda e, j=j, dmc=dmc, bsl=bsl: e.tensor_tensor(out=mT[:, dmc, bsl], in0=m1[j][:], in1=m2[j][:], op=ALU.add),
                               reads=[f"m1{j}", f"m2{j}"], writes=[f"mT{tb}"])
                    for tt in range(16):
                        tb = tt // 4
                        v = tt % 2
                        op("sync", lambda e, v=v, tt=tt: e.dma_start(out=xt[v][:], in_=x_src[b, tt * 128:(tt + 1) * 128, :]),
                           reads=[], writes=[f"cxt{v}"], dma=True)
                        for half in range(2):
                            hs = slice(half * 512, (half + 1) * 512)
                            po, pok = ringC.next()
                            mm_chain(po[:], [(mT[:, k, tt * 128:(tt + 1) * 128], wo[:, k, hs]) for k in range(8)],
                                     pok, [f"mT{tb}", "wo"])
                            op("vector", lambda e, po=po, v=v, hs=hs: e.tensor_tensor(out=xt[v][:, hs], in0=po[:], in1=xt[v][:, hs], op=ALU.add),
                               reads=[pok, f"cxt{v}"], writes=[f"cxt{v}"])
                        op("sync", lambda e, v=v, tt=tt: e.dma_start(out=out[b, tt * 128:(tt + 1) * 128, :], in_=xt[v][:]),
                           reads=[f"cxt{v}"], writes=[], dma=True)
                    sc.flush()
        print("total instructions (incl. waits):", sc.ninstr)
    return nc


_INVF = (1.0 / (2.0 * math.pi) * (10000.0 ** (-(np.arange(128) % 32) / 32.0))).astype(np.float32).reshape(128, 1)
_PROG = {}


def _get_prog(NL):
    if NL not in _PROG:
        _PROG[NL] = build_program(NL)
    return _PROG[NL]


WNAMES = ["norm_g", "w_in", "q_norm_g", "k_norm_g", "cmp_pe", "cmp_w1", "cmp_b1", "cmp_w2", "w_up_a", "w_up_b", "w_out"]


def kernel(x, positions, norm_g, w_in, q_norm_g, k_norm_g, cmp_pe, cmp_w1, cmp_b1, cmp_w2,
           w_up_a, w_up_b, w_out, _layers_per_launch=DEPTH):
    ws = dict(norm_g=norm_g, w_in=w_in, q_norm_g=q_norm_g, k_norm_g=k_norm_g, cmp_pe=cmp_pe, cmp_w1=cmp_w1,
              cmp_b1=cmp_b1, cmp_w2=cmp_w2, w_up_a=w_up_a, w_up_b=w_up_b, w_out=w_out)
    ws = {k: np.ascontiguousarray(np.asarray(v, dtype=np.float32)) for k, v in ws.items()}
    xcur = np.ascontiguousarray(np.asarray(x, dtype=np.float32))
    pos = np.ascontiguousarray(np.asarray(positions, dtype=np.int32))
    NL = _layers_per_launch
    nc = _get_prog(NL)
    for l0 in range(0, DEPTH, NL):
        in_maps = []
        for c in range(NCORES):
            m = {"x": xcur[c * NB:(c + 1) * NB], "positions": pos[c * NB:(c + 1) * NB], "invf": _INVF}
            for k, v in ws.items():
                m[k] = v[l0:l0 + NL]
            in_maps.append(m)
        res = run_bass_kernel_spmd(nc, in_maps, core_ids=list(range(NCORES)))
        xcur = np.concatenate([np.asarray(r["out"]) for r in res.results], axis=0)
    return xcur.astype(np.float32)
```
